# Optimizing a Trainium2 kernel written in Bass

```python
import jax, jax.numpy as jnp
from jax import lax
import numpy as np

D_MODEL = 2048
BATCH = 4
SEQ = 4096
DEPTH = 2

GRID_W = 64
CTX_LEN = 256
F_WIDTH = 1024
F_GROUPS = 8
F_GROUP_DIM = F_WIDTH // F_GROUPS
C_WIDTH = 1024
CONV_K = 31
N_HEADS = 16
QK_NOPE = 128
QK_ROPE = 64
V_DIM = 128
Q_LORA = 768
KV_LORA = 512
ROPE_BASE = 10000.0
ATTN_SCALE = (QK_NOPE + QK_ROPE) ** -0.5
Q_BLOCK = 128
N_BRANCH = 3
FFN_HIDDEN = -(-8 * D_MODEL // (3 * 256)) * 256
EPS = 1e-6
F_END = F_WIDTH
C_END = F_END + 2 * C_WIDTH
Q_END = C_END + Q_LORA
KV_END = Q_END + KV_LORA
KPE_END = KV_END + QK_ROPE
IN_COLS = KPE_END + N_BRANCH * D_MODEL

kernel_name = "hybrid_fnet_conformer_mla_dit_block"


def rmsnorm(x, g):
    xf = x.astype(jnp.float32)
    y = xf * lax.rsqrt(jnp.mean(xf * xf, axis=-1, keepdims=True) + EPS)
    return (y * g.astype(jnp.float32)).astype(x.dtype)


def layernorm(x, g, b):
    xf = x.astype(jnp.float32)
    mu = jnp.mean(xf, axis=-1, keepdims=True)
    var = jnp.mean(jnp.square(xf - mu), axis=-1, keepdims=True)
    y = (xf - mu) * lax.rsqrt(var + EPS)
    return (y * g.astype(jnp.float32) + b.astype(jnp.float32)).astype(x.dtype)


def modulate(h, shift, scale):
    return h * (1 + scale[:, None, :]) + shift[:, None, :]


def rope_2d(n_tok):
    rows = n_tok // GRID_W
    row = jnp.broadcast_to(jnp.arange(rows, dtype=jnp.float32)[:, None], (rows, GRID_W)).reshape(-1)
    col = jnp.broadcast_to(jnp.arange(GRID_W, dtype=jnp.float32)[None, :], (rows, GRID_W)).reshape(-1)
    n_freq = QK_ROPE // 4
    inv_freq = ROPE_BASE ** (-jnp.arange(n_freq, dtype=jnp.float32) / n_freq)
    ang = jnp.concatenate([row[:, None] * inv_freq, col[:, None] * inv_freq], axis=-1)
    return jnp.cos(ang), jnp.sin(ang)


def apply_rope(x, cos, sin):
    half = QK_ROPE // 2
    xf = x.astype(jnp.float32)
    x1, x2 = xf[..., :half], xf[..., half:]
    return jnp.concatenate([x1 * cos - x2 * sin, x1 * sin + x2 * cos], axis=-1).astype(x.dtype)


def fourier_mix(u):
    b, n, _ = u.shape
    uf = u.astype(jnp.float32).reshape(b, n, F_GROUPS, F_GROUP_DIM)
    y = jnp.fft.fft2(uf, axes=(1, 3), norm="ortho").real
    return y.reshape(b, n, F_WIDTH).astype(u.dtype)


def conformer_conv(u, lp):
    a, g = u[..., :C_WIDTH], u[..., C_WIDTH:]
    v = a * jax.nn.sigmoid(g)
    v = lax.conv_general_dilated(
        v, lp["conv_w"][:, None, :], window_strides=(1,),
        padding=[(CONV_K // 2, CONV_K // 2)],
        dimension_numbers=("NWC", "WIO", "NWC"),
        feature_group_count=C_WIDTH) + lp["conv_b"]
    v = jax.nn.silu(layernorm(v, lp["conv_ln_g"], lp["conv_ln_b"]))
    return v @ lp["w_conv_out"]


def mla_query(q_down, lp, rope):
    b, n, _ = q_down.shape
    q = (rmsnorm(q_down, lp["q_norm_g"]) @ lp["w_uq"]).reshape(b, n, N_HEADS, QK_NOPE + QK_ROPE)
    q_nope, q_pe = q[..., :QK_NOPE], q[..., QK_NOPE:]
    if rope is not None:
        q_pe = apply_rope(q_pe, rope[0][:, None, :], rope[1][:, None, :])
    return q_nope, q_pe


def mla_keys(kv_down, k_pe, lp, rope):
    b, n, _ = kv_down.shape
    kv = (rmsnorm(kv_down, lp["kv_norm_g"]) @ lp["w_ukv"]).reshape(b, n, N_HEADS, QK_NOPE + V_DIM)
    k_nope, v = kv[..., :QK_NOPE], kv[..., QK_NOPE:]
    if rope is not None:
        k_pe = apply_rope(k_pe, rope[0], rope[1])
    return k_nope, k_pe, v


def attend(q_nope, q_pe, k_nope, k_pe, v):
    s = (jnp.einsum("bqhd,bkhd->bhqk", q_nope, k_nope)
         + jnp.einsum("bqhd,bkd->bhqk", q_pe, k_pe)).astype(jnp.float32) * ATTN_SCALE
    p = jax.nn.softmax(s, axis=-1).astype(v.dtype)
    return jnp.einsum("bhqk,bkhd->bqhd", p, v)


def blocked_attention(q_nope, q_pe, k_nope, k_pe, v):
    b, n = q_nope.shape[:2]
    nb = n // Q_BLOCK
    qn = q_nope.reshape(b, nb, Q_BLOCK, N_HEADS, QK_NOPE).swapaxes(0, 1)
    qp = q_pe.reshape(b, nb, Q_BLOCK, N_HEADS, QK_ROPE).swapaxes(0, 1)
    o = lax.map(lambda qs: attend(qs[0], qs[1], k_nope, k_pe, v), (qn, qp))
    return o.swapaxes(0, 1).reshape(b, n, N_HEADS * V_DIM)


def merge_branches(proj, attn_o, lp):
    y_f = fourier_mix(proj[..., :F_END]) @ lp["w_fourier"]
    y_c = conformer_conv(proj[..., F_END:C_END], lp)
    y_a = attn_o @ lp["w_mla_o"]
    g = jax.nn.sigmoid(proj[..., KPE_END:] + lp["b_gate"])
    g_f, g_c, g_a = jnp.split(g, N_BRANCH, axis=-1)
    return (g_f * y_f + g_c * y_c + g_a * y_a) @ lp["w_out"]


def swiglu(h, lp):
    return (jax.nn.silu(h @ lp["w_ffn_gate"]) * (h @ lp["w_ffn_up"])) @ lp["w_ffn_down"]


def setup_inputs(seed: int = 0) -> dict:
    key = jax.random.key(seed)
    ks = jax.random.split(key, 26)

    def nrm(k, shape, scale):
        return jax.random.normal(k, shape, jnp.float32) * scale

    def gain(k, shape):
        return 1.0 + 0.02 * jax.random.normal(k, shape, jnp.float32)

    return {
        "x": nrm(ks[0], (BATCH, SEQ, D_MODEL), 1.0),
        "c": nrm(ks[1], (BATCH, D_MODEL), 1.0),
        "ctx": nrm(ks[2], (BATCH, CTX_LEN, D_MODEL), 1.0),
        "c_ctx": nrm(ks[3], (D_MODEL,), 1.0),
        "ada_w": nrm(ks[4], (DEPTH, D_MODEL, 6 * D_MODEL), 0.5 * D_MODEL ** -0.5),
        "ada_b": nrm(ks[5], (DEPTH, 6 * D_MODEL), 0.02),
        "norm_mix_g": gain(ks[6], (DEPTH, D_MODEL)),
        "w_in": nrm(ks[7], (DEPTH, D_MODEL, IN_COLS), D_MODEL ** -0.5),
        "b_gate": nrm(ks[8], (DEPTH, N_BRANCH * D_MODEL), 0.02),
        "w_fourier": nrm(ks[9], (DEPTH, F_WIDTH, D_MODEL), F_WIDTH ** -0.5),
        "conv_w": nrm(ks[10], (DEPTH, CONV_K, C_WIDTH), CONV_K ** -0.5),
        "conv_b": nrm(ks[11], (DEPTH, C_WIDTH), 0.02),
        "conv_ln_g": gain(ks[12], (DEPTH, C_WIDTH)),
        "conv_ln_b": nrm(ks[13], (DEPTH, C_WIDTH), 0.02),
        "w_conv_out": nrm(ks[14], (DEPTH, C_WIDTH, D_MODEL), C_WIDTH ** -0.5),
        "q_norm_g": gain(ks[15], (DEPTH, Q_LORA)),
        "w_uq": nrm(ks[16], (DEPTH, Q_LORA, N_HEADS * (QK_NOPE + QK_ROPE)), Q_LORA ** -0.5),
        "kv_norm_g": gain(ks[17], (DEPTH, KV_LORA)),
        "w_ukv": nrm(ks[18], (DEPTH, KV_LORA, N_HEADS * (QK_NOPE + V_DIM)), KV_LORA ** -0.5),
        "w_mla_o": nrm(ks[19], (DEPTH, N_HEADS * V_DIM, D_MODEL), (N_HEADS * V_DIM) ** -0.5),
        "w_out": nrm(ks[20], (DEPTH, D_MODEL, D_MODEL), D_MODEL ** -0.5),
        "norm_ffn_g": gain(ks[21], (DEPTH, D_MODEL)),
        "w_ffn_gate": nrm(ks[22], (DEPTH, D_MODEL, FFN_HIDDEN), D_MODEL ** -0.5),
        "w_ffn_up": nrm(ks[23], (DEPTH, D_MODEL, FFN_HIDDEN), D_MODEL ** -0.5),
        "w_ffn_down": nrm(ks[24], (DEPTH, FFN_HIDDEN, D_MODEL), FFN_HIDDEN ** -0.5),
        "final_norm_g": gain(ks[25], (D_MODEL,)),
    }


def reference(x, c, ctx, c_ctx, ada_w, ada_b, norm_mix_g, w_in, b_gate, w_fourier,
              conv_w, conv_b, conv_ln_g, conv_ln_b, w_conv_out, q_norm_g, w_uq,
              kv_norm_g, w_ukv, w_mla_o, w_out, norm_ffn_g, w_ffn_gate, w_ffn_up,
              w_ffn_down, final_norm_g):
    rope = rope_2d(x.shape[1])
    b = x.shape[0]
    for i in range(DEPTH):
        last = i == DEPTH - 1
        lp = dict(b_gate=b_gate[i], w_fourier=w_fourier[i], conv_w=conv_w[i], conv_b=conv_b[i],
                  conv_ln_g=conv_ln_g[i], conv_ln_b=conv_ln_b[i], w_conv_out=w_conv_out[i],
                  q_norm_g=q_norm_g[i], w_uq=w_uq[i], kv_norm_g=kv_norm_g[i], w_ukv=w_ukv[i],
                  w_mla_o=w_mla_o[i], w_out=w_out[i], w_ffn_gate=w_ffn_gate[i],
                  w_ffn_up=w_ffn_up[i], w_ffn_down=w_ffn_down[i])
        mod = jax.nn.silu(c) @ ada_w[i] + ada_b[i]
        mod_c = jax.nn.silu(c_ctx)[None, :] @ ada_w[i] + ada_b[i]
        sh1, sc1, g1, sh2, sc2, g2 = jnp.split(mod, 6, axis=-1)
        csh1, csc1, cg1, csh2, csc2, cg2 = jnp.split(mod_c, 6, axis=-1)

        hx = modulate(rmsnorm(x, norm_mix_g[i]), sh1, sc1)
        hc = modulate(rmsnorm(ctx, norm_mix_g[i]), csh1, csc1)
        px = hx @ w_in[i]
        if last:
            pkv = hc @ w_in[i][:, Q_END:KPE_END]
            kv_c, kpe_c = pkv[..., :KV_LORA], pkv[..., KV_LORA:]
        else:
            pc = hc @ w_in[i]
            kv_c, kpe_c = pc[..., Q_END:KV_END], pc[..., KV_END:KPE_END]

        qn_x, qp_x = mla_query(px[..., C_END:Q_END], lp, rope)
        kn_x, kp_x, v_x = mla_keys(px[..., Q_END:KV_END], px[..., KV_END:KPE_END], lp, rope)
        kn_c, kp_c, v_c = mla_keys(kv_c, kpe_c, lp, None)
        o_x = blocked_attention(qn_x, qp_x,
                                jnp.concatenate([kn_x, kn_c], axis=1),
                                jnp.concatenate([kp_x, kp_c], axis=1),
                                jnp.concatenate([v_x, v_c], axis=1))
        x_mixed = x + g1[:, None, :] * merge_branches(px, o_x, lp)

        if not last:
            qn_c, qp_c = mla_query(pc[..., C_END:Q_END], lp, None)
            o_c = attend(qn_c, qp_c, kn_c, kp_c, v_c).reshape(b, ctx.shape[1], N_HEADS * V_DIM)
            ctx = ctx + cg1[:, None, :] * merge_branches(pc, o_c, lp)
            ctx = ctx + cg2[:, None, :] * swiglu(modulate(rmsnorm(ctx, norm_ffn_g[i]), csh2, csc2), lp)

        x = x_mixed + g2[:, None, :] * swiglu(modulate(rmsnorm(x_mixed, norm_ffn_g[i]), sh2, sc2), lp)
    return rmsnorm(x, final_norm_g)
```

```python
import numpy as np
import ml_dtypes
from contextlib import ExitStack
import concourse.bass as bass
import concourse.mybir as mybir
from concourse.bass_utils import run_bass_kernel_spmd

F32 = mybir.dt.float32
BF16 = mybir.dt.bfloat16
AF = mybir.ActivationFunctionType
ALU = mybir.AluOpType

D = 2048
KC = 16
NLAT = 4096
NCTX = 256
T = NLAT + NCTX
DEPTH = 2
NCORE = 4
EPS = 1e-6
ATTN_SCALE = 192 ** -0.5
FFN = 5632
HC = FFN // 128
VT_LAT0 = 15
VT_CTX0 = 15 + NLAT + 30
VTW = NLAT + 30 + NCTX + 30
TILES = [(i * 512, 512, False) for i in range(8)] + [(NLAT, 256, True)]

S1_TILES = [("F", 0, 512), ("F", 512, 512), ("G", 0, 512), ("A", 0, 512), ("G", 512, 512), ("A", 512, 512),
            ("Q", 0, 512), ("Q", 512, 256), ("KV", 0, 512), ("KPE", 0, 128)]


def _tm(W, mw):
    K, M = W.shape
    return np.ascontiguousarray(W.reshape(K // 128, 128, M // mw, mw).transpose(2, 1, 0, 3))


def _weight_layout():
    specs = []
    for i, (nm, c0, w) in enumerate(S1_TILES):
        specs.append((f"s1_{i}", 1, 16, w))
    specs += [("wuq", 16, 6, 256), ("wuk", 16, 4, 128), ("wuv", 4, 4, 512),
              ("gate", 8, 16, 768), ("wf", 8, 8, 256), ("wc", 8, 8, 256), ("wo", 8, 16, 256),
              ("wout", 8, 16, 256), ("wg", 22, 16, 256), ("wu", 22, 16, 256), ("wd", 16, 44, 128),
              ("ada", 24, 16, 512)]
    lay = {}
    off = 0
    for nm, nt, kc, mw in specs:
        lay[nm] = (off, nt, kc, mw)
        off += nt * 128 * kc * mw
    tot = (off + 2047) // 2048 * 2048
    return lay, tot


WLAY, NW = _weight_layout()

VEC_COLS = {}
_o = 0
for _nm, _n in [("ada_b", 96), ("gmix", 16), ("bgate", 48), ("conv_b", 8), ("ln_g", 8), ("ln_b", 8),
                ("qg", 6), ("kvg", 4), ("gffn", 16), ("convw", 248), ("gfin", 16)]:
    VEC_COLS[_nm] = _o
    _o += _n
NV = _o


def _pm(v):
    return np.ascontiguousarray(v.reshape(-1, 128).T)


def _pack_weights(inp, l):
    buf = np.zeros(NW, np.float32)

    def put(nm, arr):
        off, nt, kc, mw = WLAY[nm]
        assert arr.shape == (nt, 128, kc, mw), (nm, arr.shape)
        buf[off:off + arr.size] = arr.reshape(-1)

    put("ada", _tm(inp["ada_w"][l], 512))
    w_in = inp["w_in"][l]
    kpe = w_in[:, 4352:4416]
    cols = {"F": w_in[:, 0:1024], "A": w_in[:, 1024:2048], "G": w_in[:, 2048:3072], "Q": w_in[:, 3072:3840],
            "KV": w_in[:, 3840:4352],
            "KPE": np.concatenate([kpe, kpe[:, 32:64], kpe[:, 0:32]], axis=1)}
    for i, (nm, c0, w) in enumerate(S1_TILES):
        put(f"s1_{i}", _tm(cols[nm][:, c0:c0 + w], w))
    wuq = inp["w_uq"][l].reshape(768, 16, 192)
    wuq = np.concatenate([wuq, wuq[:, :, 160:192], wuq[:, :, 128:160]], axis=2).reshape(768, 16 * 256)
    put("wuq", _tm(wuq, 256))
    wukv = inp["w_ukv"][l].reshape(512, 16, 256)
    put("wuk", _tm(np.ascontiguousarray(wukv[:, :, 0:128]).reshape(512, 2048), 128))
    put("wuv", _tm(np.ascontiguousarray(wukv[:, :, 128:256]).reshape(512, 2048), 512))
    g = w_in[:, 4416:].reshape(2048, 3, 16, 128).transpose(0, 2, 1, 3).reshape(2048, 6144)
    put("gate", _tm(np.ascontiguousarray(g), 768))
    put("wf", _tm(inp["w_fourier"][l], 256))
    put("wc", _tm(inp["w_conv_out"][l], 256))
    put("wo", _tm(inp["w_mla_o"][l], 256))
    put("wout", _tm(inp["w_out"][l], 256))
    put("wg", _tm(inp["w_ffn_gate"][l], 256))
    put("wu", _tm(inp["w_ffn_up"][l], 256))
    put("wd", _tm(inp["w_ffn_down"][l], 128))
    return buf


def _pack_vec(inp, l):
    v = np.zeros((128, NV), np.float32)

    def put(nm, arr):
        v[:, VEC_COLS[nm]:VEC_COLS[nm] + arr.shape[1]] = arr

    put("ada_b", _pm(inp["ada_b"][l]))
    put("gmix", _pm(inp["norm_mix_g"][l]))
    put("bgate", _pm(inp["b_gate"][l]))
    put("conv_b", _pm(inp["conv_b"][l]))
    put("ln_g", _pm(inp["conv_ln_g"][l]))
    put("ln_b", _pm(inp["conv_ln_b"][l]))
    put("qg", _pm(inp["q_norm_g"][l]))
    put("kvg", _pm(inp["kv_norm_g"][l]))
    put("gffn", _pm(inp["norm_ffn_g"][l]))
    cw = inp["conv_w"][l]
    put("convw", np.ascontiguousarray(cw.reshape(31, 8, 128).transpose(2, 1, 0)).reshape(128, 248))
    put("gfin", _pm(inp["final_norm_g"]))
    return v


_CONST_CACHE = {}


def _constants():
    if _CONST_CACHE:
        return _CONST_CACHE
    bf = ml_dtypes.bfloat16
    cst = np.zeros((128, 384), np.float32)
    cst[:, 0:128] = np.eye(128, dtype=np.float32)
    cc = np.arange(128)
    ang = 2 * np.pi * ((cc[:, None] * cc[None, :]) % 128) / 128.0
    cst[:, 128:256] = np.cos(ang)
    cst[:, 256:384] = np.sin(ang)
    n_freq = 16
    inv_freq = (10000.0 ** (-np.arange(n_freq, dtype=np.float32) / n_freq)).astype(np.float32)
    tok = np.arange(NLAT)
    row = (tok // 64).astype(np.float32)
    col = (tok % 64).astype(np.float32)
    angr = np.concatenate([row[:, None] * inv_freq, col[:, None] * inv_freq], axis=-1).astype(np.float32)
    cos = np.ones((T, 32), np.float32)
    sin = np.zeros((T, 32), np.float32)
    cos[:NLAT] = np.cos(angr)
    sin[:NLAT] = np.sin(angr)
    rope = np.zeros((64, 2, T), np.float32)
    rope[0:32, 0] = cos.T
    rope[32:64, 0] = cos.T
    rope[0:32, 1] = -sin.T
    rope[32:64, 1] = sin.T
    n = np.arange(NLAT)
    tabl = np.zeros((8, 128, 32, 2, 512), bf)
    for kt in range(8):
        k = kt * 512 + np.arange(512)
        a = 2 * np.pi * ((n[:, None] * k[None, :]) % NLAT) / float(NLAT)
        tabl[kt, :, :, 0, :] = np.cos(a).reshape(32, 128, 512).transpose(1, 0, 2).astype(bf)
        tabl[kt, :, :, 1, :] = (-np.sin(a)).reshape(32, 128, 512).transpose(1, 0, 2).astype(bf)
    n2 = np.arange(NCTX)
    a = 2 * np.pi * ((n2[:, None] * n2[None, :]) % NCTX) / float(NCTX)
    tabc = np.zeros((128, 2, 2, 256), bf)
    tabc[:, :, 0, :] = np.cos(a).reshape(2, 128, 256).transpose(1, 0, 2).astype(bf)
    tabc[:, :, 1, :] = (-np.sin(a)).reshape(2, 128, 256).transpose(1, 0, 2).astype(bf)
    _CONST_CACHE.update(cst=cst, rope=rope, tabl=tabl, tabc=tabc)
    return _CONST_CACHE


class Ev:
    __slots__ = ("sem", "val", "key")

    def __init__(self, sem, val, key):
        self.sem, self.val, self.key = sem, val, key


class Buf:
    __slots__ = ("w", "r")

    def __init__(self):
        self.w = None
        self.r = {}


class Eng:
    def __init__(self, kern, name, eng, selfsync):
        self.k, self.name, self.e, self.selfsync = kern, name, eng, selfsync
        self.waited = {}
        self.sem = None
        self.cnt = 0
        self.key = None
        self.last = None
        self.nsem = 0

    def _newsem(self):
        self.sem = self.k.es.enter_context(self.k.nc.semaphore(f"s_{self.name}_{self.nsem}"))
        self.key = (self.name, self.nsem)
        self.nsem += 1
        self.cnt = 0

    def wait(self, ev):
        if ev is None:
            return
        if ev.key == self.key and not self.selfsync:
            return
        if self.waited.get(ev.key, 0) >= ev.val:
            return
        self.e.wait_ge(ev.sem, ev.val)
        self.waited[ev.key] = ev.val

    def bump(self, inst):
        if self.sem is None or self.cnt >= 30000:
            self._newsem()
        self.cnt += 1
        inst.then_inc(self.sem, 1)
        self.last = Ev(self.sem, self.cnt, self.key)
        return self.last


class Kern:
    NDMA = 40

    def __init__(self, nc, es):
        self.nc, self.es = nc, es
        self.E = {"pe": Eng(self, "pe", nc.tensor, False), "act": Eng(self, "act", nc.scalar, True),
                  "dve": Eng(self, "dve", nc.vector, True), "pool": Eng(self, "pool", nc.gpsimd, True),
                  "sp": Eng(self, "sp", nc.sync, False)}
        self.dsem = []
        self.dpool = {}
        self.dcnt = {"sp": 0, "act": 0, "pool": 0}
        i = 0
        for q, n in (("sp", 24), ("act", 12), ("pool", 8)):
            self.dpool[q] = list(range(i, i + n))
            for _ in range(n):
                self.dsem.append([es.enter_context(nc.semaphore(f"s_dma_{i}")), 0, None])
                i += 1

    def _sync(self, E, reads, writes):
        for b in reads:
            E.wait(b.w)
        for b in writes:
            E.wait(b.w)
            for r in b.r.values():
                E.wait(r)

    def _rec(self, ev, reads, writes):
        for b in reads:
            b.r[ev.key] = ev
        for b in writes:
            b.w = ev
            b.r = {}

    def op(self, eng, fn, reads=(), writes=()):
        E = self.E[eng]
        self._sync(E, reads, writes)
        ev = E.bump(fn(E.e))
        self._rec(ev, reads, writes)
        return ev

    def mm(self, out, pb, pairs, reads, start=True, stop=True):
        E = self.E["pe"]
        self._sync(E, reads, [pb])
        n = len(pairs)
        inst = None
        for i, (l, r) in enumerate(pairs):
            inst = self.nc.tensor.matmul(out, lhsT=l, rhs=r, start=(start and i == 0), stop=(stop and i == n - 1))
        ev = E.bump(inst)
        self._rec(ev, reads, [pb])
        return ev

    def dma(self, q, out, in_, reads=(), writes=(), **kw):
        E = self.E[q]
        self._sync(E, reads, writes)
        pool = self.dpool[q]
        idx = pool[self.dcnt[q] % len(pool)]
        self.dcnt[q] += 1
        slot = self.dsem[idx]
        E.wait(slot[2])
        inst = E.e.dma_start(out=out, in_=in_, **kw)
        slot[1] += 16
        inst.then_inc(slot[0], 16)
        ev = Ev(slot[0], slot[1], ("dma", idx))
        slot[2] = ev
        self._rec(ev, reads, writes)
        return ev

    def barrier(self):
        evs = [E.last for E in self.E.values() if E.last is not None] + [s[2] for s in self.dsem if s[2] is not None]
        for E in self.E.values():
            for ev in evs:
                E.wait(ev)


def build():
    nc = bass.Bass("TRN2", target_bir_lowering=False)
    xT_in = nc.dram_tensor("xT", [D, NLAT], F32, kind="ExternalInput").ap()
    ctxT_in = nc.dram_tensor("ctxT", [D, NCTX], F32, kind="ExternalInput").ap()
    cvec_in = nc.dram_tensor("cvec", [128, 32], F32, kind="ExternalInput").ap()
    wpack = nc.dram_tensor("wpack", [DEPTH * NW // 2048, 2048], F32, kind="ExternalInput").ap()
    vec_in = nc.dram_tensor("vec", [DEPTH, 128, NV], F32, kind="ExternalInput").ap()
    cst_in = nc.dram_tensor("cst", [128, 384], F32, kind="ExternalInput").ap()
    rope_in = nc.dram_tensor("rope", [64, 2, T], F32, kind="ExternalInput").ap()
    tabl_in = nc.dram_tensor("tabl", [8, 128, 32 * 2 * 512], BF16, kind="ExternalInput").ap()
    tabc_in = nc.dram_tensor("tabc", [128, 2 * 2 * 256], BF16, kind="ExternalInput").ap()
    yT = nc.dram_tensor("yT", [D, NLAT], F32, kind="ExternalOutput").ap()

    wbf2 = [nc.dram_tensor(f"wbf{l}", [NW // 2048, 2048], BF16).ap() for l in range(DEPTH)]
    wbf = [w.rearrange("a b -> (a b)") for w in wbf2]
    XA = nc.dram_tensor("XA", [D, T], F32).ap()
    XB = nc.dram_tensor("XB", [D, T], F32).ap()
    HXT = nc.dram_tensor("HXT", [D, T], BF16).ap()
    AB = nc.dram_tensor("AB", [T, 2048], BF16).ap()
    VT = nc.dram_tensor("VT", [1024, VTW], BF16).ap()
    YFT = nc.dram_tensor("YFT", [1024, T], BF16).ap()
    CST = nc.dram_tensor("CST", [1024, T], BF16).ap()
    QNT = nc.dram_tensor("QNT", [768, T], BF16).ap()
    KVNT = nc.dram_tensor("KVNT", [512, T], BF16).ap()
    KPET = nc.dram_tensor("KPET", [64, T], BF16).ap()
    OT = nc.dram_tensor("OT", [D, T], BF16).ap()

    NT = len(TILES)
    dbufs = {nm: [Buf() for _ in range(NT)] for nm in
             ["XA", "XB", "HXT", "AB", "VT", "YFT", "CST", "QNT", "KVNT", "KPET", "OT", "Y"]}
    wb = [Buf() for _ in range(DEPTH)]
    nobuf = Buf()

    with ExitStack() as es:
        K = Kern(nc, es)

        uniq = {"n": 0}

        def TL(st, name, shape, dt):
            uniq["n"] += 1
            return st.enter_context(nc.sbuf_tensor(f"{name}_{uniq['n']}", list(shape), dt)), Buf()

        ps = [es.enter_context(nc.psum_tensor(f"ps{i}", [128, 512], F32)) for i in range(8)]
        psb = [Buf() for _ in range(8)]
        rot = {"i": 0}

        def nextps(cands):
            rot["i"] += 1
            return cands[rot["i"] % len(cands)]

        cstf, cstf_b = TL(es, "cstf", [128, 384], F32)
        csm, csm_b = TL(es, "csm", [128, 256], BF16)
        ones, ones_b = TL(es, "ones", [128, 128], BF16)
        zer, zer_b = TL(es, "zer", [128, 8, 30], BF16)
        vecs, vec_b = [], []
        for l in range(DEPTH):
            t_, b_ = TL(es, f"vec{l}", [128, NV], F32)
            vecs.append(t_)
            vec_b.append(b_)
        modv, modv_b = TL(es, "modv", [128, 96, 2], F32)
        S1, S1_b = TL(es, "S1", [128, 16, 2], F32)
        S2, S2_b = TL(es, "S2", [128, 16, 2], F32)
        zb, zb_b = TL(es, "zb", [128, 1], F32)

        K.dma("sp", cstf[:], cst_in, [nobuf], [cstf_b])
        for l in range(DEPTH):
            K.dma("sp", vecs[l][:], vec_in[l], [nobuf], [vec_b[l]])
        K.op("act", lambda e: e.activation(out=csm[:], in_=cstf[:, 128:384], func=AF.Identity), [cstf_b], [csm_b])
        K.op("dve", lambda e: e.memset(ones[:], 1.0), [], [ones_b])
        K.op("dve", lambda e: e.memset(zer[:], 0.0), [], [zer_b])
        K.op("dve", lambda e: e.memset(zb[:], 0.0), [], [zb_b])
        epsb, epsb_b = TL(es, "epsb", [128, 1], F32)
        K.op("dve", lambda e: e.memset(epsb[:], EPS), [], [epsb_b])
        vtpad_b = Buf()
        for a0 in (0, VT_LAT0 + NLAT, VT_CTX0 + NCTX):
            w_ = 15 if a0 != VT_LAT0 + NLAT else 30
            K.dma("sp", VT[:, a0:a0 + w_].rearrange("(c p) t -> p c t", p=128), zer[:, :, 0:w_], [zer_b], [vtpad_b])

        cvb = [[], []]

        def convert_weights(l, trig=None):
            rows = NW // 2048
            crow = WLAY["ada"][0] // 2048
            assert WLAY["ada"][0] % 2048 == 0
            r0 = 0
            while r0 < crow:
                n = min(1024, crow - r0)
                b_ = Buf()
                K.dma("pool", wbf2[l][r0:r0 + n, :], wpack[l * rows + r0:l * rows + r0 + n, :],
                      [nobuf] if trig is None else [trig], [b_])
                cvb[l].append(b_)
                r0 += n

        wpack_flat = wpack.rearrange("a b -> (a b)")
        lastw = {"b": []}

        def wtile(l, name, ti):
            off, nt, kc, mw = WLAY[name]
            o = off + ti * 128 * kc * mw
            b0 = (o // 2048) // 1024
            b1 = ((o + 128 * kc * mw - 1) // 2048) // 1024
            lastw["b"] = cvb[l][b0:b1 + 1]
            return wbf[l][o:o + 128 * kc * mw].rearrange("(p k m) -> p k m", p=128, k=kc), kc, mw

        def V(l, nm, i=0, n=1):
            c = VEC_COLS[nm] + i
            return vecs[l][:, c:c + n]

        def xsrc(l, phase):
            pass

        def stage0(l):
            with ExitStack() as st:
                cv, cv_b = TL(st, "cv", [128, 32], F32)
                scb, scb_b = TL(st, "scb", [128, 16, 2], BF16)
                wsl = [TL(st, f"adaw{i}", [128, 16, 512], BF16) for i in range(2)]
                K.dma("sp", cv[:], cvec_in, [nobuf], [cv_b])
                K.op("act", lambda e: e.activation(out=scb[:].rearrange("p k j -> p (k j)"), in_=cv[:], func=AF.Silu),
                     [cv_b], [scb_b])
                for ti in range(24):
                    wt, wt_b = wsl[ti % 2]
                    off_, _, kc_, mw_ = WLAY["ada"]
                    o_ = l * NW + off_ + ti * 128 * kc_ * mw_
                    K.dma("pool", wt[:], wpack_flat[o_:o_ + 128 * kc_ * mw_].rearrange("(p k m) -> p k m", p=128, k=kc_),
                          [nobuf], [wt_b], max_dma_last_dim=4096)
                    for mi in range(4):
                        m = ti * 4 + mi
                        K.mm(ps[0][:, 2 * m:2 * m + 2], psb[0],
                             [(wt[:, k, mi * 128:(mi + 1) * 128], scb[:, k, :]) for k in range(KC)], [wt_b, scb_b])
                vl = vecs[l]
                for j in range(2):
                    K.op("dve", lambda e, j=j: e.tensor_tensor(
                        out=modv[:, :, j], in0=ps[0][:, 0:192].rearrange("p (m j) -> p m j", j=2)[:, :, j],
                        in1=vl[:, VEC_COLS["ada_b"]:VEC_COLS["ada_b"] + 96], op=ALU.add),
                        [psb[0], vec_b[l]], [modv_b])
                for j in range(2):
                    K.op("dve", lambda e, j=j: e.scalar_tensor_tensor(
                        out=S1[:, :, j], in0=modv[:, 16:32, j], scalar=1.0,
                        in1=vl[:, VEC_COLS["gmix"]:VEC_COLS["gmix"] + 16], op0=ALU.add, op1=ALU.mult),
                        [modv_b, vec_b[l]], [S1_b])
                    K.op("dve", lambda e, j=j: e.scalar_tensor_tensor(
                        out=S2[:, :, j], in0=modv[:, 64:80, j], scalar=1.0,
                        in1=vl[:, VEC_COLS["gffn"]:VEC_COLS["gffn"] + 16], op0=ALU.add, op1=ALU.mult),
                        [modv_b, vec_b[l]], [S2_b])
                K.barrier()

        def front(src_ap, src_b, TT, xt, xt_b, sq, sq_b, rstd, rstd_b, tmps, out_fn, scale_fn, bias_fn, extra_reads,
                  nfeat=D):
            K.dma("sp", xt[:, :, :TT], src_ap.rearrange("(k p) t -> p k t", p=128), [src_b], [xt_b])
            K.op("act", lambda e: e.activation(out=sq[:, :, :TT], in_=xt[:, :, :TT], func=AF.Square), [xt_b], [sq_b])
            K.mm(ps[0][:, :TT], psb[0], [(ones[:], sq[:, k, :TT]) for k in range(KC)], [ones_b, sq_b])
            K.op("act", lambda e: e.activation(out=rstd[:, :TT], in_=ps[0][:, :TT], func=AF.Sqrt, scale=1.0 / nfeat,
                                               bias=epsb[:, 0:1]), [psb[0], epsb_b], [rstd_b])
            K.op("dve", lambda e: e.reciprocal(out=rstd[:, :TT], in_=rstd[:, :TT]), [rstd_b], [rstd_b])
            for k in range(KC):
                tm_, tm_b = tmps[k % 2]
                K.op("dve", lambda e, k=k, tm_=tm_: e.tensor_tensor(out=tm_[:, :TT], in0=xt[:, k, :TT], in1=rstd[:, :TT],
                                                                    op=ALU.mult), [xt_b, rstd_b], [tm_b])
                o_ap, o_b = out_fn(k)
                K.op("act", lambda e, k=k, tm_=tm_, o_ap=o_ap: e.activation(
                    out=o_ap, in_=tm_[:, :TT], func=AF.Identity, scale=scale_fn(k), bias=bias_fn(k)),
                    [tm_b] + extra_reads, [o_b])

        def rms_feat(nchunks, nfeat, TT, qd, qd_b, sqq, sqq_b, rstd, rstd_b, outt, outt_b, gcol, l):
            K.mm(ps[0][:, :TT], psb[0], [(ones[:], sqq[:, c, :TT]) for c in range(nchunks)], [ones_b, sqq_b])
            K.op("act", lambda e: e.activation(out=rstd[:, :TT], in_=ps[0][:, :TT], func=AF.Sqrt, scale=1.0 / nfeat,
                                               bias=epsb[:, 0:1]), [psb[0], epsb_b], [rstd_b])
            K.op("dve", lambda e: e.reciprocal(out=rstd[:, :TT], in_=rstd[:, :TT]), [rstd_b], [rstd_b])
            for c in range(nchunks):
                K.op("dve", lambda e, c=c: e.scalar_tensor_tensor(
                    out=outt[:, c, :TT], in0=qd[:, c, :TT], scalar=V(l, gcol, c), in1=rstd[:, :TT],
                    op0=ALU.mult, op1=ALU.mult), [qd_b, rstd_b, vec_b[l]], [outt_b])

        def stage1(l, last):
            with ExitStack() as st:
                xt, xt_b = TL(st, "xt", [128, 16, 512], F32)
                sq, sq_b = TL(st, "sq", [128, 16, 512], BF16)
                hx, hx_b = TL(st, "hx", [128, 16, 512], BF16)
                rstd, rstd_b = TL(st, "rstd", [128, 512], F32)
                tmps = [TL(st, f"tmp{i}", [128, 512], F32) for i in range(2)]
                wsl = [TL(st, f"w1_{i}", [128, 16, 512], BF16) for i in range(2)]
                ut = [TL(st, f"ut{i}", [128, 512], BF16) for i in range(2)]
                abt, abt_b = TL(st, "abt", [128, 4, 8, 256], BF16)
                sg, sg_b = TL(st, "sg", [128, 4, 512], BF16)
                vt, vt_b = TL(st, "vt", [128, 8, 512], BF16)
                qd, qd_b = TL(st, "qd", [128, 6, 512], F32)
                sqq, sqq_b = TL(st, "sqq", [128, 6, 512], BF16)
                qn, qn_b = TL(st, "qn", [128, 6, 512], BF16)
                kvn, kvn_b = TL(st, "kvn", [128, 4, 512], BF16)
                rp, rp_b = TL(st, "rp", [64, 2, 512], F32)
                r1, r1_b = TL(st, "r1", [64, 512], F32)
                r2, r2_b = TL(st, "r2", [64, 512], F32)
                kr, kr_b = TL(st, "kr", [64, 512], BF16)
                wi = 0
                for ti, (t0, TT, isctx) in enumerate(TILES):
                    j = 1 if isctx else 0
                    if l == 0:
                        src = ctxT_in[:, :] if isctx else xT_in[:, t0:t0 + TT]
                        src_b = nobuf
                    else:
                        src, src_b = XB[:, t0:t0 + TT], dbufs["XB"][ti]
                    front(src, src_b, TT, xt, xt_b, sq, sq_b, rstd, rstd_b, tmps,
                          lambda k: (hx[:, k, :TT], hx_b), lambda k: S1[:, k, j:j + 1], lambda k: modv[:, k, j:j + 1],
                          [S1_b, modv_b])
                    only_kv = last and isctx
                    if not only_kv:
                        K.dma("act", HXT[:, t0:t0 + TT].rearrange("(k p) t -> p k t", p=128), hx[:, :, :TT], [hx_b],
                              [dbufs["HXT"][ti]])
                    K.dma("sp", rp[:, :, :TT], rope_in[:, :, t0:t0 + TT], [nobuf], [rp_b])
                    for si, (nm, c0, w) in enumerate(S1_TILES):
                        if only_kv and nm not in ("KV", "KPE"):
                            continue
                        wt, wt_b = wsl[wi % 2]
                        wi += 1
                        src_w, _, _ = wtile(l, f"s1_{si}", 0)
                        K.dma("sp", wt[:, :, :w], src_w, lastw["b"], [wt_b])
                        if nm == "KPE":
                            pa, pb_ = 5, 6
                            K.mm(ps[pa][0:64, :TT], psb[pa], [(wt[:, k, 0:64], hx[:, k, :TT]) for k in range(KC)],
                                 [wt_b, hx_b])
                            K.mm(ps[pb_][0:64, :TT], psb[pb_], [(wt[:, k, 64:128], hx[:, k, :TT]) for k in range(KC)],
                                 [wt_b, hx_b])
                            K.op("dve", lambda e: e.tensor_tensor(out=r1[:, :TT], in0=ps[pa][0:64, :TT], in1=rp[:, 0, :TT],
                                                                  op=ALU.mult), [psb[pa], rp_b], [r1_b])
                            K.op("dve", lambda e: e.tensor_tensor(out=r2[:, :TT], in0=ps[pb_][0:64, :TT],
                                                                  in1=rp[:, 1, :TT], op=ALU.mult), [psb[pb_], rp_b], [r2_b])
                            K.op("dve", lambda e: e.tensor_tensor(out=kr[:, :TT], in0=r1[:, :TT], in1=r2[:, :TT],
                                                                   op=ALU.add), [r1_b, r2_b], [kr_b])
                            K.dma("act", KPET[:, t0:t0 + TT], kr[:, :TT], [kr_b], [dbufs["KPET"][ti]])
                            continue
                        for mi in range(w // 128):
                            c = c0 // 128 + mi
                            p = nextps([1, 2, 3, 4])
                            K.mm(ps[p][:, :TT], psb[p], [(wt[:, k, mi * 128:(mi + 1) * 128], hx[:, k, :TT])
                                                         for k in range(KC)], [wt_b, hx_b])
                            if nm == "F":
                                u_, u_b = ut[c % 2]
                                K.op("act", lambda e, u_=u_, p=p: e.activation(out=u_[:, :TT], in_=ps[p][:, :TT],
                                                                               func=AF.Identity), [psb[p]], [u_b])
                                for s in range(TT // 128):
                                    p2 = nextps([6, 7])
                                    K.mm(ps[p2][:, 0:256], psb[p2], [(u_[:, s * 128:(s + 1) * 128], csm[:])],
                                         [u_b, csm_b])
                                    K.op("dve", lambda e, s=s, c=c, p2=p2: e.tensor_copy(out=abt[:, s, c, :],
                                                                                         in_=ps[p2][:, 0:256]),
                                         [psb[p2]], [abt_b])
                            elif nm == "G":
                                K.op("act", lambda e, mi=mi, p=p: e.activation(out=sg[:, mi, :TT], in_=ps[p][:, :TT],
                                                                               func=AF.Sigmoid), [psb[p]], [sg_b])
                            elif nm == "A":
                                K.op("dve", lambda e, mi=mi, c=c, p=p: e.tensor_tensor(
                                    out=vt[:, c, :TT], in0=ps[p][:, :TT], in1=sg[:, mi, :TT], op=ALU.mult),
                                    [psb[p], sg_b], [vt_b])
                            elif nm == "Q":
                                K.op("act", lambda e, c=c, p=p: e.activation(out=qd[:, c, :TT], in_=ps[p][:, :TT],
                                                                             func=AF.Identity), [psb[p]], [qd_b])
                                K.op("act", lambda e, c=c, p=p: e.activation(out=sqq[:, c, :TT], in_=ps[p][:, :TT],
                                                                             func=AF.Square), [psb[p]], [sqq_b])
                            elif nm == "KV":
                                K.op("act", lambda e, c=c, p=p: e.activation(out=qd[:, c, :TT], in_=ps[p][:, :TT],
                                                                             func=AF.Identity), [psb[p]], [qd_b])
                                K.op("act", lambda e, c=c, p=p: e.activation(out=sqq[:, c, :TT], in_=ps[p][:, :TT],
                                                                             func=AF.Square), [psb[p]], [sqq_b])
                        if nm == "F" and c0 == 512:
                            K.dma("act", AB[t0:t0 + TT, :].rearrange("(s p) c -> p s c", p=128),
                                  abt[:, 0:TT // 128, :, :].rearrange("p s g x -> p s (g x)"), [abt_b], [dbufs["AB"][ti]])
                        if nm == "A" and c0 == 512:
                            v0 = (VT_CTX0 if isctx else VT_LAT0 + t0)
                            K.dma("act", VT[:, v0:v0 + TT].rearrange("(c p) t -> p c t", p=128), vt[:, :, :TT], [vt_b],
                                  [dbufs["VT"][ti]])
                        if nm == "Q" and c0 == 512:
                            rms_feat(6, 768, TT, qd, qd_b, sqq, sqq_b, rstd, rstd_b, qn, qn_b, "qg", l)
                            K.dma("act", QNT[:, t0:t0 + TT].rearrange("(c p) t -> p c t", p=128), qn[:, :, :TT], [qn_b],
                                  [dbufs["QNT"][ti]])
                        if nm == "KV":
                            rms_feat(4, 512, TT, qd, qd_b, sqq, sqq_b, rstd, rstd_b, kvn, kvn_b, "kvg", l)
                            K.dma("act", KVNT[:, t0:t0 + TT].rearrange("(c p) t -> p c t", p=128), kvn[:, :, :TT],
                                  [kvn_b], [dbufs["KVNT"][ti]])
                K.barrier()

        def stage2_dft(l, last):
            with ExitStack() as st:
                tb = st.enter_context(nc.sbuf_tensor(f"tb{l}", [128, 32, 2, 512], BF16))
                tb_b = [Buf() for _ in range(32)]
                abg = [TL(st, f"abg{i}", [128, 32, 256], BF16) for i in range(2)]
                yf = [TL(st, f"yfo{i}", [128, 512], BF16) for i in range(2)]
                gi = 0
                jobs = [(kt, False) for kt in range(8)] + ([] if last else [(0, True)])
                for kt, isctx in jobs:
                    if isctx:
                        ncn, KW, ti, n0 = 2, 256, 8, NLAT
                        K.dma("sp", tb[:, 0:2, :, 0:256], tabc_in.rearrange("p (c j k) -> p c j k", c=2, j=2),
                              [nobuf], tb_b[0:2])
                        scale = (NCTX * 128.0) ** -0.5
                    else:
                        ncn, KW, ti, n0 = 32, 512, kt, 0
                        tsrc = tabl_in[kt].rearrange("p (c j k) -> p c j k", c=32, j=2)
                        for c4 in range(8):
                            K.dma("sp", tb[:, c4 * 4:(c4 + 1) * 4, :, :], tsrc[:, c4 * 4:(c4 + 1) * 4, :, :], [nobuf],
                                  tb_b[c4 * 4:(c4 + 1) * 4])
                        scale = (NLAT * 128.0) ** -0.5
                    for g in range(8):
                        ab_, ab_b = abg[gi % 2]
                        yo, yo_b = yf[gi % 2]
                        gi += 1
                        rd = dbufs["AB"][8:9] if isctx else dbufs["AB"][0:8]
                        K.dma("sp", ab_[:, 0:ncn, :],
                              AB[n0:n0 + ncn * 128, g * 256:(g + 1) * 256].rearrange("(c p) x -> p c x", p=128), rd,
                              [ab_b])
                        p = nextps([1, 2, 3, 4])
                        pairs = []
                        for c in range(ncn):
                            pairs.append((ab_[:, c, 0:128], tb[:, c, 0, 0:KW]))
                            pairs.append((ab_[:, c, 128:256], tb[:, c, 1, 0:KW]))
                        K.mm(ps[p][:, :KW], psb[p], pairs, [ab_b] + tb_b[0:ncn])
                        K.op("act", lambda e, p=p, yo=yo, KW=KW, scale=scale: e.activation(
                            out=yo[:, :KW], in_=ps[p][:, :KW], func=AF.Identity, scale=scale), [psb[p]], [yo_b])
                        t0 = n0 + kt * 512
                        K.dma("act", YFT[g * 128:(g + 1) * 128, t0:t0 + KW], yo[:, :KW], [yo_b], [dbufs["YFT"][ti]])
                K.barrier()

        def stage3_conv(l, last):
            with ExitStack() as st:
                dg, dg_b = TL(st, "dg", [128, 8, 31, 128], BF16)
                vh = [TL(st, f"vh{i}", [128, 542], BF16) for i in range(3)]
                cvb, cvb_b = TL(st, "cvb", [128, 8, 512], BF16)
                sq, sq_b = TL(st, "csq", [128, 8, 512], BF16)
                mean, mean_b = TL(st, "mean", [128, 512], F32)
                m2, m2_b = TL(st, "m2", [128, 512], F32)
                rstd, rstd_b = TL(st, "crstd", [128, 512], F32)
                nmr, nmr_b = TL(st, "nmr", [128, 512], F32)
                tmps = [TL(st, f"ctmp{i}", [128, 512], F32) for i in range(2)]
                cs, cs_b = TL(st, "cs", [128, 8, 512], BF16)
                for c in range(8):
                    for k in range(31):
                        K.op("dve", lambda e, c=c, k=k: e.tensor_scalar(
                            out=dg[:, c, k, :], in0=cstf[:, 0:128], scalar1=V(l, "convw", c * 31 + k), scalar2=None,
                            op0=ALU.mult), [cstf_b, vec_b[l]], [dg_b])
                vi = 0
                for ti, (t0, TT, isctx) in enumerate(TILES):
                    if isctx and last:
                        continue
                    v0 = (VT_CTX0 if isctx else VT_LAT0 + t0) - 15
                    rdl = [dbufs["VT"][ti], vtpad_b]
                    if not isctx:
                        if ti > 0:
                            rdl.append(dbufs["VT"][ti - 1])
                        if ti < 7:
                            rdl.append(dbufs["VT"][ti + 1])
                    for c in range(8):
                        vh_, vh_b = vh[vi % 3]
                        vi += 1
                        K.dma("sp", vh_[:, 0:TT + 30], VT[c * 128:(c + 1) * 128, v0:v0 + TT + 30], rdl, [vh_b])
                        p = nextps([1, 2, 3, 4])
                        K.mm(ps[p][:, :TT], psb[p], [(dg[:, c, k, :], vh_[:, k:k + TT]) for k in range(31)],
                             [dg_b, vh_b])
                        K.op("act", lambda e, c=c, p=p: e.activation(out=cvb[:, c, :TT], in_=ps[p][:, :TT],
                                                                     func=AF.Identity, bias=V(l, "conv_b", c)),
                             [psb[p], vec_b[l]], [cvb_b])
                        K.op("act", lambda e, c=c, p=p: e.activation(out=sq[:, c, :TT], in_=ps[p][:, :TT],
                                                                     func=AF.Square, bias=V(l, "conv_b", c)),
                             [psb[p], vec_b[l]], [sq_b])
                    K.mm(ps[5][:, :TT], psb[5], [(ones[:], cvb[:, c, :TT]) for c in range(8)], [ones_b, cvb_b])
                    K.mm(ps[6][:, :TT], psb[6], [(ones[:], sq[:, c, :TT]) for c in range(8)], [ones_b, sq_b])
                    K.op("dve", lambda e: e.tensor_scalar(out=mean[:, :TT], in0=ps[5][:, :TT], scalar1=1.0 / 1024,
                                                          scalar2=None, op0=ALU.mult), [psb[5]], [mean_b])
                    K.op("dve", lambda e: e.tensor_tensor(out=m2[:, :TT], in0=mean[:, :TT], in1=mean[:, :TT],
                                                          op=ALU.mult), [mean_b], [m2_b])
                    K.op("dve", lambda e: e.scalar_tensor_tensor(out=rstd[:, :TT], in0=ps[6][:, :TT], scalar=1.0 / 1024,
                                                                 in1=m2[:, :TT], op0=ALU.mult, op1=ALU.subtract),
                         [psb[6], m2_b], [rstd_b])
                    K.op("act", lambda e: e.activation(out=rstd[:, :TT], in_=rstd[:, :TT], func=AF.Sqrt,
                                                       bias=epsb[:, 0:1]), [rstd_b, epsb_b], [rstd_b])
                    K.op("dve", lambda e: e.reciprocal(out=rstd[:, :TT], in_=rstd[:, :TT]), [rstd_b], [rstd_b])
                    K.op("dve", lambda e: e.scalar_tensor_tensor(out=nmr[:, :TT], in0=mean[:, :TT], scalar=-1.0,
                                                                 in1=rstd[:, :TT], op0=ALU.mult, op1=ALU.mult),
                         [mean_b, rstd_b], [nmr_b])
                    for c in range(8):
                        tm_, tm_b = tmps[c % 2]
                        K.op("dve", lambda e, c=c, tm_=tm_: e.tensor_tensor(out=tm_[:, :TT], in0=cvb[:, c, :TT],
                                                                            in1=rstd[:, :TT], op=ALU.mult),
                             [cvb_b, rstd_b], [tm_b])
                        K.op("pool", lambda e, tm_=tm_: e.tensor_tensor(out=tm_[:, :TT], in0=tm_[:, :TT],
                                                                        in1=nmr[:, :TT], op=ALU.add), [tm_b, nmr_b],
                             [tm_b])
                        K.op("act", lambda e, c=c, tm_=tm_: e.activation(
                            out=cs[:, c, :TT], in_=tm_[:, :TT], func=AF.Silu, scale=V(l, "ln_g", c),
                            bias=V(l, "ln_b", c)), [tm_b, vec_b[l]], [cs_b])
                    K.dma("act", CST[:, t0:t0 + TT].rearrange("(c p) t -> p c t", p=128), cs[:, :, :TT], [cs_b],
                          [dbufs["CST"][ti]])
                K.barrier()

        def stage4_attn(l, last):
            with ExitStack() as st:
                kvn, kvn_b = TL(st, "akvn", [128, 4, T], BF16)
                kra, kra_b = TL(st, "akr", [128, T], BF16)
                v4, v4_b = TL(st, "v4", [128, 34, 512], BF16)
                kh = [TL(st, f"kh{i}", [128, T], BF16) for i in range(2)]
                qn = [TL(st, f"aqn{i}", [128, 6, 512], BF16) for i in range(3)]
                rp = [TL(st, f"arp{i}", [64, 2, 512], F32) for i in range(3)]
                qnp = [TL(st, f"qnp{i}", [128, 512], BF16) for i in range(2)]
                qr = [TL(st, f"qr{i}", [128, 512], BF16) for i in range(2)]
                r1, r1_b = TL(st, "ar1", [64, 512], F32)
                r2, r2_b = TL(st, "ar2", [64, 512], F32)
                pt = [TL(st, f"pt{i}", [128, 512], BF16) for i in range(6)]
                accs = [TL(st, f"acc{i}", [128, 512], F32) for i in range(2)]
                dhi, dhi_b = TL(st, "dhi", [128, 512], BF16)
                dlo, dlo_b = TL(st, "dlo", [128, 512], BF16)
                rd, rd_b = TL(st, "rd", [128, 512], F32)
                ob = [TL(st, f"ob{i}", [128, 512], BF16) for i in range(2)]
                wuk = [TL(st, f"wuk{i}", [128, 4, 128], BF16) for i in range(2)]
                wuv, wuv_b = TL(st, "wuv", [128, 4, 512], BF16)
                wuq = [TL(st, f"wuq{i}", [128, 6, 256], BF16) for i in range(2)]
                trg, _ = TL(st, "trg", [128, 1], F32)
                K.dma("sp", kvn[:], KVNT.rearrange("(c p) t -> p c t", p=128), dbufs["KVNT"], [kvn_b])
                K.op("dve", lambda e: e.memset(kra[64:128, :], 0.0), [], [kra_b])
                for qr_i, qr_ib in qr:
                    K.op("dve", lambda e, qr_i=qr_i: e.memset(qr_i[64:128, :], 0.0), [], [qr_ib])
                K.dma("sp", kra[0:64, :], KPET, dbufs["KPET"], [kra_b])
                qi = 0
                pi = 0
                oi = 0
                for hg in range(4):
                    srcv, _, _ = wtile(l, "wuv", hg)
                    K.dma("sp", wuv[:], srcv, lastw["b"], [wuv_b])
                    for ck in range(34):
                        p = nextps([0, 1, 2, 7, 6])
                        K.mm(ps[p][:, :], psb[p], [(kvn[:, kc, ck * 128:(ck + 1) * 128], wuv[:, kc, :]) for kc in range(4)],
                             [kvn_b, wuv_b])
                        if ck % 2 == 0:
                            K.op("dve", lambda e, ck=ck, p=p: e.tensor_copy(out=v4[:, ck, :], in_=ps[p][:, :]), [psb[p]],
                                 [v4_b])
                        else:
                            K.op("act", lambda e, ck=ck, p=p: e.activation(out=v4[:, ck, :], in_=ps[p][:, :],
                                                                           func=AF.Identity), [psb[p]], [v4_b])
                    for hl in range(4):
                        h = hg * 4 + hl
                        wk_, wk_b = wuk[h % 2]
                        wq_, wq_b = wuq[h % 2]
                        kh_, kh_b = kh[h % 2]
                        srck, _, _ = wtile(l, "wuk", h)
                        K.dma("sp", wk_[:], srck, lastw["b"], [wk_b])
                        srcq, _, _ = wtile(l, "wuq", h)
                        K.dma("sp", wq_[:], srcq, lastw["b"], [wq_b])
                        for kt in range(9):
                            k0 = kt * 512
                            KW = min(512, T - k0)
                            p = nextps([0, 1, 2, 7, 6])
                            K.mm(ps[p][:, :KW], psb[p], [(wk_[:, kc, :], kvn[:, kc, k0:k0 + KW]) for kc in range(4)],
                                 [wk_b, kvn_b])
                            K.op("dve", lambda e, p=p, k0=k0, KW=KW, kh_=kh_: e.tensor_copy(out=kh_[:, k0:k0 + KW],
                                                                                           in_=ps[p][:, :KW]),
                                 [psb[p]], [kh_b])
                        qtiles = [(ti, t0, TT, isctx) for ti, (t0, TT, isctx) in enumerate(TILES) if not (isctx and last)]
                        qst = {}

                        def load_q(idx, qtiles=qtiles, qst=qst):
                            ti, t0, TT, isctx = qtiles[idx]
                            qn_, qn_b = qn[idx % 3]
                            rp_, rp_b = rp[idx % 3]
                            K.dma("sp", qn_[:, :, :TT], QNT[:, t0:t0 + TT].rearrange("(c p) t -> p c t", p=128),
                                  [dbufs["QNT"][ti]], [qn_b])
                            K.dma("sp", rp_[:, :, :TT], rope_in[:, :, t0:t0 + TT], [nobuf], [rp_b])

                        def proj_q(idx, qtiles=qtiles, qst=qst, wq_=wq_, wq_b=wq_b):
                            ti, t0, TT, isctx = qtiles[idx]
                            qn_, qn_b = qn[idx % 3]
                            rp_, rp_b = rp[idx % 3]
                            qnp_, qnp_b = qnp[idx % 2]
                            qr_, qr_b = qr[idx % 2]
                            pa = nextps([0, 1, 2, 7, 6])
                            K.mm(ps[pa][:, :TT], psb[pa], [(wq_[:, kc, 0:128], qn_[:, kc, :TT]) for kc in range(6)],
                                 [wq_b, qn_b])
                            K.op("act", lambda e: e.activation(out=qnp_[:, :TT], in_=ps[pa][:, :TT],
                                                               func=AF.Identity, scale=ATTN_SCALE), [psb[pa]], [qnp_b])
                            pb1 = nextps([0, 1, 2, 7, 6])
                            K.mm(ps[pb1][0:64, :TT], psb[pb1], [(wq_[:, kc, 128:192], qn_[:, kc, :TT]) for kc in range(6)],
                                 [wq_b, qn_b])
                            K.op("dve", lambda e: e.scalar_tensor_tensor(
                                out=r1[:, :TT], in0=ps[pb1][0:64, :TT], scalar=ATTN_SCALE, in1=rp_[:, 0, :TT],
                                op0=ALU.mult, op1=ALU.mult), [psb[pb1], rp_b], [r1_b])
                            pb2 = nextps([0, 1, 2, 7, 6])
                            K.mm(ps[pb2][0:64, :TT], psb[pb2], [(wq_[:, kc, 192:256], qn_[:, kc, :TT]) for kc in range(6)],
                                 [wq_b, qn_b])
                            K.op("dve", lambda e: e.scalar_tensor_tensor(
                                out=r2[:, :TT], in0=ps[pb2][0:64, :TT], scalar=ATTN_SCALE, in1=rp_[:, 1, :TT],
                                op0=ALU.mult, op1=ALU.mult), [psb[pb2], rp_b], [r2_b])
                            K.op("dve", lambda e: e.tensor_tensor(out=qr_[0:64, :TT], in0=r1[:, :TT], in1=r2[:, :TT],
                                                                  op=ALU.add), [r1_b, r2_b], [qr_b])

                        load_q(0)
                        if len(qtiles) > 1:
                            load_q(1)
                        proj_q(0)
                        for idx, (ti, t0, TT, isctx) in enumerate(qtiles):
                            if idx + 2 < len(qtiles):
                                load_q(idx + 2)
                            if idx + 1 < len(qtiles):
                                proj_q(idx + 1)
                            qnp_, qnp_b = qnp[idx % 2]
                            qr_, qr_b = qr[idx % 2]
                            cks = [32, 33] if isctx else list(range(34))
                            ncks = len(cks)
                            po = 3 + (oi % 2)
                            pd = 5
                            ob_, ob_b = ob[oi % 2]
                            oi += 1
                            sbank = {}
                            used = {}

                            def emit_S(i):
                                ck = cks[i]
                                p = nextps([0, 1, 2, 7, 6])
                                K.mm(ps[p][:, :TT], psb[p],
                                     [(kh_[:, ck * 128:(ck + 1) * 128], qnp_[:, :TT]),
                                      (kra[:, ck * 128:(ck + 1) * 128], qr_[:, :TT])],
                                     [kh_b, qnp_b, kra_b, qr_b])
                                sbank[i] = p

                            LOOK = 3
                            for i in range(min(LOOK, ncks)):
                                emit_S(i)
                            for i, ck in enumerate(cks):
                                if i + LOOK < ncks:
                                    emit_S(i + LOOK)
                                p = sbank[i]
                                pt_, pt_b = pt[pi % 6]
                                pi += 1
                                K.op("act", lambda e, p=p, pt_=pt_: e.activation(out=pt_[:, :TT], in_=ps[p][:, :TT],
                                                                                 func=AF.Exp), [psb[p]], [pt_b])
                                K.mm(ps[po][:, :TT], psb[po], [(v4[:, ck, hl * 128:(hl + 1) * 128], pt_[:, :TT])],
                                     [v4_b, pt_b], start=(i == 0), stop=(i == ncks - 1))
                                r3 = i % 3
                                if r3 == 0:
                                    K.mm(ps[pd][:, :TT], psb[pd], [(ones[:], pt_[:, :TT])], [ones_b, pt_b],
                                         start=(i == 0), stop=False)
                                else:
                                    ae = "pool" if r3 == 1 else "dve"
                                    ac_, ac_b = accs[r3 - 1]
                                    if r3 not in used:
                                        used[r3] = 1
                                        K.op(ae, lambda e, ac_=ac_, pt_=pt_: e.tensor_copy(out=ac_[:, :TT], in_=pt_[:, :TT]),
                                             [pt_b], [ac_b])
                                    else:
                                        K.op(ae, lambda e, ac_=ac_, pt_=pt_: e.tensor_tensor(
                                            out=ac_[:, :TT], in0=ac_[:, :TT], in1=pt_[:, :TT], op=ALU.add),
                                            [pt_b, ac_b], [ac_b])
                            if 2 in used:
                                K.op("dve", lambda e: e.tensor_tensor(out=accs[0][0][:, :TT], in0=accs[0][0][:, :TT],
                                                                      in1=accs[1][0][:, :TT], op=ALU.add),
                                     [accs[0][1], accs[1][1]], [accs[0][1]])
                            K.op("dve", lambda e: e.tensor_copy(out=dhi[:, :TT], in_=accs[0][0][:, :TT]), [accs[0][1]],
                                 [dhi_b])
                            K.op("dve", lambda e: e.tensor_tensor(out=dlo[:, :TT], in0=accs[0][0][:, :TT], in1=dhi[:, :TT],
                                                                  op=ALU.subtract), [accs[0][1], dhi_b], [dlo_b])
                            K.mm(ps[pd][:, :TT], psb[pd], [(ones[:], dhi[:, :TT]), (ones[:], dlo[:, :TT])],
                                 [ones_b, dhi_b, dlo_b], start=False, stop=True)
                            K.op("dve", lambda e, pd=pd: e.reciprocal(out=rd[:, :TT], in_=ps[pd][:, :TT]), [psb[pd]],
                                 [rd_b])
                            K.op("dve", lambda e, po=po, ob_=ob_: e.tensor_tensor(out=ob_[:, :TT], in0=ps[po][:, :TT],
                                                                                  in1=rd[:, :TT], op=ALU.mult),
                                 [psb[po], rd_b], [ob_b])
                            K.dma("act", OT[h * 128:(h + 1) * 128, t0:t0 + TT], ob_[:, :TT], [ob_b],
                                  [dbufs["OT"][ti]])
                        if h == 0 and l + 1 < DEPTH:
                            trig_b = Buf()
                            K.op("dve", lambda e: e.memset(trg[:], 0.0), [], [trig_b])
                            convert_weights(l + 1, trig_b)
                K.barrier()

        def stage5a(l, last):
            with ExitStack() as st:
                hx, hx_b = TL(st, "mhx", [128, 16, 512], BF16)
                yf, yf_b = TL(st, "myf", [128, 8, 512], BF16)
                cs, cs_b = TL(st, "mcs", [128, 8, 512], BF16)
                ot, ot_b = TL(st, "mot", [128, 16, 512], BF16)
                gw = [TL(st, f"gw{i}", [128, 16, 768], BF16) for i in range(2)]
                wf = [TL(st, f"mwf{i}", [128, 8, 256], BF16) for i in range(2)]
                wc = [TL(st, f"mwc{i}", [128, 8, 256], BF16) for i in range(2)]
                wo = [TL(st, f"mwo{i}", [128, 16, 256], BF16) for i in range(2)]
                wout = [TL(st, f"mwout{i}", [128, 16, 256], BF16) for i in range(2)]
                mg, mg_b = TL(st, "mg", [128, 16, 512], BF16)
                sgs = [TL(st, f"msg{i}", [128, 512], F32) for i in range(3)]
                ms = [TL(st, f"mm{i}", [128, 512], F32) for i in range(3)]
                xr = [TL(st, f"xr{i}", [128, 512], F32) for i in range(2)]
                xo = [TL(st, f"xo{i}", [128, 512], F32) for i in range(2)]
                wi = 0
                xi = 0
                for ti, (t0, TT, isctx) in enumerate(TILES):
                    if isctx and last:
                        continue
                    jj_ = 1 if isctx else 0
                    K.dma("sp", hx[:, :, :TT], HXT[:, t0:t0 + TT].rearrange("(k p) t -> p k t", p=128),
                          [dbufs["HXT"][ti]], [hx_b])
                    K.dma("sp", yf[:, :, :TT], YFT[:, t0:t0 + TT].rearrange("(k p) t -> p k t", p=128),
                          [dbufs["YFT"][ti]], [yf_b])
                    K.dma("sp", cs[:, :, :TT], CST[:, t0:t0 + TT].rearrange("(k p) t -> p k t", p=128),
                          [dbufs["CST"][ti]], [cs_b])
                    K.dma("sp", ot[:, :, :TT], OT[:, t0:t0 + TT].rearrange("(k p) t -> p k t", p=128),
                          [dbufs["OT"][ti]], [ot_b])
                    for jp in range(8):
                        gw_, gw_b = gw[wi % 2]
                        wf_, wf_b = wf[wi % 2]
                        wc_, wc_b = wc[wi % 2]
                        wo_, wo_b = wo[wi % 2]
                        wi += 1
                        K.dma("sp", gw_[:], wtile(l, "gate", jp)[0], lastw["b"], [gw_b])
                        K.dma("sp", wf_[:], wtile(l, "wf", jp)[0], lastw["b"], [wf_b])
                        K.dma("sp", wc_[:], wtile(l, "wc", jp)[0], lastw["b"], [wc_b])
                        K.dma("sp", wo_[:], wtile(l, "wo", jp)[0], lastw["b"], [wo_b])
                        for jj in range(2):
                            j = jp * 2 + jj
                            pg = [0, 1, 2]
                            py = [3, 4, 5]
                            for bi in range(3):
                                K.mm(ps[pg[bi]][:, :TT], psb[pg[bi]],
                                     [(gw_[:, k, jj * 384 + bi * 128:jj * 384 + (bi + 1) * 128], hx[:, k, :TT])
                                      for k in range(KC)], [gw_b, hx_b])
                                K.op("act", lambda e, bi=bi, j=j: e.activation(
                                    out=sgs[bi][0][:, :TT], in_=ps[pg[bi]][:, :TT], func=AF.Sigmoid,
                                    bias=V(l, "bgate", bi * 16 + j)), [psb[pg[bi]], vec_b[l]], [sgs[bi][1]])
                            K.mm(ps[3][:, :TT], psb[3], [(wf_[:, k, jj * 128:(jj + 1) * 128], yf[:, k, :TT])
                                                         for k in range(8)], [wf_b, yf_b])
                            K.mm(ps[4][:, :TT], psb[4], [(wc_[:, k, jj * 128:(jj + 1) * 128], cs[:, k, :TT])
                                                         for k in range(8)], [wc_b, cs_b])
                            K.mm(ps[5][:, :TT], psb[5], [(wo_[:, k, jj * 128:(jj + 1) * 128], ot[:, k, :TT])
                                                         for k in range(KC)], [wo_b, ot_b])
                            for bi in range(3):
                                K.op("dve", lambda e, bi=bi: e.tensor_tensor(
                                    out=ms[bi][0][:, :TT], in0=ps[py[bi]][:, :TT], in1=sgs[bi][0][:, :TT], op=ALU.mult),
                                    [psb[py[bi]], sgs[bi][1]], [ms[bi][1]])
                            K.op("pool", lambda e: e.tensor_tensor(out=ms[0][0][:, :TT], in0=ms[0][0][:, :TT],
                                                                   in1=ms[1][0][:, :TT], op=ALU.add),
                                 [ms[0][1], ms[1][1]], [ms[0][1]])
                            K.op("pool", lambda e, j=j: e.tensor_tensor(out=mg[:, j, :TT], in0=ms[0][0][:, :TT],
                                                                        in1=ms[2][0][:, :TT], op=ALU.add),
                                 [ms[0][1], ms[2][1]], [mg_b])
                    xs_ap = (ctxT_in if isctx else xT_in[:, t0:t0 + TT]) if l == 0 else XB[:, t0:t0 + TT]
                    xs_b = nobuf if l == 0 else dbufs["XB"][ti]
                    for mp in range(8):
                        wo2, wo2_b = wout[mp % 2]
                        K.dma("sp", wo2[:], wtile(l, "wout", mp)[0], lastw["b"], [wo2_b])
                        for jj in range(2):
                            j = mp * 2 + jj
                            xr_, xr_b = xr[xi % 2]
                            xo_, xo_b = xo[xi % 2]
                            xi += 1
                            K.dma("sp", xr_[:, :TT], xs_ap[j * 128:(j + 1) * 128, :], [xs_b], [xr_b])
                            p = nextps([6, 7])
                            K.mm(ps[p][:, :TT], psb[p], [(wo2[:, k, jj * 128:(jj + 1) * 128], mg[:, k, :TT])
                                                         for k in range(KC)], [wo2_b, mg_b])
                            K.op("dve", lambda e, p=p, j=j, xr_=xr_, xo_=xo_: e.scalar_tensor_tensor(
                                out=xo_[:, :TT], in0=ps[p][:, :TT], scalar=modv[:, 32 + j, jj_:jj_ + 1], in1=xr_[:, :TT],
                                op0=ALU.mult, op1=ALU.add), [psb[p], xr_b, modv_b], [xo_b])
                            K.dma("act", XA[j * 128:(j + 1) * 128, t0:t0 + TT], xo_[:, :TT], [xo_b], [dbufs["XA"][ti]])
                K.barrier()

        def stage5b(l, last):
            with ExitStack() as st:
                xt, xt_b = TL(st, "fxt", [128, 16, 512], F32)
                sq, sq_b = TL(st, "fsq", [128, 16, 512], BF16)
                h2, h2_b = TL(st, "h2", [128, 16, 512], BF16)
                rstd, rstd_b = TL(st, "frstd", [128, 512], F32)
                tmps = [TL(st, f"ftmp{i}", [128, 512], F32) for i in range(2)]
                wg = [TL(st, f"fwg{i}", [128, 16, 256], BF16) for i in range(2)]
                wu = [TL(st, f"fwu{i}", [128, 16, 256], BF16) for i in range(2)]
                wd = [TL(st, f"fwd{i}", [128, 44, 128], BF16) for i in range(2)]
                hid, hid_b = TL(st, "hid", [128, HC, 512], BF16)
                sgl = [TL(st, f"fsg{i}", [128, 512], BF16) for i in range(2)]
                xo = [TL(st, f"fxo{i}", [128, 512], F32) for i in range(2)]
                wi = 0
                si = 0
                for ti, (t0, TT, isctx) in enumerate(TILES):
                    if isctx and last:
                        continue
                    jj_ = 1 if isctx else 0
                    front(XA[:, t0:t0 + TT], dbufs["XA"][ti], TT, xt, xt_b, sq, sq_b, rstd, rstd_b, tmps,
                          lambda k: (h2[:, k, :TT], h2_b), lambda k: S2[:, k, jj_:jj_ + 1],
                          lambda k: modv[:, 48 + k, jj_:jj_ + 1], [S2_b, modv_b])
                    for hp in range(22):
                        wg_, wg_b = wg[wi % 2]
                        wu_, wu_b = wu[wi % 2]
                        wi += 1
                        K.dma("sp", wg_[:], wtile(l, "wg", hp)[0], lastw["b"], [wg_b])
                        K.dma("sp", wu_[:], wtile(l, "wu", hp)[0], lastw["b"], [wu_b])
                        for jj in range(2):
                            hc = hp * 2 + jj
                            pg = nextps([1, 2])
                            pu = nextps([3, 4])
                            K.mm(ps[pg][:, :TT], psb[pg], [(wg_[:, k, jj * 128:(jj + 1) * 128], h2[:, k, :TT])
                                                           for k in range(KC)], [wg_b, h2_b])
                            K.mm(ps[pu][:, :TT], psb[pu], [(wu_[:, k, jj * 128:(jj + 1) * 128], h2[:, k, :TT])
                                                           for k in range(KC)], [wu_b, h2_b])
                            sg_, sg_b = sgl[si % 2]
                            si += 1
                            K.op("act", lambda e, pg=pg, sg_=sg_: e.activation(out=sg_[:, :TT], in_=ps[pg][:, :TT],
                                                                               func=AF.Silu), [psb[pg]], [sg_b])
                            K.op("dve", lambda e, pu=pu, hc=hc, sg_=sg_: e.tensor_tensor(
                                out=hid[:, hc, :TT], in0=ps[pu][:, :TT], in1=sg_[:, :TT], op=ALU.mult),
                                [psb[pu], sg_b], [hid_b])
                    for j in range(16):
                        wd_, wd_b = wd[j % 2]
                        xo_, xo_b = xo[j % 2]
                        K.dma("sp", wd_[:], wtile(l, "wd", j)[0], lastw["b"], [wd_b])
                        p = nextps([5, 6, 7])
                        K.mm(ps[p][:, :TT], psb[p], [(wd_[:, k, :], hid[:, k, :TT]) for k in range(HC)], [wd_b, hid_b])
                        K.op("dve", lambda e, p=p, j=j, xo_=xo_: e.scalar_tensor_tensor(
                            out=xo_[:, :TT], in0=ps[p][:, :TT], scalar=modv[:, 80 + j, jj_:jj_ + 1], in1=xt[:, j, :TT],
                            op0=ALU.mult, op1=ALU.add), [psb[p], xt_b, modv_b], [xo_b])
                        K.dma("act", XB[j * 128:(j + 1) * 128, t0:t0 + TT], xo_[:, :TT], [xo_b], [dbufs["XB"][ti]])
                K.barrier()

        def final_norm():
            with ExitStack() as st:
                xt, xt_b = TL(st, "nxt", [128, 16, 512], F32)
                sq, sq_b = TL(st, "nsq", [128, 16, 512], BF16)
                yo, yo_b = TL(st, "nyo", [128, 16, 512], F32)
                rstd, rstd_b = TL(st, "nrstd", [128, 512], F32)
                tmps = [TL(st, f"ntmp{i}", [128, 512], F32) for i in range(2)]
                lastl = DEPTH - 1
                for ti, (t0, TT, isctx) in enumerate(TILES):
                    if isctx:
                        continue
                    front(XB[:, t0:t0 + TT], dbufs["XB"][ti], TT, xt, xt_b, sq, sq_b, rstd, rstd_b, tmps,
                          lambda k: (yo[:, k, :TT], yo_b), lambda k: V(lastl, "gfin", k), lambda k: zb[:, 0:1],
                          [vec_b[lastl], zb_b])
                    K.dma("act", yT[:, t0:t0 + TT].rearrange("(k p) t -> p k t", p=128), yo[:, :, :TT], [yo_b],
                          [dbufs["Y"][ti]])
                K.barrier()

        for l in range(DEPTH):
            last = l == DEPTH - 1
            stage0(l)
            if l == 0:
                convert_weights(0)
            stage1(l, last)
            stage2_dft(l, last)
            stage3_conv(l, last)
            stage4_attn(l, last)
            stage5a(l, last)
            stage5b(l, last)
        final_norm()
        K.barrier()
    return nc


_NC_CACHE = {}


def kernel(**inp):
    inp = {k: np.asarray(v) for k, v in inp.items()}
    cst = _constants()
    wpack = np.concatenate([_pack_weights(inp, l) for l in range(DEPTH)]).reshape(-1, 2048)
    vec = np.stack([_pack_vec(inp, l) for l in range(DEPTH)])
    if "nc" not in _NC_CACHE:
        _NC_CACHE["nc"] = build()
    nc = _NC_CACHE["nc"]
    in_maps = []
    for b in range(NCORE):
        cv = np.stack([_pm(inp["c"][b]), _pm(inp["c_ctx"])], axis=-1).reshape(128, 32)
        in_maps.append({
            "xT": np.ascontiguousarray(inp["x"][b].T),
            "ctxT": np.ascontiguousarray(inp["ctx"][b].T),
            "cvec": np.ascontiguousarray(cv.astype(np.float32)),
            "wpack": wpack, "vec": vec, "cst": cst["cst"], "rope": cst["rope"],
            "tabl": cst["tabl"].reshape(8, 128, -1), "tabc": cst["tabc"].reshape(128, -1),
        })
    res = run_bass_kernel_spmd(nc, in_maps, core_ids=list(range(NCORE)))
    out = np.stack([np.ascontiguousarray(res.results[b]["yT"].T) for b in range(NCORE)])
    return out.astype(np.float32)
```

```python
import numpy as np
import ml_dtypes
from contextlib import ExitStack
import concourse.bass as bass
import concourse.mybir as mybir
from concourse.bass_utils import run_bass_kernel_spmd

F32 = mybir.dt.float32
BF16 = mybir.dt.bfloat16
AF = mybir.ActivationFunctionType
ALU = mybir.AluOpType

D = 2048
KC = 16
NLAT = 4096
NCTX = 256
T = NLAT + NCTX
DEPTH = 2
NCORE = 8
NOWN = 4
EPS = 1e-6
ATTN_SCALE = 192 ** -0.5
FFN = 5632
HC = FFN // 128
VT_LAT0 = 16
VT_CTX0 = 16 + NLAT + 32
VTW = NLAT + 32 + NCTX + 32
TILES = [(i * 512, 512, False) for i in range(8)] + [(NLAT, 256, True)]

S1_TILES = [("F", 0, 512), ("F", 512, 512), ("G", 0, 512), ("A", 0, 512), ("G", 512, 512), ("A", 512, 512),
            ("Q", 0, 512), ("Q", 512, 256), ("KV", 0, 512), ("KPE", 0, 128)]


def _tm(W, mw):
    K, M = W.shape
    return np.ascontiguousarray(W.reshape(K // 128, 128, M // mw, mw).transpose(2, 1, 0, 3))


def _weight_layout():
    specs = []
    for i, (nm, c0, w) in enumerate(S1_TILES):
        specs.append((f"s1_{i}", 1, 16, w))
    specs += [("wuq", 16, 6, 256), ("wuk", 16, 4, 128), ("wuv", 4, 4, 512),
              ("gate", 8, 16, 768), ("wf", 8, 8, 256), ("wc", 8, 8, 256), ("wo", 8, 16, 256),
              ("wout", 8, 16, 256), ("wg", 22, 16, 256), ("wu", 22, 16, 256), ("wd", 16, 44, 128),
              ("ada", 24, 16, 512)]
    lay = {}
    off = 0
    for nm, nt, kc, mw in specs:
        lay[nm] = (off, nt, kc, mw)
        off += nt * 128 * kc * mw
    tot = (off + 2047) // 2048 * 2048
    return lay, tot


WLAY, NW = _weight_layout()

VEC_COLS = {}
_o = 0
for _nm, _n in [("ada_b", 96), ("gmix", 16), ("bgate", 48), ("conv_b", 8), ("ln_g", 8), ("ln_b", 8),
                ("qg", 6), ("kvg", 4), ("gffn", 16), ("convw", 248), ("gfin", 16)]:
    VEC_COLS[_nm] = _o
    _o += _n
NV = _o


def _pm(v):
    return np.ascontiguousarray(v.reshape(-1, 128).T)


def _pack_weights(inp, l):
    buf = np.zeros(NW, np.float32)

    def put(nm, arr):
        off, nt, kc, mw = WLAY[nm]
        assert arr.shape == (nt, 128, kc, mw), (nm, arr.shape)
        buf[off:off + arr.size] = arr.reshape(-1)

    put("ada", _tm(inp["ada_w"][l], 512))
    w_in = inp["w_in"][l]
    kpe = w_in[:, 4352:4416]
    cols = {"F": w_in[:, 0:1024], "A": w_in[:, 1024:2048], "G": w_in[:, 2048:3072], "Q": w_in[:, 3072:3840],
            "KV": w_in[:, 3840:4352],
            "KPE": np.concatenate([kpe, kpe[:, 32:64], kpe[:, 0:32]], axis=1)}
    for i, (nm, c0, w) in enumerate(S1_TILES):
        put(f"s1_{i}", _tm(cols[nm][:, c0:c0 + w], w))
    wuq = inp["w_uq"][l].reshape(768, 16, 192)
    wuq = np.concatenate([wuq, wuq[:, :, 160:192], wuq[:, :, 128:160]], axis=2).reshape(768, 16 * 256)
    put("wuq", _tm(wuq, 256))
    wukv = inp["w_ukv"][l].reshape(512, 16, 256)
    put("wuk", _tm(np.ascontiguousarray(wukv[:, :, 0:128]).reshape(512, 2048), 128))
    put("wuv", _tm(np.ascontiguousarray(wukv[:, :, 128:256]).reshape(512, 2048), 512))
    g = w_in[:, 4416:].reshape(2048, 3, 16, 128).transpose(0, 2, 1, 3).reshape(2048, 6144)
    put("gate", _tm(np.ascontiguousarray(g), 768))
    put("wf", _tm(inp["w_fourier"][l], 256))
    put("wc", _tm(inp["w_conv_out"][l], 256))
    put("wo", _tm(inp["w_mla_o"][l], 256))
    put("wout", _tm(inp["w_out"][l], 256))
    put("wg", _tm(inp["w_ffn_gate"][l], 256))
    put("wu", _tm(inp["w_ffn_up"][l], 256))
    put("wd", _tm(inp["w_ffn_down"][l], 128))
    return buf


def _pack_vec(inp, l):
    v = np.zeros((128, NV), np.float32)

    def put(nm, arr):
        v[:, VEC_COLS[nm]:VEC_COLS[nm] + arr.shape[1]] = arr

    put("ada_b", _pm(inp["ada_b"][l]))
    put("gmix", _pm(inp["norm_mix_g"][l]))
    put("bgate", _pm(inp["b_gate"][l]))
    put("conv_b", _pm(inp["conv_b"][l]))
    put("ln_g", _pm(inp["conv_ln_g"][l]))
    put("ln_b", _pm(inp["conv_ln_b"][l]))
    put("qg", _pm(inp["q_norm_g"][l]))
    put("kvg", _pm(inp["kv_norm_g"][l]))
    put("gffn", _pm(inp["norm_ffn_g"][l]))
    cw = inp["conv_w"][l]
    put("convw", np.ascontiguousarray(cw.reshape(31, 8, 128).transpose(2, 1, 0)).reshape(128, 248))
    put("gfin", _pm(inp["final_norm_g"]))
    return v


_CONST_CACHE = {}


def _constants(half):
    if half in _CONST_CACHE:
        return _CONST_CACHE[half]
    bf = ml_dtypes.bfloat16
    cst = np.zeros((128, 386), np.float32)
    cst[:, 0:128] = np.eye(128, dtype=np.float32)
    cc = np.arange(128)
    ang = 2 * np.pi * ((cc[:, None] * cc[None, :]) % 128) / 128.0
    cst[:, 128:256] = np.cos(ang)
    cst[:, 256:384] = np.sin(ang)
    gidx = (np.arange(NLAT) + half * (NLAT // 2)) % NLAT
    n_freq = 16
    inv_freq = (10000.0 ** (-np.arange(n_freq, dtype=np.float32) / n_freq)).astype(np.float32)
    row = (gidx // 64).astype(np.float32)
    col = (gidx % 64).astype(np.float32)
    angr = np.concatenate([row[:, None] * inv_freq, col[:, None] * inv_freq], axis=-1).astype(np.float32)
    cos = np.ones((T, 32), np.float32)
    sin = np.zeros((T, 32), np.float32)
    cos[:NLAT] = np.cos(angr)
    sin[:NLAT] = np.sin(angr)
    rope = np.zeros((64, 2, T), np.float32)
    rope[0:32, 0] = cos.T
    rope[32:64, 0] = cos.T
    rope[0:32, 1] = -sin.T
    rope[32:64, 1] = sin.T
    tabl = np.zeros((8, 128, 32, 2, 512), bf)
    for kt in range(8):
        k = gidx[kt * 512 + np.arange(512)]
        a_ = 2 * np.pi * ((gidx[:, None] * k[None, :]) % NLAT) / float(NLAT)
        tabl[kt, :, :, 0, :] = np.cos(a_).reshape(32, 128, 512).transpose(1, 0, 2).astype(bf)
        tabl[kt, :, :, 1, :] = (-np.sin(a_)).reshape(32, 128, 512).transpose(1, 0, 2).astype(bf)
    n2 = np.arange(NCTX)
    a_ = 2 * np.pi * ((n2[:, None] * n2[None, :]) % NCTX) / float(NCTX)
    tabc = np.zeros((128, 2, 2, 256), bf)
    tabc[:, :, 0, :] = np.cos(a_).reshape(2, 128, 256).transpose(1, 0, 2).astype(bf)
    tabc[:, :, 1, :] = (-np.sin(a_)).reshape(2, 128, 256).transpose(1, 0, 2).astype(bf)
    cst[:, 384] = 1.0 if half == 1 else 0.0
    cst[:, 385] = 1.0 if half == 0 else 0.0
    _CONST_CACHE[half] = dict(cst=cst, rope=rope, tabl=tabl, tabc=tabc)
    return _CONST_CACHE[half]


class Ev:
    __slots__ = ("sem", "val", "key")

    def __init__(self, sem, val, key):
        self.sem, self.val, self.key = sem, val, key


class Buf:
    __slots__ = ("w", "r")

    def __init__(self):
        self.w = None
        self.r = {}


class Eng:
    def __init__(self, kern, name, eng, selfsync):
        self.k, self.name, self.e, self.selfsync = kern, name, eng, selfsync
        self.waited = {}
        self.sem = None
        self.cnt = 0
        self.key = None
        self.last = None
        self.nsem = 0

    def _newsem(self):
        self.sem = self.k.es.enter_context(self.k.nc.semaphore(f"s_{self.name}_{self.nsem}"))
        self.key = (self.name, self.nsem)
        self.nsem += 1
        self.cnt = 0

    def wait(self, ev):
        if ev is None:
            return
        if ev.key == self.key and not self.selfsync:
            return
        if self.waited.get(ev.key, 0) >= ev.val:
            return
        self.e.wait_ge(ev.sem, ev.val)
        self.waited[ev.key] = ev.val

    def bump(self, inst):
        if self.sem is None or self.cnt >= 30000:
            self._newsem()
        self.cnt += 1
        inst.then_inc(self.sem, 1)
        self.last = Ev(self.sem, self.cnt, self.key)
        return self.last


class Kern:
    NDMA = 40

    def __init__(self, nc, es):
        self.nc, self.es = nc, es
        self.E = {"pe": Eng(self, "pe", nc.tensor, False), "act": Eng(self, "act", nc.scalar, True),
                  "dve": Eng(self, "dve", nc.vector, True), "pool": Eng(self, "pool", nc.gpsimd, True),
                  "sp": Eng(self, "sp", nc.sync, False)}
        self.dsem = []
        self.dpool = {}
        self.dcnt = {"sp": 0, "act": 0, "pool": 0}
        i = 0
        for q, n in (("sp", 24), ("act", 12), ("pool", 8)):
            self.dpool[q] = list(range(i, i + n))
            for _ in range(n):
                self.dsem.append([es.enter_context(nc.semaphore(f"s_dma_{i}")), 0, None])
                i += 1

    def _sync(self, E, reads, writes):
        for b in reads:
            E.wait(b.w)
        for b in writes:
            E.wait(b.w)
            for r in b.r.values():
                E.wait(r)

    def _rec(self, ev, reads, writes):
        for b in reads:
            b.r[ev.key] = ev
        for b in writes:
            b.w = ev
            b.r = {}

    def op(self, eng, fn, reads=(), writes=()):
        E = self.E[eng]
        self._sync(E, reads, writes)
        ev = E.bump(fn(E.e))
        self._rec(ev, reads, writes)
        return ev

    def mm(self, out, pb, pairs, reads, start=True, stop=True):
        E = self.E["pe"]
        self._sync(E, reads, [pb])
        n = len(pairs)
        inst = None
        for i, (l, r) in enumerate(pairs):
            inst = self.nc.tensor.matmul(out, lhsT=l, rhs=r, start=(start and i == 0), stop=(stop and i == n - 1))
        ev = E.bump(inst)
        self._rec(ev, reads, [pb])
        return ev

    def dma(self, q, out, in_, reads=(), writes=(), **kw):
        E = self.E[q]
        self._sync(E, reads, writes)
        pool = self.dpool[q]
        idx = pool[self.dcnt[q] % len(pool)]
        self.dcnt[q] += 1
        slot = self.dsem[idx]
        E.wait(slot[2])
        inst = E.e.dma_start(out=out, in_=in_, **kw)
        slot[1] += 16
        inst.then_inc(slot[0], 16)
        ev = Ev(slot[0], slot[1], ("dma", idx))
        slot[2] = ev
        self._rec(ev, reads, writes)
        return ev

    def barrier(self):
        evs = [E.last for E in self.E.values() if E.last is not None] + [s[2] for s in self.dsem if s[2] is not None]
        for E in self.E.values():
            for ev in evs:
                E.wait(ev)


def build():
    nc = bass.Bass("TRN2", target_bir_lowering=False)
    xT_in = nc.dram_tensor("xT", [D, NLAT], F32, kind="ExternalInput").ap()
    ctxT_in = nc.dram_tensor("ctxT", [D, NCTX], F32, kind="ExternalInput").ap()
    cvec_in = nc.dram_tensor("cvec", [128, 32], F32, kind="ExternalInput").ap()
    wpack = nc.dram_tensor("wpack", [DEPTH * NW // 2048, 2048], F32, kind="ExternalInput").ap()
    vec_in = nc.dram_tensor("vec", [DEPTH, 128, NV], F32, kind="ExternalInput").ap()
    cst_in = nc.dram_tensor("cst", [128, 386], F32, kind="ExternalInput").ap()
    rope_in = nc.dram_tensor("rope", [64, 2, T], F32, kind="ExternalInput").ap()
    tabl_in = nc.dram_tensor("tabl", [8, 128, 32 * 2 * 512], BF16, kind="ExternalInput").ap()
    tabc_in = nc.dram_tensor("tabc", [128, 2 * 2 * 256], BF16, kind="ExternalInput").ap()
    yT = nc.dram_tensor("yT", [D, NLAT // 2], F32, kind="ExternalOutput").ap()

    wbf2 = [nc.dram_tensor(f"wbf{l}", [NW // 2048, 2048], BF16).ap() for l in range(DEPTH)]
    wbf = [w.rearrange("a b -> (a b)") for w in wbf2]
    XA = nc.dram_tensor("XA", [D, T], F32).ap()
    XB = nc.dram_tensor("XB", [D, T], F32).ap()
    HXT = nc.dram_tensor("HXT", [D, T], BF16).ap()
    AB = nc.dram_tensor("AB", [T, 2048], BF16).ap()
    VT = nc.dram_tensor("VT", [1024, VTW], BF16).ap()
    YFT = nc.dram_tensor("YFT", [1024, T], BF16).ap()
    CST = nc.dram_tensor("CST", [1024, T], BF16).ap()
    QNT = nc.dram_tensor("QNT", [768, T], BF16).ap()
    KVNT = nc.dram_tensor("KVNT", [512, T], BF16).ap()
    KPET = nc.dram_tensor("KPET", [64, T], BF16).ap()
    OT = nc.dram_tensor("OT", [D, T], BF16).ap()
    SEAM = nc.dram_tensor("SEAM", [4, 1024, 16], BF16).ap()

    NT = len(TILES)
    dbufs = {nm: [Buf() for _ in range(NT)] for nm in
             ["XA", "XB", "HXT", "AB", "VT", "YFT", "CST", "QNT", "KVNT", "KPET", "OT", "Y"]}
    seam_b = [Buf() for _ in range(4)]
    wb = [Buf() for _ in range(DEPTH)]
    nobuf = Buf()

    with ExitStack() as es:
        K = Kern(nc, es)

        uniq = {"n": 0}

        def TL(st, name, shape, dt):
            uniq["n"] += 1
            return st.enter_context(nc.sbuf_tensor(f"{name}_{uniq['n']}", list(shape), dt)), Buf()

        ps = [es.enter_context(nc.psum_tensor(f"ps{i}", [128, 512], F32)) for i in range(8)]
        psb = [Buf() for _ in range(8)]
        rot = {"i": 0}

        def nextps(cands):
            rot["i"] += 1
            return cands[rot["i"] % len(cands)]

        cstf, cstf_b = TL(es, "cstf", [128, 386], F32)
        csm, csm_b = TL(es, "csm", [128, 256], BF16)
        ones, ones_b = TL(es, "ones", [128, 128], BF16)
        zer, zer_b = TL(es, "zer", [128, 8, 32], BF16)
        vecs, vec_b = [], []
        for l in range(DEPTH):
            t_, b_ = TL(es, f"vec{l}", [128, NV], F32)
            vecs.append(t_)
            vec_b.append(b_)
        modv, modv_b = TL(es, "modv", [128, 96, 2], F32)
        S1, S1_b = TL(es, "S1", [128, 16, 2], F32)
        S2, S2_b = TL(es, "S2", [128, 16, 2], F32)
        zb, zb_b = TL(es, "zb", [128, 1], F32)

        K.dma("sp", cstf[:], cst_in, [nobuf], [cstf_b])
        for l in range(DEPTH):
            K.dma("sp", vecs[l][:], vec_in[l], [nobuf], [vec_b[l]])
        K.op("act", lambda e: e.activation(out=csm[:], in_=cstf[:, 128:384], func=AF.Identity), [cstf_b], [csm_b])
        K.op("dve", lambda e: e.memset(ones[:], 1.0), [], [ones_b])
        K.op("dve", lambda e: e.memset(zer[:], 0.0), [], [zer_b])
        K.op("dve", lambda e: e.memset(zb[:], 0.0), [], [zb_b])
        epsb, epsb_b = TL(es, "epsb", [128, 1], F32)
        K.op("dve", lambda e: e.memset(epsb[:], EPS), [], [epsb_b])
        vtpad_b = Buf()
        for a0 in (0, VT_LAT0 + NLAT, VT_CTX0 + NCTX):
            w_ = 16 if a0 != VT_LAT0 + NLAT else 32
            K.dma("sp", VT[:, a0:a0 + w_].rearrange("(c p) t -> p c t", p=128), zer[:, :, 0:w_], [zer_b], [vtpad_b])

        cvb = [[], []]

        def convert_weights(l, trig=None):
            rows = NW // 2048
            crow = WLAY["ada"][0] // 2048
            assert WLAY["ada"][0] % 2048 == 0
            r0 = 0
            while r0 < crow:
                n = min(1024, crow - r0)
                b_ = Buf()
                K.dma("pool", wbf2[l][r0:r0 + n, :], wpack[l * rows + r0:l * rows + r0 + n, :],
                      [nobuf] if trig is None else [trig], [b_])
                cvb[l].append(b_)
                r0 += n

        wpack_flat = wpack.rearrange("a b -> (a b)")
        lastw = {"b": []}

        def wtile(l, name, ti):
            off, nt, kc, mw = WLAY[name]
            o = off + ti * 128 * kc * mw
            b0 = (o // 2048) // 1024
            b1 = ((o + 128 * kc * mw - 1) // 2048) // 1024
            lastw["b"] = cvb[l][b0:b1 + 1]
            return wbf[l][o:o + 128 * kc * mw].rearrange("(p k m) -> p k m", p=128, k=kc), kc, mw

        def V(l, nm, i=0, n=1):
            c = VEC_COLS[nm] + i
            return vecs[l][:, c:c + n]

        def xsrc(l, phase):
            pass

        def stage0(l):
            with ExitStack() as st:
                cv, cv_b = TL(st, "cv", [128, 32], F32)
                scb, scb_b = TL(st, "scb", [128, 16, 2], BF16)
                wsl = [TL(st, f"adaw{i}", [128, 16, 512], BF16) for i in range(2)]
                K.dma("sp", cv[:], cvec_in, [nobuf], [cv_b])
                K.op("act", lambda e: e.activation(out=scb[:].rearrange("p k j -> p (k j)"), in_=cv[:], func=AF.Silu),
                     [cv_b], [scb_b])
                for ti in range(24):
                    wt, wt_b = wsl[ti % 2]
                    off_, _, kc_, mw_ = WLAY["ada"]
                    o_ = l * NW + off_ + ti * 128 * kc_ * mw_
                    K.dma("pool", wt[:], wpack_flat[o_:o_ + 128 * kc_ * mw_].rearrange("(p k m) -> p k m", p=128, k=kc_),
                          [nobuf], [wt_b], max_dma_last_dim=4096)
                    for mi in range(4):
                        m = ti * 4 + mi
                        K.mm(ps[0][:, 2 * m:2 * m + 2], psb[0],
                             [(wt[:, k, mi * 128:(mi + 1) * 128], scb[:, k, :]) for k in range(KC)], [wt_b, scb_b])
                vl = vecs[l]
                for j in range(2):
                    K.op("dve", lambda e, j=j: e.tensor_tensor(
                        out=modv[:, :, j], in0=ps[0][:, 0:192].rearrange("p (m j) -> p m j", j=2)[:, :, j],
                        in1=vl[:, VEC_COLS["ada_b"]:VEC_COLS["ada_b"] + 96], op=ALU.add),
                        [psb[0], vec_b[l]], [modv_b])
                for j in range(2):
                    K.op("dve", lambda e, j=j: e.scalar_tensor_tensor(
                        out=S1[:, :, j], in0=modv[:, 16:32, j], scalar=1.0,
                        in1=vl[:, VEC_COLS["gmix"]:VEC_COLS["gmix"] + 16], op0=ALU.add, op1=ALU.mult),
                        [modv_b, vec_b[l]], [S1_b])
                    K.op("dve", lambda e, j=j: e.scalar_tensor_tensor(
                        out=S2[:, :, j], in0=modv[:, 64:80, j], scalar=1.0,
                        in1=vl[:, VEC_COLS["gffn"]:VEC_COLS["gffn"] + 16], op0=ALU.add, op1=ALU.mult),
                        [modv_b, vec_b[l]], [S2_b])
                K.barrier()

        def front(src_ap, src_b, TT, xt, xt_b, sq, sq_b, rstd, rstd_b, tmps, out_fn, scale_fn, bias_fn, extra_reads,
                  nfeat=D):
            K.dma("sp", xt[:, :, :TT], src_ap.rearrange("(k p) t -> p k t", p=128), [src_b], [xt_b])
            K.op("act", lambda e: e.activation(out=sq[:, :, :TT], in_=xt[:, :, :TT], func=AF.Square), [xt_b], [sq_b])
            K.mm(ps[0][:, :TT], psb[0], [(ones[:], sq[:, k, :TT]) for k in range(KC)], [ones_b, sq_b])
            K.op("act", lambda e: e.activation(out=rstd[:, :TT], in_=ps[0][:, :TT], func=AF.Sqrt, scale=1.0 / nfeat,
                                               bias=epsb[:, 0:1]), [psb[0], epsb_b], [rstd_b])
            K.op("dve", lambda e: e.reciprocal(out=rstd[:, :TT], in_=rstd[:, :TT]), [rstd_b], [rstd_b])
            for k in range(KC):
                tm_, tm_b = tmps[k % 2]
                K.op("dve", lambda e, k=k, tm_=tm_: e.tensor_tensor(out=tm_[:, :TT], in0=xt[:, k, :TT], in1=rstd[:, :TT],
                                                                    op=ALU.mult), [xt_b, rstd_b], [tm_b])
                o_ap, o_b = out_fn(k)
                K.op("act", lambda e, k=k, tm_=tm_, o_ap=o_ap: e.activation(
                    out=o_ap, in_=tm_[:, :TT], func=AF.Identity, scale=scale_fn(k), bias=bias_fn(k)),
                    [tm_b] + extra_reads, [o_b])

        def rms_feat(nchunks, nfeat, TT, qd, qd_b, sqq, sqq_b, rstd, rstd_b, outt, outt_b, gcol, l):
            K.mm(ps[0][:, :TT], psb[0], [(ones[:], sqq[:, c, :TT]) for c in range(nchunks)], [ones_b, sqq_b])
            K.op("act", lambda e: e.activation(out=rstd[:, :TT], in_=ps[0][:, :TT], func=AF.Sqrt, scale=1.0 / nfeat,
                                               bias=epsb[:, 0:1]), [psb[0], epsb_b], [rstd_b])
            K.op("dve", lambda e: e.reciprocal(out=rstd[:, :TT], in_=rstd[:, :TT]), [rstd_b], [rstd_b])
            for c in range(nchunks):
                K.op("dve", lambda e, c=c: e.scalar_tensor_tensor(
                    out=outt[:, c, :TT], in0=qd[:, c, :TT], scalar=V(l, gcol, c), in1=rstd[:, :TT],
                    op0=ALU.mult, op1=ALU.mult), [qd_b, rstd_b, vec_b[l]], [outt_b])

        def stage1(l, last):
            with ExitStack() as st:
                xt, xt_b = TL(st, "xt", [128, 16, 512], F32)
                sq, sq_b = TL(st, "sq", [128, 16, 512], BF16)
                hx, hx_b = TL(st, "hx", [128, 16, 512], BF16)
                rstd, rstd_b = TL(st, "rstd", [128, 512], F32)
                tmps = [TL(st, f"tmp{i}", [128, 512], F32) for i in range(2)]
                wsl = [TL(st, f"w1_{i}", [128, 16, 512], BF16) for i in range(2)]
                ut = [TL(st, f"ut{i}", [128, 512], BF16) for i in range(2)]
                abt, abt_b = TL(st, "abt", [128, 4, 8, 256], BF16)
                sg, sg_b = TL(st, "sg", [128, 4, 512], BF16)
                vt, vt_b = TL(st, "vt", [128, 8, 512], BF16)
                sm, sm_b = TL(st, "sm", [128, 8, 16], BF16)
                qd, qd_b = TL(st, "qd", [128, 6, 512], F32)
                sqq, sqq_b = TL(st, "sqq", [128, 6, 512], BF16)
                qn, qn_b = TL(st, "qn", [128, 6, 512], BF16)
                kvn, kvn_b = TL(st, "kvn", [128, 4, 512], BF16)
                rp, rp_b = TL(st, "rp", [64, 2, 512], F32)
                r1, r1_b = TL(st, "r1", [64, 512], F32)
                r2, r2_b = TL(st, "r2", [64, 512], F32)
                kr, kr_b = TL(st, "kr", [64, 512], BF16)
                wi = 0
                for ti, (t0, TT, isctx) in enumerate(TILES):
                    j = 1 if isctx else 0
                    if l == 0:
                        src = ctxT_in[:, :] if isctx else xT_in[:, t0:t0 + TT]
                        src_b = nobuf
                    else:
                        src, src_b = XB[:, t0:t0 + TT], dbufs["XB"][ti]
                    front(src, src_b, TT, xt, xt_b, sq, sq_b, rstd, rstd_b, tmps,
                          lambda k: (hx[:, k, :TT], hx_b), lambda k: S1[:, k, j:j + 1], lambda k: modv[:, k, j:j + 1],
                          [S1_b, modv_b])
                    only_kv = last and isctx
                    if last and not isctx:
                        allowed = None if ti < NOWN else (("F", "G", "A", "KV", "KPE") if ti in (NOWN, 7) else
                                                          ("F", "KV", "KPE"))
                    else:
                        allowed = None
                    if not only_kv and not (last and ti >= NOWN):
                        K.dma("act", HXT[:, t0:t0 + TT].rearrange("(k p) t -> p k t", p=128), hx[:, :, :TT], [hx_b],
                              [dbufs["HXT"][ti]])
                    K.dma("sp", rp[:, :, :TT], rope_in[:, :, t0:t0 + TT], [nobuf], [rp_b])
                    for si, (nm, c0, w) in enumerate(S1_TILES):
                        if only_kv and nm not in ("KV", "KPE"):
                            continue
                        if allowed is not None and nm not in allowed:
                            continue
                        wt, wt_b = wsl[wi % 2]
                        wi += 1
                        src_w, _, _ = wtile(l, f"s1_{si}", 0)
                        K.dma("sp", wt[:, :, :w], src_w, lastw["b"], [wt_b])
                        if nm == "KPE":
                            pa, pb_ = 5, 6
                            K.mm(ps[pa][0:64, :TT], psb[pa], [(wt[:, k, 0:64], hx[:, k, :TT]) for k in range(KC)],
                                 [wt_b, hx_b])
                            K.mm(ps[pb_][0:64, :TT], psb[pb_], [(wt[:, k, 64:128], hx[:, k, :TT]) for k in range(KC)],
                                 [wt_b, hx_b])
                            K.op("dve", lambda e: e.tensor_tensor(out=r1[:, :TT], in0=ps[pa][0:64, :TT], in1=rp[:, 0, :TT],
                                                                  op=ALU.mult), [psb[pa], rp_b], [r1_b])
                            K.op("dve", lambda e: e.tensor_tensor(out=r2[:, :TT], in0=ps[pb_][0:64, :TT],
                                                                  in1=rp[:, 1, :TT], op=ALU.mult), [psb[pb_], rp_b], [r2_b])
                            K.op("dve", lambda e: e.tensor_tensor(out=kr[:, :TT], in0=r1[:, :TT], in1=r2[:, :TT],
                                                                   op=ALU.add), [r1_b, r2_b], [kr_b])
                            K.dma("act", KPET[:, t0:t0 + TT], kr[:, :TT], [kr_b], [dbufs["KPET"][ti]])
                            continue
                        for mi in range(w // 128):
                            c = c0 // 128 + mi
                            p = nextps([1, 2, 3, 4])
                            K.mm(ps[p][:, :TT], psb[p], [(wt[:, k, mi * 128:(mi + 1) * 128], hx[:, k, :TT])
                                                         for k in range(KC)], [wt_b, hx_b])
                            if nm == "F":
                                u_, u_b = ut[c % 2]
                                K.op("act", lambda e, u_=u_, p=p: e.activation(out=u_[:, :TT], in_=ps[p][:, :TT],
                                                                               func=AF.Identity), [psb[p]], [u_b])
                                for s in range(TT // 128):
                                    p2 = nextps([6, 7])
                                    K.mm(ps[p2][:, 0:256], psb[p2], [(u_[:, s * 128:(s + 1) * 128], csm[:])],
                                         [u_b, csm_b])
                                    K.op("dve", lambda e, s=s, c=c, p2=p2: e.tensor_copy(out=abt[:, s, c, :],
                                                                                         in_=ps[p2][:, 0:256]),
                                         [psb[p2]], [abt_b])
                            elif nm == "G":
                                K.op("act", lambda e, mi=mi, p=p: e.activation(out=sg[:, mi, :TT], in_=ps[p][:, :TT],
                                                                               func=AF.Sigmoid), [psb[p]], [sg_b])
                            elif nm == "A":
                                K.op("dve", lambda e, mi=mi, c=c, p=p: e.tensor_tensor(
                                    out=vt[:, c, :TT], in0=ps[p][:, :TT], in1=sg[:, mi, :TT], op=ALU.mult),
                                    [psb[p], sg_b], [vt_b])
                            elif nm == "Q":
                                K.op("act", lambda e, c=c, p=p: e.activation(out=qd[:, c, :TT], in_=ps[p][:, :TT],
                                                                             func=AF.Identity), [psb[p]], [qd_b])
                                K.op("act", lambda e, c=c, p=p: e.activation(out=sqq[:, c, :TT], in_=ps[p][:, :TT],
                                                                             func=AF.Square), [psb[p]], [sqq_b])
                            elif nm == "KV":
                                K.op("act", lambda e, c=c, p=p: e.activation(out=qd[:, c, :TT], in_=ps[p][:, :TT],
                                                                             func=AF.Identity), [psb[p]], [qd_b])
                                K.op("act", lambda e, c=c, p=p: e.activation(out=sqq[:, c, :TT], in_=ps[p][:, :TT],
                                                                             func=AF.Square), [psb[p]], [sqq_b])
                        if nm == "F" and c0 == 512:
                            K.dma("act", AB[t0:t0 + TT, :].rearrange("(s p) c -> p s c", p=128),
                                  abt[:, 0:TT // 128, :, :].rearrange("p s g x -> p s (g x)"), [abt_b], [dbufs["AB"][ti]])
                        if nm == "A" and c0 == 512:
                            v0 = (VT_CTX0 if isctx else VT_LAT0 + t0)
                            K.dma("act", VT[:, v0:v0 + TT].rearrange("(c p) t -> p c t", p=128), vt[:, :, :TT], [vt_b],
                                  [dbufs["VT"][ti]])
                            if not isctx and ti in (0, 3, 4, 7):
                                sidx = {7: 0, 0: 1, 3: 2, 4: 3}[ti]
                                mcol = 384 if ti in (0, 7) else 385
                                e0 = 0 if ti in (0, 4) else TT - 16
                                K.op("dve", lambda e, e0=e0, mcol=mcol: e.tensor_scalar(
                                    out=sm[:, :, :], in0=vt[:, :, e0:e0 + 16], scalar1=cstf[:, mcol:mcol + 1], scalar2=None,
                                    op0=ALU.mult), [vt_b, cstf_b], [sm_b])
                                K.dma("act", SEAM[sidx].rearrange("(c p) x -> p c x", p=128), sm[:, :, :], [sm_b],
                                      [seam_b[sidx]])
                        if nm == "Q" and c0 == 512:
                            rms_feat(6, 768, TT, qd, qd_b, sqq, sqq_b, rstd, rstd_b, qn, qn_b, "qg", l)
                            K.dma("act", QNT[:, t0:t0 + TT].rearrange("(c p) t -> p c t", p=128), qn[:, :, :TT], [qn_b],
                                  [dbufs["QNT"][ti]])
                        if nm == "KV":
                            rms_feat(4, 512, TT, qd, qd_b, sqq, sqq_b, rstd, rstd_b, kvn, kvn_b, "kvg", l)
                            K.dma("act", KVNT[:, t0:t0 + TT].rearrange("(c p) t -> p c t", p=128), kvn[:, :, :TT],
                                  [kvn_b], [dbufs["KVNT"][ti]])
                K.barrier()

        def stage2_dft(l, last):
            with ExitStack() as st:
                tb = st.enter_context(nc.sbuf_tensor(f"tb{l}", [128, 32, 2, 512], BF16))
                tb_b = [Buf() for _ in range(32)]
                abg = [TL(st, f"abg{i}", [128, 32, 256], BF16) for i in range(2)]
                yf = [TL(st, f"yfo{i}", [128, 512], BF16) for i in range(2)]
                gi = 0
                jobs = [(kt, False) for kt in range(NOWN if last else 8)] + ([] if last else [(0, True)])
                for kt, isctx in jobs:
                    if isctx:
                        ncn, KW, ti, n0 = 2, 256, 8, NLAT
                        K.dma("sp", tb[:, 0:2, :, 0:256], tabc_in.rearrange("p (c j k) -> p c j k", c=2, j=2),
                              [nobuf], tb_b[0:2])
                        scale = (NCTX * 128.0) ** -0.5
                    else:
                        ncn, KW, ti, n0 = 32, 512, kt, 0
                        tsrc = tabl_in[kt].rearrange("p (c j k) -> p c j k", c=32, j=2)
                        for c4 in range(8):
                            K.dma("sp", tb[:, c4 * 4:(c4 + 1) * 4, :, :], tsrc[:, c4 * 4:(c4 + 1) * 4, :, :], [nobuf],
                                  tb_b[c4 * 4:(c4 + 1) * 4])
                        scale = (NLAT * 128.0) ** -0.5
                    for g in range(8):
                        ab_, ab_b = abg[gi % 2]
                        yo, yo_b = yf[gi % 2]
                        gi += 1
                        rd = dbufs["AB"][8:9] if isctx else dbufs["AB"][0:8]
                        K.dma("sp", ab_[:, 0:ncn, :],
                              AB[n0:n0 + ncn * 128, g * 256:(g + 1) * 256].rearrange("(c p) x -> p c x", p=128), rd,
                              [ab_b])
                        p = nextps([1, 2, 3, 4])
                        pairs = []
                        for c in range(ncn):
                            pairs.append((ab_[:, c, 0:128], tb[:, c, 0, 0:KW]))
                            pairs.append((ab_[:, c, 128:256], tb[:, c, 1, 0:KW]))
                        K.mm(ps[p][:, :KW], psb[p], pairs, [ab_b] + tb_b[0:ncn])
                        K.op("act", lambda e, p=p, yo=yo, KW=KW, scale=scale: e.activation(
                            out=yo[:, :KW], in_=ps[p][:, :KW], func=AF.Identity, scale=scale), [psb[p]], [yo_b])
                        t0 = n0 + kt * 512
                        K.dma("act", YFT[g * 128:(g + 1) * 128, t0:t0 + KW], yo[:, :KW], [yo_b], [dbufs["YFT"][ti]])
                K.barrier()

        def stage3_conv(l, last):
            with ExitStack() as st:
                dg, dg_b = TL(st, "dg", [128, 8, 31, 128], BF16)
                vh = [TL(st, f"vh{i}", [128, 544], BF16) for i in range(3)]
                cvb, cvb_b = TL(st, "cvb", [128, 8, 512], BF16)
                sq, sq_b = TL(st, "csq", [128, 8, 512], BF16)
                mean, mean_b = TL(st, "mean", [128, 512], F32)
                m2, m2_b = TL(st, "m2", [128, 512], F32)
                rstd, rstd_b = TL(st, "crstd", [128, 512], F32)
                nmr, nmr_b = TL(st, "nmr", [128, 512], F32)
                tmps = [TL(st, f"ctmp{i}", [128, 512], F32) for i in range(2)]
                cs, cs_b = TL(st, "cs", [128, 8, 512], BF16)
                for c in range(8):
                    for k in range(31):
                        K.op("dve", lambda e, c=c, k=k: e.tensor_scalar(
                            out=dg[:, c, k, :], in0=cstf[:, 0:128], scalar1=V(l, "convw", c * 31 + k), scalar2=None,
                            op0=ALU.mult), [cstf_b, vec_b[l]], [dg_b])
                vi = 0
                for ti, (t0, TT, isctx) in enumerate(TILES):
                    if last and (isctx or ti >= NOWN):
                        continue
                    v0 = (VT_CTX0 if isctx else VT_LAT0 + t0) - 16
                    rdl = [dbufs["VT"][ti], vtpad_b]
                    if not isctx:
                        rdl.append(dbufs["VT"][(ti - 1) % 8])
                        rdl.append(dbufs["VT"][(ti + 1) % 8])
                    for c in range(8):
                        vh_, vh_b = vh[vi % 3]
                        vi += 1
                        crow = VT[c * 128:(c + 1) * 128, :]
                        if isctx or ti not in (0, 3, 4, 7):
                            K.dma("sp", vh_[:, 0:TT + 32], crow[:, v0:v0 + TT + 32], rdl, [vh_b])
                        elif ti in (0, 4):
                            sidx = 0 if ti == 0 else 2
                            K.dma("sp", vh_[:, 0:16], SEAM[sidx][c * 128:(c + 1) * 128, :], [seam_b[sidx]], [vh_b])
                            K.dma("sp", vh_[:, 16:TT + 32], crow[:, v0 + 16:v0 + TT + 32], rdl, [vh_b])
                        else:
                            sidx = 3 if ti == 3 else 1
                            K.dma("sp", vh_[:, 0:TT + 16], crow[:, v0:v0 + TT + 16], rdl, [vh_b])
                            K.dma("sp", vh_[:, TT + 16:TT + 32], SEAM[sidx][c * 128:(c + 1) * 128, :], [seam_b[sidx]],
                                  [vh_b])
                        p = nextps([1, 2, 3, 4])
                        K.mm(ps[p][:, :TT], psb[p], [(dg[:, c, k, :], vh_[:, k + 1:k + 1 + TT]) for k in range(31)],
                             [dg_b, vh_b])
                        K.op("act", lambda e, c=c, p=p: e.activation(out=cvb[:, c, :TT], in_=ps[p][:, :TT],
                                                                     func=AF.Identity, bias=V(l, "conv_b", c)),
                             [psb[p], vec_b[l]], [cvb_b])
                        K.op("act", lambda e, c=c, p=p: e.activation(out=sq[:, c, :TT], in_=ps[p][:, :TT],
                                                                     func=AF.Square, bias=V(l, "conv_b", c)),
                             [psb[p], vec_b[l]], [sq_b])
                    K.mm(ps[5][:, :TT], psb[5], [(ones[:], cvb[:, c, :TT]) for c in range(8)], [ones_b, cvb_b])
                    K.mm(ps[6][:, :TT], psb[6], [(ones[:], sq[:, c, :TT]) for c in range(8)], [ones_b, sq_b])
                    K.op("dve", lambda e: e.tensor_scalar(out=mean[:, :TT], in0=ps[5][:, :TT], scalar1=1.0 / 1024,
                                                          scalar2=None, op0=ALU.mult), [psb[5]], [mean_b])
                    K.op("dve", lambda e: e.tensor_tensor(out=m2[:, :TT], in0=mean[:, :TT], in1=mean[:, :TT],
                                                          op=ALU.mult), [mean_b], [m2_b])
                    K.op("dve", lambda e: e.scalar_tensor_tensor(out=rstd[:, :TT], in0=ps[6][:, :TT], scalar=1.0 / 1024,
                                                                 in1=m2[:, :TT], op0=ALU.mult, op1=ALU.subtract),
                         [psb[6], m2_b], [rstd_b])
                    K.op("act", lambda e: e.activation(out=rstd[:, :TT], in_=rstd[:, :TT], func=AF.Sqrt,
                                                       bias=epsb[:, 0:1]), [rstd_b, epsb_b], [rstd_b])
                    K.op("dve", lambda e: e.reciprocal(out=rstd[:, :TT], in_=rstd[:, :TT]), [rstd_b], [rstd_b])
                    K.op("dve", lambda e: e.scalar_tensor_tensor(out=nmr[:, :TT], in0=mean[:, :TT], scalar=-1.0,
                                                                 in1=rstd[:, :TT], op0=ALU.mult, op1=ALU.mult),
                         [mean_b, rstd_b], [nmr_b])
                    for c in range(8):
                        tm_, tm_b = tmps[c % 2]
                        K.op("dve", lambda e, c=c, tm_=tm_: e.tensor_tensor(out=tm_[:, :TT], in0=cvb[:, c, :TT],
                                                                            in1=rstd[:, :TT], op=ALU.mult),
                             [cvb_b, rstd_b], [tm_b])
                        K.op("pool", lambda e, tm_=tm_: e.tensor_tensor(out=tm_[:, :TT], in0=tm_[:, :TT],
                                                                        in1=nmr[:, :TT], op=ALU.add), [tm_b, nmr_b],
                             [tm_b])
                        K.op("act", lambda e, c=c, tm_=tm_: e.activation(
                            out=cs[:, c, :TT], in_=tm_[:, :TT], func=AF.Silu, scale=V(l, "ln_g", c),
                            bias=V(l, "ln_b", c)), [tm_b, vec_b[l]], [cs_b])
                    K.dma("act", CST[:, t0:t0 + TT].rearrange("(c p) t -> p c t", p=128), cs[:, :, :TT], [cs_b],
                          [dbufs["CST"][ti]])
                K.barrier()

        def stage4_attn(l, last):
            with ExitStack() as st:
                kvn, kvn_b = TL(st, "akvn", [128, 4, T], BF16)
                kra, kra_b = TL(st, "akr", [128, T], BF16)
                v4, v4_b = TL(st, "v4", [128, 34, 512], BF16)
                kh = [TL(st, f"kh{i}", [128, T], BF16) for i in range(2)]
                qn = [TL(st, f"aqn{i}", [128, 6, 512], BF16) for i in range(3)]
                rp = [TL(st, f"arp{i}", [64, 2, 512], F32) for i in range(3)]
                qnp = [TL(st, f"qnp{i}", [128, 512], BF16) for i in range(2)]
                qr = [TL(st, f"qr{i}", [128, 512], BF16) for i in range(2)]
                r1, r1_b = TL(st, "ar1", [64, 512], F32)
                r2, r2_b = TL(st, "ar2", [64, 512], F32)
                pt = [TL(st, f"pt{i}", [128, 512], BF16) for i in range(6)]
                accs = [TL(st, f"acc{i}", [128, 512], F32) for i in range(2)]
                dhi, dhi_b = TL(st, "dhi", [128, 512], BF16)
                dlo, dlo_b = TL(st, "dlo", [128, 512], BF16)
                rd, rd_b = TL(st, "rd", [128, 512], F32)
                ob = [TL(st, f"ob{i}", [128, 512], BF16) for i in range(2)]
                wuk = [TL(st, f"wuk{i}", [128, 4, 128], BF16) for i in range(2)]
                wuv, wuv_b = TL(st, "wuv", [128, 4, 512], BF16)
                wuq = [TL(st, f"wuq{i}", [128, 6, 256], BF16) for i in range(2)]
                trg, _ = TL(st, "trg", [128, 1], F32)
                K.dma("sp", kvn[:], KVNT.rearrange("(c p) t -> p c t", p=128), dbufs["KVNT"], [kvn_b])
                K.op("dve", lambda e: e.memset(kra[64:128, :], 0.0), [], [kra_b])
                for qr_i, qr_ib in qr:
                    K.op("dve", lambda e, qr_i=qr_i: e.memset(qr_i[64:128, :], 0.0), [], [qr_ib])
                K.dma("sp", kra[0:64, :], KPET, dbufs["KPET"], [kra_b])
                qi = 0
                pi = 0
                oi = 0
                for hg in range(4):
                    srcv, _, _ = wtile(l, "wuv", hg)
                    K.dma("sp", wuv[:], srcv, lastw["b"], [wuv_b])
                    for ck in range(34):
                        p = nextps([0, 1, 2, 7, 6])
                        K.mm(ps[p][:, :], psb[p], [(kvn[:, kc, ck * 128:(ck + 1) * 128], wuv[:, kc, :]) for kc in range(4)],
                             [kvn_b, wuv_b])
                        if ck % 2 == 0:
                            K.op("dve", lambda e, ck=ck, p=p: e.tensor_copy(out=v4[:, ck, :], in_=ps[p][:, :]), [psb[p]],
                                 [v4_b])
                        else:
                            K.op("act", lambda e, ck=ck, p=p: e.activation(out=v4[:, ck, :], in_=ps[p][:, :],
                                                                           func=AF.Identity), [psb[p]], [v4_b])
                    for hl in range(4):
                        h = hg * 4 + hl
                        wk_, wk_b = wuk[h % 2]
                        wq_, wq_b = wuq[h % 2]
                        kh_, kh_b = kh[h % 2]
                        srck, _, _ = wtile(l, "wuk", h)
                        K.dma("sp", wk_[:], srck, lastw["b"], [wk_b])
                        srcq, _, _ = wtile(l, "wuq", h)
                        K.dma("sp", wq_[:], srcq, lastw["b"], [wq_b])
                        for kt in range(9):
                            k0 = kt * 512
                            KW = min(512, T - k0)
                            p = nextps([0, 1, 2, 7, 6])
                            K.mm(ps[p][:, :KW], psb[p], [(wk_[:, kc, :], kvn[:, kc, k0:k0 + KW]) for kc in range(4)],
                                 [wk_b, kvn_b])
                            K.op("dve", lambda e, p=p, k0=k0, KW=KW, kh_=kh_: e.tensor_copy(out=kh_[:, k0:k0 + KW],
                                                                                           in_=ps[p][:, :KW]),
                                 [psb[p]], [kh_b])
                        qtiles = [(ti, t0, TT, isctx) for ti, (t0, TT, isctx) in enumerate(TILES) if not (last and (isctx or ti >= NOWN))]
                        qst = {}

                        def load_q(idx, qtiles=qtiles, qst=qst):
                            ti, t0, TT, isctx = qtiles[idx]
                            qn_, qn_b = qn[idx % 3]
                            rp_, rp_b = rp[idx % 3]
                            K.dma("sp", qn_[:, :, :TT], QNT[:, t0:t0 + TT].rearrange("(c p) t -> p c t", p=128),
                                  [dbufs["QNT"][ti]], [qn_b])
                            K.dma("sp", rp_[:, :, :TT], rope_in[:, :, t0:t0 + TT], [nobuf], [rp_b])

                        def proj_q(idx, qtiles=qtiles, qst=qst, wq_=wq_, wq_b=wq_b):
                            ti, t0, TT, isctx = qtiles[idx]
                            qn_, qn_b = qn[idx % 3]
                            rp_, rp_b = rp[idx % 3]
                            qnp_, qnp_b = qnp[idx % 2]
                            qr_, qr_b = qr[idx % 2]
                            pa = nextps([0, 1, 2, 7, 6])
                            K.mm(ps[pa][:, :TT], psb[pa], [(wq_[:, kc, 0:128], qn_[:, kc, :TT]) for kc in range(6)],
                                 [wq_b, qn_b])
                            K.op("act", lambda e: e.activation(out=qnp_[:, :TT], in_=ps[pa][:, :TT],
                                                               func=AF.Identity, scale=ATTN_SCALE), [psb[pa]], [qnp_b])
                            pb1 = nextps([0, 1, 2, 7, 6])
                            K.mm(ps[pb1][0:64, :TT], psb[pb1], [(wq_[:, kc, 128:192], qn_[:, kc, :TT]) for kc in range(6)],
                                 [wq_b, qn_b])
                            K.op("dve", lambda e: e.scalar_tensor_tensor(
                                out=r1[:, :TT], in0=ps[pb1][0:64, :TT], scalar=ATTN_SCALE, in1=rp_[:, 0, :TT],
                                op0=ALU.mult, op1=ALU.mult), [psb[pb1], rp_b], [r1_b])
                            pb2 = nextps([0, 1, 2, 7, 6])
                            K.mm(ps[pb2][0:64, :TT], psb[pb2], [(wq_[:, kc, 192:256], qn_[:, kc, :TT]) for kc in range(6)],
                                 [wq_b, qn_b])
                            K.op("dve", lambda e: e.scalar_tensor_tensor(
                                out=r2[:, :TT], in0=ps[pb2][0:64, :TT], scalar=ATTN_SCALE, in1=rp_[:, 1, :TT],
                                op0=ALU.mult, op1=ALU.mult), [psb[pb2], rp_b], [r2_b])
                            K.op("dve", lambda e: e.tensor_tensor(out=qr_[0:64, :TT], in0=r1[:, :TT], in1=r2[:, :TT],
                                                                  op=ALU.add), [r1_b, r2_b], [qr_b])

                        load_q(0)
                        if len(qtiles) > 1:
                            load_q(1)
                        proj_q(0)
                        for idx, (ti, t0, TT, isctx) in enumerate(qtiles):
                            if idx + 2 < len(qtiles):
                                load_q(idx + 2)
                            if idx + 1 < len(qtiles):
                                proj_q(idx + 1)
                            qnp_, qnp_b = qnp[idx % 2]
                            qr_, qr_b = qr[idx % 2]
                            cks = [32, 33] if isctx else list(range(34))
                            ncks = len(cks)
                            po = 3 + (oi % 2)
                            pd = 5
                            ob_, ob_b = ob[oi % 2]
                            oi += 1
                            sbank = {}
                            used = {}

                            def emit_S(i):
                                ck = cks[i]
                                p = nextps([0, 1, 2, 7, 6])
                                K.mm(ps[p][:, :TT], psb[p],
                                     [(kh_[:, ck * 128:(ck + 1) * 128], qnp_[:, :TT]),
                                      (kra[:, ck * 128:(ck + 1) * 128], qr_[:, :TT])],
                                     [kh_b, qnp_b, kra_b, qr_b])
                                sbank[i] = p

                            LOOK = 3
                            for i in range(min(LOOK, ncks)):
                                emit_S(i)
                            for i, ck in enumerate(cks):
                                if i + LOOK < ncks:
                                    emit_S(i + LOOK)
                                p = sbank[i]
                                pt_, pt_b = pt[pi % 6]
                                pi += 1
                                K.op("act", lambda e, p=p, pt_=pt_: e.activation(out=pt_[:, :TT], in_=ps[p][:, :TT],
                                                                                 func=AF.Exp), [psb[p]], [pt_b])
                                K.mm(ps[po][:, :TT], psb[po], [(v4[:, ck, hl * 128:(hl + 1) * 128], pt_[:, :TT])],
                                     [v4_b, pt_b], start=(i == 0), stop=(i == ncks - 1))
                                r3 = i % 3
                                if r3 == 0:
                                    K.mm(ps[pd][:, :TT], psb[pd], [(ones[:], pt_[:, :TT])], [ones_b, pt_b],
                                         start=(i == 0), stop=False)
                                else:
                                    ae = "pool" if r3 == 1 else "dve"
                                    ac_, ac_b = accs[r3 - 1]
                                    if r3 not in used:
                                        used[r3] = 1
                                        K.op(ae, lambda e, ac_=ac_, pt_=pt_: e.tensor_copy(out=ac_[:, :TT], in_=pt_[:, :TT]),
                                             [pt_b], [ac_b])
                                    else:
                                        K.op(ae, lambda e, ac_=ac_, pt_=pt_: e.tensor_tensor(
                                            out=ac_[:, :TT], in0=ac_[:, :TT], in1=pt_[:, :TT], op=ALU.add),
                                            [pt_b, ac_b], [ac_b])
                            if 2 in used:
                                K.op("dve", lambda e: e.tensor_tensor(out=accs[0][0][:, :TT], in0=accs[0][0][:, :TT],
                                                                      in1=accs[1][0][:, :TT], op=ALU.add),
                                     [accs[0][1], accs[1][1]], [accs[0][1]])
                            K.op("dve", lambda e: e.tensor_copy(out=dhi[:, :TT], in_=accs[0][0][:, :TT]), [accs[0][1]],
                                 [dhi_b])
                            K.op("dve", lambda e: e.tensor_tensor(out=dlo[:, :TT], in0=accs[0][0][:, :TT], in1=dhi[:, :TT],
                                                                  op=ALU.subtract), [accs[0][1], dhi_b], [dlo_b])
                            K.mm(ps[pd][:, :TT], psb[pd], [(ones[:], dhi[:, :TT]), (ones[:], dlo[:, :TT])],
                                 [ones_b, dhi_b, dlo_b], start=False, stop=True)
                            K.op("dve", lambda e, pd=pd: e.reciprocal(out=rd[:, :TT], in_=ps[pd][:, :TT]), [psb[pd]],
                                 [rd_b])
                            K.op("dve", lambda e, po=po, ob_=ob_: e.tensor_tensor(out=ob_[:, :TT], in0=ps[po][:, :TT],
                                                                                  in1=rd[:, :TT], op=ALU.mult),
                                 [psb[po], rd_b], [ob_b])
                            K.dma("act", OT[h * 128:(h + 1) * 128, t0:t0 + TT], ob_[:, :TT], [ob_b],
                                  [dbufs["OT"][ti]])
                        if h == 0 and l + 1 < DEPTH:
                            trig_b = Buf()
                            K.op("dve", lambda e: e.memset(trg[:], 0.0), [], [trig_b])
                            convert_weights(l + 1, trig_b)
                K.barrier()

        def stage5a(l, last):
            with ExitStack() as st:
                hx, hx_b = TL(st, "mhx", [128, 16, 512], BF16)
                yf, yf_b = TL(st, "myf", [128, 8, 512], BF16)
                cs, cs_b = TL(st, "mcs", [128, 8, 512], BF16)
                ot, ot_b = TL(st, "mot", [128, 16, 512], BF16)
                gw = [TL(st, f"gw{i}", [128, 16, 768], BF16) for i in range(2)]
                wf = [TL(st, f"mwf{i}", [128, 8, 256], BF16) for i in range(2)]
                wc = [TL(st, f"mwc{i}", [128, 8, 256], BF16) for i in range(2)]
                wo = [TL(st, f"mwo{i}", [128, 16, 256], BF16) for i in range(2)]
                wout = [TL(st, f"mwout{i}", [128, 16, 256], BF16) for i in range(2)]
                mg, mg_b = TL(st, "mg", [128, 16, 512], BF16)
                sgs = [TL(st, f"msg{i}", [128, 512], F32) for i in range(3)]
                ms = [TL(st, f"mm{i}", [128, 512], F32) for i in range(3)]
                xr = [TL(st, f"xr{i}", [128, 512], F32) for i in range(2)]
                xo = [TL(st, f"xo{i}", [128, 512], F32) for i in range(2)]
                wi = 0
                xi = 0
                for ti, (t0, TT, isctx) in enumerate(TILES):
                    if last and (isctx or ti >= NOWN):
                        continue
                    jj_ = 1 if isctx else 0
                    K.dma("sp", hx[:, :, :TT], HXT[:, t0:t0 + TT].rearrange("(k p) t -> p k t", p=128),
                          [dbufs["HXT"][ti]], [hx_b])
                    K.dma("sp", yf[:, :, :TT], YFT[:, t0:t0 + TT].rearrange("(k p) t -> p k t", p=128),
                          [dbufs["YFT"][ti]], [yf_b])
                    K.dma("sp", cs[:, :, :TT], CST[:, t0:t0 + TT].rearrange("(k p) t -> p k t", p=128),
                          [dbufs["CST"][ti]], [cs_b])
                    K.dma("sp", ot[:, :, :TT], OT[:, t0:t0 + TT].rearrange("(k p) t -> p k t", p=128),
                          [dbufs["OT"][ti]], [ot_b])
                    for jp in range(8):
                        gw_, gw_b = gw[wi % 2]
                        wf_, wf_b = wf[wi % 2]
                        wc_, wc_b = wc[wi % 2]
                        wo_, wo_b = wo[wi % 2]
                        wi += 1
                        K.dma("sp", gw_[:], wtile(l, "gate", jp)[0], lastw["b"], [gw_b])
                        K.dma("sp", wf_[:], wtile(l, "wf", jp)[0], lastw["b"], [wf_b])
                        K.dma("sp", wc_[:], wtile(l, "wc", jp)[0], lastw["b"], [wc_b])
                        K.dma("sp", wo_[:], wtile(l, "wo", jp)[0], lastw["b"], [wo_b])
                        for jj in range(2):
                            j = jp * 2 + jj
                            pg = [0, 1, 2]
                            py = [3, 4, 5]
                            for bi in range(3):
                                K.mm(ps[pg[bi]][:, :TT], psb[pg[bi]],
                                     [(gw_[:, k, jj * 384 + bi * 128:jj * 384 + (bi + 1) * 128], hx[:, k, :TT])
                                      for k in range(KC)], [gw_b, hx_b])
                                K.op("act", lambda e, bi=bi, j=j: e.activation(
                                    out=sgs[bi][0][:, :TT], in_=ps[pg[bi]][:, :TT], func=AF.Sigmoid,
                                    bias=V(l, "bgate", bi * 16 + j)), [psb[pg[bi]], vec_b[l]], [sgs[bi][1]])
                            K.mm(ps[3][:, :TT], psb[3], [(wf_[:, k, jj * 128:(jj + 1) * 128], yf[:, k, :TT])
                                                         for k in range(8)], [wf_b, yf_b])
                            K.mm(ps[4][:, :TT], psb[4], [(wc_[:, k, jj * 128:(jj + 1) * 128], cs[:, k, :TT])
                                                         for k in range(8)], [wc_b, cs_b])
                            K.mm(ps[5][:, :TT], psb[5], [(wo_[:, k, jj * 128:(jj + 1) * 128], ot[:, k, :TT])
                                                         for k in range(KC)], [wo_b, ot_b])
                            for bi in range(3):
                                K.op("dve", lambda e, bi=bi: e.tensor_tensor(
                                    out=ms[bi][0][:, :TT], in0=ps[py[bi]][:, :TT], in1=sgs[bi][0][:, :TT], op=ALU.mult),
                                    [psb[py[bi]], sgs[bi][1]], [ms[bi][1]])
                            K.op("pool", lambda e: e.tensor_tensor(out=ms[0][0][:, :TT], in0=ms[0][0][:, :TT],
                                                                   in1=ms[1][0][:, :TT], op=ALU.add),
                                 [ms[0][1], ms[1][1]], [ms[0][1]])
                            K.op("pool", lambda e, j=j: e.tensor_tensor(out=mg[:, j, :TT], in0=ms[0][0][:, :TT],
                                                                        in1=ms[2][0][:, :TT], op=ALU.add),
                                 [ms[0][1], ms[2][1]], [mg_b])
                    xs_ap = (ctxT_in if isctx else xT_in[:, t0:t0 + TT]) if l == 0 else XB[:, t0:t0 + TT]
                    xs_b = nobuf if l == 0 else dbufs["XB"][ti]
                    for mp in range(8):
                        wo2, wo2_b = wout[mp % 2]
                        K.dma("sp", wo2[:], wtile(l, "wout", mp)[0], lastw["b"], [wo2_b])
                        for jj in range(2):
                            j = mp * 2 + jj
                            xr_, xr_b = xr[xi % 2]
                            xo_, xo_b = xo[xi % 2]
                            xi += 1
                            K.dma("sp", xr_[:, :TT], xs_ap[j * 128:(j + 1) * 128, :], [xs_b], [xr_b])
                            p = nextps([6, 7])
                            K.mm(ps[p][:, :TT], psb[p], [(wo2[:, k, jj * 128:(jj + 1) * 128], mg[:, k, :TT])
                                                         for k in range(KC)], [wo2_b, mg_b])
                            K.op("dve", lambda e, p=p, j=j, xr_=xr_, xo_=xo_: e.scalar_tensor_tensor(
                                out=xo_[:, :TT], in0=ps[p][:, :TT], scalar=modv[:, 32 + j, jj_:jj_ + 1], in1=xr_[:, :TT],
                                op0=ALU.mult, op1=ALU.add), [psb[p], xr_b, modv_b], [xo_b])
                            K.dma("act", XA[j * 128:(j + 1) * 128, t0:t0 + TT], xo_[:, :TT], [xo_b], [dbufs["XA"][ti]])
                K.barrier()

        def stage5b(l, last):
            with ExitStack() as st:
                xt, xt_b = TL(st, "fxt", [128, 16, 512], F32)
                sq, sq_b = TL(st, "fsq", [128, 16, 512], BF16)
                h2, h2_b = TL(st, "h2", [128, 16, 512], BF16)
                rstd, rstd_b = TL(st, "frstd", [128, 512], F32)
                tmps = [TL(st, f"ftmp{i}", [128, 512], F32) for i in range(2)]
                wg = [TL(st, f"fwg{i}", [128, 16, 256], BF16) for i in range(2)]
                wu = [TL(st, f"fwu{i}", [128, 16, 256], BF16) for i in range(2)]
                wd = [TL(st, f"fwd{i}", [128, 44, 128], BF16) for i in range(2)]
                hid, hid_b = TL(st, "hid", [128, HC, 512], BF16)
                sgl = [TL(st, f"fsg{i}", [128, 512], BF16) for i in range(2)]
                xo = [TL(st, f"fxo{i}", [128, 512], F32) for i in range(2)]
                wi = 0
                si = 0
                for ti, (t0, TT, isctx) in enumerate(TILES):
                    if last and (isctx or ti >= NOWN):
                        continue
                    jj_ = 1 if isctx else 0
                    front(XA[:, t0:t0 + TT], dbufs["XA"][ti], TT, xt, xt_b, sq, sq_b, rstd, rstd_b, tmps,
                          lambda k: (h2[:, k, :TT], h2_b), lambda k: S2[:, k, jj_:jj_ + 1],
                          lambda k: modv[:, 48 + k, jj_:jj_ + 1], [S2_b, modv_b])
                    for hp in range(22):
                        wg_, wg_b = wg[wi % 2]
                        wu_, wu_b = wu[wi % 2]
                        wi += 1
                        K.dma("sp", wg_[:], wtile(l, "wg", hp)[0], lastw["b"], [wg_b])
                        K.dma("sp", wu_[:], wtile(l, "wu", hp)[0], lastw["b"], [wu_b])
                        for jj in range(2):
                            hc = hp * 2 + jj
                            pg = nextps([1, 2])
                            pu = nextps([3, 4])
                            K.mm(ps[pg][:, :TT], psb[pg], [(wg_[:, k, jj * 128:(jj + 1) * 128], h2[:, k, :TT])
                                                           for k in range(KC)], [wg_b, h2_b])
                            K.mm(ps[pu][:, :TT], psb[pu], [(wu_[:, k, jj * 128:(jj + 1) * 128], h2[:, k, :TT])
                                                           for k in range(KC)], [wu_b, h2_b])
                            sg_, sg_b = sgl[si % 2]
                            si += 1
                            K.op("act", lambda e, pg=pg, sg_=sg_: e.activation(out=sg_[:, :TT], in_=ps[pg][:, :TT],
                                                                               func=AF.Silu), [psb[pg]], [sg_b])
                            K.op("dve", lambda e, pu=pu, hc=hc, sg_=sg_: e.tensor_tensor(
                                out=hid[:, hc, :TT], in0=ps[pu][:, :TT], in1=sg_[:, :TT], op=ALU.mult),
                                [psb[pu], sg_b], [hid_b])
                    for j in range(16):
                        wd_, wd_b = wd[j % 2]
                        xo_, xo_b = xo[j % 2]
                        K.dma("sp", wd_[:], wtile(l, "wd", j)[0], lastw["b"], [wd_b])
                        p = nextps([5, 6, 7])
                        K.mm(ps[p][:, :TT], psb[p], [(wd_[:, k, :], hid[:, k, :TT]) for k in range(HC)], [wd_b, hid_b])
                        K.op("dve", lambda e, p=p, j=j, xo_=xo_: e.scalar_tensor_tensor(
                            out=xo_[:, :TT], in0=ps[p][:, :TT], scalar=modv[:, 80 + j, jj_:jj_ + 1], in1=xt[:, j, :TT],
                            op0=ALU.mult, op1=ALU.add), [psb[p], xt_b, modv_b], [xo_b])
                        K.dma("act", XB[j * 128:(j + 1) * 128, t0:t0 + TT], xo_[:, :TT], [xo_b], [dbufs["XB"][ti]])
                K.barrier()

        def final_norm():
            with ExitStack() as st:
                xt, xt_b = TL(st, "nxt", [128, 16, 512], F32)
                sq, sq_b = TL(st, "nsq", [128, 16, 512], BF16)
                yo, yo_b = TL(st, "nyo", [128, 16, 512], F32)
                rstd, rstd_b = TL(st, "nrstd", [128, 512], F32)
                tmps = [TL(st, f"ntmp{i}", [128, 512], F32) for i in range(2)]
                lastl = DEPTH - 1
                for ti, (t0, TT, isctx) in enumerate(TILES):
                    if isctx or ti >= NOWN:
                        continue
                    front(XB[:, t0:t0 + TT], dbufs["XB"][ti], TT, xt, xt_b, sq, sq_b, rstd, rstd_b, tmps,
                          lambda k: (yo[:, k, :TT], yo_b), lambda k: V(lastl, "gfin", k), lambda k: zb[:, 0:1],
                          [vec_b[lastl], zb_b])
                    K.dma("act", yT[:, t0:t0 + TT].rearrange("(k p) t -> p k t", p=128), yo[:, :, :TT], [yo_b],
                          [dbufs["Y"][ti]])
                K.barrier()

        for l in range(DEPTH):
            last = l == DEPTH - 1
            stage0(l)
            if l == 0:
                convert_weights(0)
            stage1(l, last)
            stage2_dft(l, last)
            stage3_conv(l, last)
            stage4_attn(l, last)
            stage5a(l, last)
            stage5b(l, last)
        final_norm()
        K.barrier()
    return nc


_NC_CACHE = {}


def kernel(**inp):
    inp = {k: np.asarray(v) for k, v in inp.items()}
    wpack = np.concatenate([_pack_weights(inp, l) for l in range(DEPTH)]).reshape(-1, 2048)
    vec = np.stack([_pack_vec(inp, l) for l in range(DEPTH)])
    if "nc" not in _NC_CACHE:
        _NC_CACHE["nc"] = build()
    nc = _NC_CACHE["nc"]
    H = NLAT // 2
    in_maps = []
    for c in range(NCORE):
        b, half = c // 2, c % 2
        cst = _constants(half)
        cv = np.stack([_pm(inp["c"][b]), _pm(inp["c_ctx"])], axis=-1).reshape(128, 32)
        xb = inp["x"][b]
        if half == 1:
            xb = np.concatenate([xb[H:], xb[:H]], axis=0)
        in_maps.append({
            "xT": np.ascontiguousarray(xb.T),
            "ctxT": np.ascontiguousarray(inp["ctx"][b].T),
            "cvec": np.ascontiguousarray(cv.astype(np.float32)),
            "wpack": wpack, "vec": vec, "cst": cst["cst"], "rope": cst["rope"],
            "tabl": cst["tabl"].reshape(8, 128, -1), "tabc": cst["tabc"].reshape(128, -1),
        })
    res = run_bass_kernel_spmd(nc, in_maps, core_ids=list(range(NCORE)))
    out = np.empty((4, NLAT, D), np.float32)
    for c in range(NCORE):
        b, half = c // 2, c % 2
        out[b, half * H:(half + 1) * H, :] = res.results[c]["yT"].T
    return out
```

```python
import numpy as np
import ml_dtypes
from contextlib import ExitStack
import concourse.bass as bass
import concourse.mybir as mybir
from concourse.bass_utils import run_bass_kernel_spmd

F32 = mybir.dt.float32
BF16 = mybir.dt.bfloat16
AF = mybir.ActivationFunctionType
ALU = mybir.AluOpType

D = 2048
KC = 16
NLAT = 4096
NCTX = 256
T = NLAT + NCTX
DEPTH = 2
NCORE = 8
NOWN = 4
EPS = 1e-6
ATTN_SCALE = 192 ** -0.5
FFN = 5632
HC = FFN // 128
VT_LAT0 = 16
VT_CTX0 = 16 + NLAT + 32
VTW = NLAT + 32 + NCTX + 32
TILES = [(i * 512, 512, False) for i in range(8)] + [(NLAT, 256, True)]

S1_TILES = [("F", 0, 512), ("F", 512, 512), ("G", 0, 512), ("A", 0, 512), ("G", 512, 512), ("A", 512, 512),
            ("Q", 0, 512), ("Q", 512, 256), ("KV", 0, 512), ("KPE", 0, 128)]


def _tm(W, mw):
    K, M = W.shape
    return np.ascontiguousarray(W.reshape(K // 128, 128, M // mw, mw).transpose(2, 1, 0, 3))


def _weight_layout():
    specs = []
    for i, (nm, c0, w) in enumerate(S1_TILES):
        specs.append((f"s1_{i}", 1, 16, w))
    specs += [("wuq", 16, 6, 256), ("wuk", 16, 4, 128), ("wuv", 4, 4, 512),
              ("gate", 8, 16, 768), ("wf", 8, 8, 256), ("wc", 8, 8, 256), ("wo", 8, 16, 256),
              ("wout", 8, 16, 256), ("wg", 22, 16, 256), ("wu", 22, 16, 256), ("wd", 16, 44, 128),
              ("ada", 24, 16, 512)]
    lay = {}
    off = 0
    for nm, nt, kc, mw in specs:
        lay[nm] = (off, nt, kc, mw)
        off += nt * 128 * kc * mw
    tot = (off + 2047) // 2048 * 2048
    return lay, tot


WLAY, NW = _weight_layout()

VEC_COLS = {}
_o = 0
for _nm, _n in [("ada_b", 96), ("gmix", 16), ("bgate", 48), ("conv_b", 8), ("ln_g", 8), ("ln_b", 8),
                ("qg", 6), ("kvg", 4), ("gffn", 16), ("convw", 248), ("gfin", 16)]:
    VEC_COLS[_nm] = _o
    _o += _n
NV = _o


def _pm(v):
    return np.ascontiguousarray(v.reshape(-1, 128).T)


def _pack_weights(inp, l):
    buf = np.zeros(NW, np.float32)

    def put(nm, arr):
        off, nt, kc, mw = WLAY[nm]
        assert arr.shape == (nt, 128, kc, mw), (nm, arr.shape)
        buf[off:off + arr.size] = arr.reshape(-1)

    put("ada", _tm(inp["ada_w"][l], 512))
    w_in = inp["w_in"][l]
    kpe = w_in[:, 4352:4416]
    cols = {"F": w_in[:, 0:1024], "A": w_in[:, 1024:2048], "G": w_in[:, 2048:3072], "Q": w_in[:, 3072:3840],
            "KV": w_in[:, 3840:4352],
            "KPE": np.concatenate([kpe, kpe[:, 32:64], kpe[:, 0:32]], axis=1)}
    for i, (nm, c0, w) in enumerate(S1_TILES):
        put(f"s1_{i}", _tm(cols[nm][:, c0:c0 + w], w))
    wuq = inp["w_uq"][l].reshape(768, 16, 192)
    wuq = np.concatenate([wuq, wuq[:, :, 160:192], wuq[:, :, 128:160]], axis=2).reshape(768, 16 * 256)
    put("wuq", _tm(wuq, 256))
    wukv = inp["w_ukv"][l].reshape(512, 16, 256)
    put("wuk", _tm(np.ascontiguousarray(wukv[:, :, 0:128]).reshape(512, 2048), 128))
    put("wuv", _tm(np.ascontiguousarray(wukv[:, :, 128:256]).reshape(512, 2048), 512))
    g = w_in[:, 4416:].reshape(2048, 3, 16, 128).transpose(0, 2, 1, 3).reshape(2048, 6144)
    put("gate", _tm(np.ascontiguousarray(g), 768))
    put("wf", _tm(inp["w_fourier"][l], 256))
    put("wc", _tm(inp["w_conv_out"][l], 256))
    put("wo", _tm(inp["w_mla_o"][l], 256))
    put("wout", _tm(inp["w_out"][l], 256))
    put("wg", _tm(inp["w_ffn_gate"][l], 256))
    put("wu", _tm(inp["w_ffn_up"][l], 256))
    put("wd", _tm(inp["w_ffn_down"][l], 128))
    return buf


def _pack_vec(inp, l):
    v = np.zeros((128, NV), np.float32)

    def put(nm, arr):
        v[:, VEC_COLS[nm]:VEC_COLS[nm] + arr.shape[1]] = arr

    put("ada_b", _pm(inp["ada_b"][l]))
    put("gmix", _pm(inp["norm_mix_g"][l]))
    put("bgate", _pm(inp["b_gate"][l]))
    put("conv_b", _pm(inp["conv_b"][l]))
    put("ln_g", _pm(inp["conv_ln_g"][l]))
    put("ln_b", _pm(inp["conv_ln_b"][l]))
    put("qg", _pm(inp["q_norm_g"][l]))
    put("kvg", _pm(inp["kv_norm_g"][l]))
    put("gffn", _pm(inp["norm_ffn_g"][l]))
    cw = inp["conv_w"][l]
    put("convw", np.ascontiguousarray(cw.reshape(31, 8, 128).transpose(2, 1, 0)).reshape(128, 248))
    put("gfin", _pm(inp["final_norm_g"]))
    return v


_CONST_CACHE = {}


def _constants(half):
    if half in _CONST_CACHE:
        return _CONST_CACHE[half]
    bf = ml_dtypes.bfloat16
    cst = np.zeros((128, 386), np.float32)
    cst[:, 0:128] = np.eye(128, dtype=np.float32)
    cc = np.arange(128)
    ang = 2 * np.pi * ((cc[:, None] * cc[None, :]) % 128) / 128.0
    cst[:, 128:256] = np.cos(ang)
    cst[:, 256:384] = np.sin(ang)
    gidx = (np.arange(NLAT) + half * (NLAT // 2)) % NLAT
    n_freq = 16
    inv_freq = (10000.0 ** (-np.arange(n_freq, dtype=np.float32) / n_freq)).astype(np.float32)
    row = (gidx // 64).astype(np.float32)
    col = (gidx % 64).astype(np.float32)
    angr = np.concatenate([row[:, None] * inv_freq, col[:, None] * inv_freq], axis=-1).astype(np.float32)
    cos = np.ones((T, 32), np.float32)
    sin = np.zeros((T, 32), np.float32)
    cos[:NLAT] = np.cos(angr)
    sin[:NLAT] = np.sin(angr)
    rope = np.zeros((64, 2, T), np.float32)
    rope[0:32, 0] = cos.T
    rope[32:64, 0] = cos.T
    rope[0:32, 1] = -sin.T
    rope[32:64, 1] = sin.T
    tabl = np.zeros((8, 128, 32, 2, 512), bf)
    for kt in range(8):
        k = gidx[kt * 512 + np.arange(512)]
        a_ = 2 * np.pi * ((gidx[:, None] * k[None, :]) % NLAT) / float(NLAT)
        tabl[kt, :, :, 0, :] = np.cos(a_).reshape(32, 128, 512).transpose(1, 0, 2).astype(bf)
        tabl[kt, :, :, 1, :] = (-np.sin(a_)).reshape(32, 128, 512).transpose(1, 0, 2).astype(bf)
    n2 = np.arange(NCTX)
    a_ = 2 * np.pi * ((n2[:, None] * n2[None, :]) % NCTX) / float(NCTX)
    tabc = np.zeros((128, 2, 2, 256), bf)
    tabc[:, :, 0, :] = np.cos(a_).reshape(2, 128, 256).transpose(1, 0, 2).astype(bf)
    tabc[:, :, 1, :] = (-np.sin(a_)).reshape(2, 128, 256).transpose(1, 0, 2).astype(bf)
    cst[:, 384] = 1.0 if half == 1 else 0.0
    cst[:, 385] = 1.0 if half == 0 else 0.0
    _CONST_CACHE[half] = dict(cst=cst, rope=rope, tabl=tabl, tabc=tabc)
    return _CONST_CACHE[half]


class Ev:
    __slots__ = ("sem", "val", "key")

    def __init__(self, sem, val, key):
        self.sem, self.val, self.key = sem, val, key


class Buf:
    __slots__ = ("w", "r")

    def __init__(self):
        self.w = None
        self.r = {}


class Eng:
    def __init__(self, kern, name, eng, selfsync):
        self.k, self.name, self.e, self.selfsync = kern, name, eng, selfsync
        self.waited = {}
        self.sem = None
        self.cnt = 0
        self.key = None
        self.last = None
        self.nsem = 0

    def _newsem(self):
        self.sem = self.k.es.enter_context(self.k.nc.semaphore(f"s_{self.name}_{self.nsem}"))
        self.key = (self.name, self.nsem)
        self.nsem += 1
        self.cnt = 0

    def wait(self, ev):
        if ev is None:
            return
        if ev.key == self.key and not self.selfsync:
            return
        if self.waited.get(ev.key, 0) >= ev.val:
            return
        self.e.wait_ge(ev.sem, ev.val)
        self.waited[ev.key] = ev.val

    def bump(self, inst):
        if self.sem is None or self.cnt >= 30000:
            self._newsem()
        self.cnt += 1
        inst.then_inc(self.sem, 1)
        self.last = Ev(self.sem, self.cnt, self.key)
        return self.last


class Kern:
    NDMA = 40

    def __init__(self, nc, es):
        self.nc, self.es = nc, es
        self.E = {"pe": Eng(self, "pe", nc.tensor, False), "act": Eng(self, "act", nc.scalar, True),
                  "dve": Eng(self, "dve", nc.vector, True), "pool": Eng(self, "pool", nc.gpsimd, True),
                  "sp": Eng(self, "sp", nc.sync, False)}
        self.dsem = []
        self.dpool = {}
        self.dcnt = {"sp": 0, "act": 0, "pool": 0}
        i = 0
        for q, n in (("sp", 24), ("act", 12), ("pool", 8)):
            self.dpool[q] = list(range(i, i + n))
            for _ in range(n):
                self.dsem.append([es.enter_context(nc.semaphore(f"s_dma_{i}")), 0, None])
                i += 1

    def _sync(self, E, reads, writes):
        for b in reads:
            E.wait(b.w)
        for b in writes:
            E.wait(b.w)
            for r in b.r.values():
                E.wait(r)

    def _rec(self, ev, reads, writes):
        for b in reads:
            b.r[ev.key] = ev
        for b in writes:
            b.w = ev
            b.r = {}

    def op(self, eng, fn, reads=(), writes=()):
        E = self.E[eng]
        self._sync(E, reads, writes)
        ev = E.bump(fn(E.e))
        self._rec(ev, reads, writes)
        return ev

    def mm(self, out, pb, pairs, reads, start=True, stop=True):
        E = self.E["pe"]
        self._sync(E, reads, [pb])
        n = len(pairs)
        inst = None
        for i, (l, r) in enumerate(pairs):
            inst = self.nc.tensor.matmul(out, lhsT=l, rhs=r, start=(start and i == 0), stop=(stop and i == n - 1))
        ev = E.bump(inst)
        self._rec(ev, reads, [pb])
        return ev

    def dma(self, q, out, in_, reads=(), writes=(), **kw):
        E = self.E[q]
        self._sync(E, reads, writes)
        pool = self.dpool[q]
        idx = pool[self.dcnt[q] % len(pool)]
        self.dcnt[q] += 1
        slot = self.dsem[idx]
        E.wait(slot[2])
        inst = E.e.dma_start(out=out, in_=in_, **kw)
        slot[1] += 16
        inst.then_inc(slot[0], 16)
        ev = Ev(slot[0], slot[1], ("dma", idx))
        slot[2] = ev
        self._rec(ev, reads, writes)
        return ev

    def barrier(self):
        evs = [E.last for E in self.E.values() if E.last is not None] + [s[2] for s in self.dsem if s[2] is not None]
        for E in self.E.values():
            for ev in evs:
                E.wait(ev)


def build():
    nc = bass.Bass("TRN2", target_bir_lowering=False)
    xT_in = nc.dram_tensor("xT", [D, NLAT], F32, kind="ExternalInput").ap()
    ctxT_in = nc.dram_tensor("ctxT", [D, NCTX], F32, kind="ExternalInput").ap()
    cvec_in = nc.dram_tensor("cvec", [128, 32], F32, kind="ExternalInput").ap()
    wpack = nc.dram_tensor("wpack", [DEPTH * NW // 2048, 2048], F32, kind="ExternalInput").ap()
    vec_in = nc.dram_tensor("vec", [DEPTH, 128, NV], F32, kind="ExternalInput").ap()
    cst_in = nc.dram_tensor("cst", [128, 386], F32, kind="ExternalInput").ap()
    rope_in = nc.dram_tensor("rope", [64, 2, T], F32, kind="ExternalInput").ap()
    tabl_in = nc.dram_tensor("tabl", [8, 128, 32 * 2 * 512], BF16, kind="ExternalInput").ap()
    tabc_in = nc.dram_tensor("tabc", [128, 2 * 2 * 256], BF16, kind="ExternalInput").ap()
    yT = nc.dram_tensor("yT", [D, NLAT // 2], F32, kind="ExternalOutput").ap()

    wbf2 = [nc.dram_tensor(f"wbf{l}", [NW // 2048, 2048], BF16).ap() for l in range(DEPTH)]
    wbf = [w.rearrange("a b -> (a b)") for w in wbf2]
    XA = nc.dram_tensor("XA", [D, T], F32).ap()
    XB = nc.dram_tensor("XB", [D, T], F32).ap()
    HXT = nc.dram_tensor("HXT", [D, T], BF16).ap()
    AB = nc.dram_tensor("AB", [T, 2048], BF16).ap()
    VT = nc.dram_tensor("VT", [1024, VTW], BF16).ap()
    YFT = nc.dram_tensor("YFT", [1024, T], BF16).ap()
    CST = nc.dram_tensor("CST", [1024, T], BF16).ap()
    QNT = nc.dram_tensor("QNT", [768, T], BF16).ap()
    KVNT = nc.dram_tensor("KVNT", [512, T], BF16).ap()
    KPET = nc.dram_tensor("KPET", [64, T], BF16).ap()
    OT = nc.dram_tensor("OT", [D, T], BF16).ap()
    SEAM = nc.dram_tensor("SEAM", [4, 1024, 16], BF16).ap()

    NT = len(TILES)
    dbufs = {nm: [Buf() for _ in range(NT)] for nm in
             ["XA", "XB", "HXT", "AB", "VT", "YFT", "CST", "QNT", "KVNT", "KPET", "OT", "Y"]}
    seam_b = [Buf() for _ in range(4)]
    wb = [Buf() for _ in range(DEPTH)]
    nobuf = Buf()

    with ExitStack() as es:
        K = Kern(nc, es)

        uniq = {"n": 0}

        def TL(st, name, shape, dt):
            uniq["n"] += 1
            return st.enter_context(nc.sbuf_tensor(f"{name}_{uniq['n']}", list(shape), dt)), Buf()

        ps = [es.enter_context(nc.psum_tensor(f"ps{i}", [128, 512], F32)) for i in range(8)]
        psb = [Buf() for _ in range(8)]
        rot = {"i": 0}

        def nextps(cands):
            rot["i"] += 1
            return cands[rot["i"] % len(cands)]

        cstf, cstf_b = TL(es, "cstf", [128, 386], F32)
        csm, csm_b = TL(es, "csm", [128, 256], BF16)
        ones, ones_b = TL(es, "ones", [128, 128], BF16)
        zer, zer_b = TL(es, "zer", [128, 8, 32], BF16)
        vecs, vec_b = [], []
        for l in range(DEPTH):
            t_, b_ = TL(es, f"vec{l}", [128, NV], F32)
            vecs.append(t_)
            vec_b.append(b_)
        modv, modv_b = TL(es, "modv", [128, 96, 2], F32)
        S1, S1_b = TL(es, "S1", [128, 16, 2], F32)
        S2, S2_b = TL(es, "S2", [128, 16, 2], F32)
        zb, zb_b = TL(es, "zb", [128, 1], F32)

        K.dma("sp", cstf[:], cst_in, [nobuf], [cstf_b])
        for l in range(DEPTH):
            K.dma("sp", vecs[l][:], vec_in[l], [nobuf], [vec_b[l]])
        K.op("act", lambda e: e.activation(out=csm[:], in_=cstf[:, 128:384], func=AF.Identity), [cstf_b], [csm_b])
        K.op("dve", lambda e: e.memset(ones[:], 1.0), [], [ones_b])
        K.op("dve", lambda e: e.memset(zer[:], 0.0), [], [zer_b])
        K.op("dve", lambda e: e.memset(zb[:], 0.0), [], [zb_b])
        epsb, epsb_b = TL(es, "epsb", [128, 1], F32)
        K.op("dve", lambda e: e.memset(epsb[:], EPS), [], [epsb_b])
        vtpad_b = Buf()
        for a0 in (0, VT_LAT0 + NLAT, VT_CTX0 + NCTX):
            w_ = 16 if a0 != VT_LAT0 + NLAT else 32
            K.dma("sp", VT[:, a0:a0 + w_].rearrange("(c p) t -> p c t", p=128), zer[:, :, 0:w_], [zer_b], [vtpad_b])

        cvb = [[], []]

        def convert_weights(l, trig=None):
            rows = NW // 2048
            crow = WLAY["ada"][0] // 2048
            assert WLAY["ada"][0] % 2048 == 0
            r0 = 0
            while r0 < crow:
                n = min(1024, crow - r0)
                b_ = Buf()
                K.dma("pool", wbf2[l][r0:r0 + n, :], wpack[l * rows + r0:l * rows + r0 + n, :],
                      [nobuf] if trig is None else [trig], [b_])
                cvb[l].append(b_)
                r0 += n

        wpack_flat = wpack.rearrange("a b -> (a b)")
        lastw = {"b": []}

        def wtile(l, name, ti):
            off, nt, kc, mw = WLAY[name]
            o = off + ti * 128 * kc * mw
            b0 = (o // 2048) // 1024
            b1 = ((o + 128 * kc * mw - 1) // 2048) // 1024
            lastw["b"] = cvb[l][b0:b1 + 1]
            return wbf[l][o:o + 128 * kc * mw].rearrange("(p k m) -> p k m", p=128, k=kc), kc, mw

        def V(l, nm, i=0, n=1):
            c = VEC_COLS[nm] + i
            return vecs[l][:, c:c + n]

        def xsrc(l, phase):
            pass

        def stage0(l):
            with ExitStack() as st:
                cv, cv_b = TL(st, "cv", [128, 32], F32)
                scb, scb_b = TL(st, "scb", [128, 16, 2], BF16)
                wsl = [TL(st, f"adaw{i}", [128, 16, 512], BF16) for i in range(2)]
                K.dma("sp", cv[:], cvec_in, [nobuf], [cv_b])
                K.op("act", lambda e: e.activation(out=scb[:].rearrange("p k j -> p (k j)"), in_=cv[:], func=AF.Silu),
                     [cv_b], [scb_b])
                for ti in range(24):
                    wt, wt_b = wsl[ti % 2]
                    off_, _, kc_, mw_ = WLAY["ada"]
                    o_ = l * NW + off_ + ti * 128 * kc_ * mw_
                    K.dma("pool", wt[:], wpack_flat[o_:o_ + 128 * kc_ * mw_].rearrange("(p k m) -> p k m", p=128, k=kc_),
                          [nobuf], [wt_b], max_dma_last_dim=4096)
                    for mi in range(4):
                        m = ti * 4 + mi
                        K.mm(ps[0][:, 2 * m:2 * m + 2], psb[0],
                             [(wt[:, k, mi * 128:(mi + 1) * 128], scb[:, k, :]) for k in range(KC)], [wt_b, scb_b])
                vl = vecs[l]
                for j in range(2):
                    K.op("dve", lambda e, j=j: e.tensor_tensor(
                        out=modv[:, :, j], in0=ps[0][:, 0:192].rearrange("p (m j) -> p m j", j=2)[:, :, j],
                        in1=vl[:, VEC_COLS["ada_b"]:VEC_COLS["ada_b"] + 96], op=ALU.add),
                        [psb[0], vec_b[l]], [modv_b])
                for j in range(2):
                    K.op("dve", lambda e, j=j: e.scalar_tensor_tensor(
                        out=S1[:, :, j], in0=modv[:, 16:32, j], scalar=1.0,
                        in1=vl[:, VEC_COLS["gmix"]:VEC_COLS["gmix"] + 16], op0=ALU.add, op1=ALU.mult),
                        [modv_b, vec_b[l]], [S1_b])
                    K.op("dve", lambda e, j=j: e.scalar_tensor_tensor(
                        out=S2[:, :, j], in0=modv[:, 64:80, j], scalar=1.0,
                        in1=vl[:, VEC_COLS["gffn"]:VEC_COLS["gffn"] + 16], op0=ALU.add, op1=ALU.mult),
                        [modv_b, vec_b[l]], [S2_b])
                K.barrier()

        def front(src_ap, src_b, TT, xt, xt_b, sq, sq_b, rstd, rstd_b, tmps, out_fn, scale_fn, bias_fn, extra_reads,
                  nfeat=D):
            K.dma("sp", xt[:, :, :TT], src_ap.rearrange("(k p) t -> p k t", p=128), [src_b], [xt_b])
            K.op("act", lambda e: e.activation(out=sq[:, :, :TT], in_=xt[:, :, :TT], func=AF.Square), [xt_b], [sq_b])
            K.mm(ps[0][:, :TT], psb[0], [(ones[:], sq[:, k, :TT]) for k in range(KC)], [ones_b, sq_b])
            K.op("act", lambda e: e.activation(out=rstd[:, :TT], in_=ps[0][:, :TT], func=AF.Sqrt, scale=1.0 / nfeat,
                                               bias=epsb[:, 0:1]), [psb[0], epsb_b], [rstd_b])
            K.op("dve", lambda e: e.reciprocal(out=rstd[:, :TT], in_=rstd[:, :TT]), [rstd_b], [rstd_b])
            for k in range(KC):
                tm_, tm_b = tmps[k % 2]
                K.op("dve", lambda e, k=k, tm_=tm_: e.tensor_tensor(out=tm_[:, :TT], in0=xt[:, k, :TT], in1=rstd[:, :TT],
                                                                    op=ALU.mult), [xt_b, rstd_b], [tm_b])
                o_ap, o_b = out_fn(k)
                K.op("act", lambda e, k=k, tm_=tm_, o_ap=o_ap: e.activation(
                    out=o_ap, in_=tm_[:, :TT], func=AF.Identity, scale=scale_fn(k), bias=bias_fn(k)),
                    [tm_b] + extra_reads, [o_b])

        def rms_feat(nchunks, nfeat, TT, qd, qd_b, sqq, sqq_b, rstd, rstd_b, outt, outt_b, gcol, l):
            K.mm(ps[0][:, :TT], psb[0], [(ones[:], sqq[:, c, :TT]) for c in range(nchunks)], [ones_b, sqq_b])
            K.op("act", lambda e: e.activation(out=rstd[:, :TT], in_=ps[0][:, :TT], func=AF.Sqrt, scale=1.0 / nfeat,
                                               bias=epsb[:, 0:1]), [psb[0], epsb_b], [rstd_b])
            K.op("dve", lambda e: e.reciprocal(out=rstd[:, :TT], in_=rstd[:, :TT]), [rstd_b], [rstd_b])
            for c in range(nchunks):
                K.op("dve", lambda e, c=c: e.scalar_tensor_tensor(
                    out=outt[:, c, :TT], in0=qd[:, c, :TT], scalar=V(l, gcol, c), in1=rstd[:, :TT],
                    op0=ALU.mult, op1=ALU.mult), [qd_b, rstd_b, vec_b[l]], [outt_b])

        def stage1(l, last):
            with ExitStack() as st:
                xt, xt_b = TL(st, "xt", [128, 16, 512], F32)
                sq, sq_b = TL(st, "sq", [128, 16, 512], BF16)
                hx, hx_b = TL(st, "hx", [128, 16, 512], BF16)
                rstd, rstd_b = TL(st, "rstd", [128, 512], F32)
                tmps = [TL(st, f"tmp{i}", [128, 512], F32) for i in range(2)]
                wsl = [TL(st, f"w1_{i}", [128, 16, 512], BF16) for i in range(2)]
                ut = [TL(st, f"ut{i}", [128, 512], BF16) for i in range(2)]
                abt, abt_b = TL(st, "abt", [128, 4, 8, 256], BF16)
                sg, sg_b = TL(st, "sg", [128, 4, 512], BF16)
                vt, vt_b = TL(st, "vt", [128, 8, 512], BF16)
                sm, sm_b = TL(st, "sm", [128, 8, 16], BF16)
                qd, qd_b = TL(st, "qd", [128, 6, 512], F32)
                sqq, sqq_b = TL(st, "sqq", [128, 6, 512], BF16)
                qn, qn_b = TL(st, "qn", [128, 6, 512], BF16)
                kvn, kvn_b = TL(st, "kvn", [128, 4, 512], BF16)
                rp, rp_b = TL(st, "rp", [64, 2, 512], F32)
                r1, r1_b = TL(st, "r1", [64, 512], F32)
                r2, r2_b = TL(st, "r2", [64, 512], F32)
                kr, kr_b = TL(st, "kr", [64, 512], BF16)
                wi = 0
                for ti, (t0, TT, isctx) in enumerate(TILES):
                    j = 1 if isctx else 0
                    if l == 0:
                        src = ctxT_in[:, :] if isctx else xT_in[:, t0:t0 + TT]
                        src_b = nobuf
                    else:
                        src, src_b = XB[:, t0:t0 + TT], dbufs["XB"][ti]
                    front(src, src_b, TT, xt, xt_b, sq, sq_b, rstd, rstd_b, tmps,
                          lambda k: (hx[:, k, :TT], hx_b), lambda k: S1[:, k, j:j + 1], lambda k: modv[:, k, j:j + 1],
                          [S1_b, modv_b])
                    only_kv = last and isctx
                    if last and not isctx:
                        allowed = None if ti < NOWN else (("F", "G", "A", "KV", "KPE") if ti in (NOWN, 7) else
                                                          ("F", "KV", "KPE"))
                    else:
                        allowed = None
                    if not only_kv and not (last and ti >= NOWN):
                        K.dma("act", HXT[:, t0:t0 + TT].rearrange("(k p) t -> p k t", p=128), hx[:, :, :TT], [hx_b],
                              [dbufs["HXT"][ti]])
                    K.dma("sp", rp[:, :, :TT], rope_in[:, :, t0:t0 + TT], [nobuf], [rp_b])
                    for si, (nm, c0, w) in enumerate(S1_TILES):
                        if only_kv and nm not in ("KV", "KPE"):
                            continue
                        if allowed is not None and nm not in allowed:
                            continue
                        wt, wt_b = wsl[wi % 2]
                        wi += 1
                        src_w, _, _ = wtile(l, f"s1_{si}", 0)
                        K.dma("sp", wt[:, :, :w], src_w, lastw["b"], [wt_b])
                        if nm == "KPE":
                            pa, pb_ = 5, 6
                            K.mm(ps[pa][0:64, :TT], psb[pa], [(wt[:, k, 0:64], hx[:, k, :TT]) for k in range(KC)],
                                 [wt_b, hx_b])
                            K.mm(ps[pb_][0:64, :TT], psb[pb_], [(wt[:, k, 64:128], hx[:, k, :TT]) for k in range(KC)],
                                 [wt_b, hx_b])
                            K.op("dve", lambda e: e.tensor_tensor(out=r1[:, :TT], in0=ps[pa][0:64, :TT], in1=rp[:, 0, :TT],
                                                                  op=ALU.mult), [psb[pa], rp_b], [r1_b])
                            K.op("dve", lambda e: e.tensor_tensor(out=r2[:, :TT], in0=ps[pb_][0:64, :TT],
                                                                  in1=rp[:, 1, :TT], op=ALU.mult), [psb[pb_], rp_b], [r2_b])
                            K.op("dve", lambda e: e.tensor_tensor(out=kr[:, :TT], in0=r1[:, :TT], in1=r2[:, :TT],
                                                                   op=ALU.add), [r1_b, r2_b], [kr_b])
                            K.dma("act", KPET[:, t0:t0 + TT], kr[:, :TT], [kr_b], [dbufs["KPET"][ti]])
                            continue
                        for mi in range(w // 128):
                            c = c0 // 128 + mi
                            p = nextps([1, 2, 3, 4])
                            K.mm(ps[p][:, :TT], psb[p], [(wt[:, k, mi * 128:(mi + 1) * 128], hx[:, k, :TT])
                                                         for k in range(KC)], [wt_b, hx_b])
                            if nm == "F":
                                u_, u_b = ut[c % 2]
                                K.op("act", lambda e, u_=u_, p=p: e.activation(out=u_[:, :TT], in_=ps[p][:, :TT],
                                                                               func=AF.Identity), [psb[p]], [u_b])
                                for s in range(TT // 128):
                                    p2 = nextps([6, 7])
                                    K.mm(ps[p2][:, 0:256], psb[p2], [(u_[:, s * 128:(s + 1) * 128], csm[:])],
                                         [u_b, csm_b])
                                    K.op("dve", lambda e, s=s, c=c, p2=p2: e.tensor_copy(out=abt[:, s, c, :],
                                                                                         in_=ps[p2][:, 0:256]),
                                         [psb[p2]], [abt_b])
                            elif nm == "G":
                                K.op("act", lambda e, mi=mi, p=p: e.activation(out=sg[:, mi, :TT], in_=ps[p][:, :TT],
                                                                               func=AF.Sigmoid), [psb[p]], [sg_b])
                            elif nm == "A":
                                K.op("dve", lambda e, mi=mi, c=c, p=p: e.tensor_tensor(
                                    out=vt[:, c, :TT], in0=ps[p][:, :TT], in1=sg[:, mi, :TT], op=ALU.mult),
                                    [psb[p], sg_b], [vt_b])
                            elif nm == "Q":
                                K.op("act", lambda e, c=c, p=p: e.activation(out=qd[:, c, :TT], in_=ps[p][:, :TT],
                                                                             func=AF.Identity), [psb[p]], [qd_b])
                                K.op("act", lambda e, c=c, p=p: e.activation(out=sqq[:, c, :TT], in_=ps[p][:, :TT],
                                                                             func=AF.Square), [psb[p]], [sqq_b])
                            elif nm == "KV":
                                K.op("act", lambda e, c=c, p=p: e.activation(out=qd[:, c, :TT], in_=ps[p][:, :TT],
                                                                             func=AF.Identity), [psb[p]], [qd_b])
                                K.op("act", lambda e, c=c, p=p: e.activation(out=sqq[:, c, :TT], in_=ps[p][:, :TT],
                                                                             func=AF.Square), [psb[p]], [sqq_b])
                        if nm == "F" and c0 == 512:
                            K.dma("act", AB[t0:t0 + TT, :].rearrange("(s p) c -> p s c", p=128),
                                  abt[:, 0:TT // 128, :, :].rearrange("p s g x -> p s (g x)"), [abt_b], [dbufs["AB"][ti]])
                        if nm == "A" and c0 == 512:
                            v0 = (VT_CTX0 if isctx else VT_LAT0 + t0)
                            K.dma("act", VT[:, v0:v0 + TT].rearrange("(c p) t -> p c t", p=128), vt[:, :, :TT], [vt_b],
                                  [dbufs["VT"][ti]])
                            if not isctx and ti in (0, 3, 4, 7):
                                sidx = {7: 0, 0: 1, 3: 2, 4: 3}[ti]
                                mcol = 384 if ti in (0, 7) else 385
                                e0 = 0 if ti in (0, 4) else TT - 16
                                K.op("dve", lambda e, e0=e0, mcol=mcol: e.tensor_scalar(
                                    out=sm[:, :, :], in0=vt[:, :, e0:e0 + 16], scalar1=cstf[:, mcol:mcol + 1], scalar2=None,
                                    op0=ALU.mult), [vt_b, cstf_b], [sm_b])
                                K.dma("act", SEAM[sidx].rearrange("(c p) x -> p c x", p=128), sm[:, :, :], [sm_b],
                                      [seam_b[sidx]])
                        if nm == "Q" and c0 == 512:
                            rms_feat(6, 768, TT, qd, qd_b, sqq, sqq_b, rstd, rstd_b, qn, qn_b, "qg", l)
                            K.dma("act", QNT[:, t0:t0 + TT].rearrange("(c p) t -> p c t", p=128), qn[:, :, :TT], [qn_b],
                                  [dbufs["QNT"][ti]])
                        if nm == "KV":
                            rms_feat(4, 512, TT, qd, qd_b, sqq, sqq_b, rstd, rstd_b, kvn, kvn_b, "kvg", l)
                            K.dma("act", KVNT[:, t0:t0 + TT].rearrange("(c p) t -> p c t", p=128), kvn[:, :, :TT],
                                  [kvn_b], [dbufs["KVNT"][ti]])
                K.barrier()

        def stage2_dft(l, last):
            with ExitStack() as st:
                tbs = [st.enter_context(nc.sbuf_tensor(f"tb{l}_{i}", [128, 32, 2, 512], BF16)) for i in range(2)]
                tbs_b = [[Buf() for _ in range(32)] for _ in range(2)]
                abg = [TL(st, f"abg{i}", [128, 32, 256], BF16) for i in range(2)]
                yf = [TL(st, f"yfo{i}", [128, 512], BF16) for i in range(2)]
                gi = 0
                jobs = [(kt, False) for kt in range(NOWN if last else 8)] + ([] if last else [(0, True)])
                def load_slab(ji):
                    kt, isctx = jobs[ji]
                    tb, tb_b = tbs[ji % 2], tbs_b[ji % 2]
                    if isctx:
                        K.dma("sp", tb[:, 0:2, :, 0:256], tabc_in.rearrange("p (c j k) -> p c j k", c=2, j=2),
                              [nobuf], tb_b[0:2])
                    else:
                        tsrc = tabl_in[kt].rearrange("p (c j k) -> p c j k", c=32, j=2)
                        for c4 in range(8):
                            K.dma("sp", tb[:, c4 * 4:(c4 + 1) * 4, :, :], tsrc[:, c4 * 4:(c4 + 1) * 4, :, :], [nobuf],
                                  tb_b[c4 * 4:(c4 + 1) * 4])

                load_slab(0)
                for ji, (kt, isctx) in enumerate(jobs):
                    tb, tb_b = tbs[ji % 2], tbs_b[ji % 2]
                    if isctx:
                        ncn, KW, ti, n0 = 2, 256, 8, NLAT
                        scale = (NCTX * 128.0) ** -0.5
                    else:
                        ncn, KW, ti, n0 = 32, 512, kt, 0
                        scale = (NLAT * 128.0) ** -0.5
                    for g in range(8):
                        ab_, ab_b = abg[gi % 2]
                        yo, yo_b = yf[gi % 2]
                        gi += 1
                        if g == 1 and ji + 1 < len(jobs):
                            load_slab(ji + 1)
                        rd = dbufs["AB"][8:9] if isctx else dbufs["AB"][0:8]
                        K.dma("sp", ab_[:, 0:ncn, :],
                              AB[n0:n0 + ncn * 128, g * 256:(g + 1) * 256].rearrange("(c p) x -> p c x", p=128), rd,
                              [ab_b])
                        p = nextps([1, 2, 3, 4])
                        pairs = []
                        for c in range(ncn):
                            pairs.append((ab_[:, c, 0:128], tb[:, c, 0, 0:KW]))
                            pairs.append((ab_[:, c, 128:256], tb[:, c, 1, 0:KW]))
                        K.mm(ps[p][:, :KW], psb[p], pairs, [ab_b] + tb_b[0:ncn])
                        K.op("act", lambda e, p=p, yo=yo, KW=KW, scale=scale: e.activation(
                            out=yo[:, :KW], in_=ps[p][:, :KW], func=AF.Identity, scale=scale), [psb[p]], [yo_b])
                        t0 = n0 + kt * 512
                        K.dma("act", YFT[g * 128:(g + 1) * 128, t0:t0 + KW], yo[:, :KW], [yo_b], [dbufs["YFT"][ti]])
                K.barrier()

        def stage3_conv(l, last):
            with ExitStack() as st:
                dg, dg_b = TL(st, "dg", [128, 8, 31, 128], BF16)
                vh = [TL(st, f"vh{i}", [128, 544], BF16) for i in range(3)]
                cvb, cvb_b = TL(st, "cvb", [128, 8, 512], BF16)
                sq, sq_b = TL(st, "csq", [128, 8, 512], BF16)
                mean, mean_b = TL(st, "mean", [128, 512], F32)
                m2, m2_b = TL(st, "m2", [128, 512], F32)
                rstd, rstd_b = TL(st, "crstd", [128, 512], F32)
                nmr, nmr_b = TL(st, "nmr", [128, 512], F32)
                tmps = [TL(st, f"ctmp{i}", [128, 512], F32) for i in range(2)]
                cs, cs_b = TL(st, "cs", [128, 8, 512], BF16)
                for c in range(8):
                    for k in range(31):
                        K.op("dve", lambda e, c=c, k=k: e.tensor_scalar(
                            out=dg[:, c, k, :], in0=cstf[:, 0:128], scalar1=V(l, "convw", c * 31 + k), scalar2=None,
                            op0=ALU.mult), [cstf_b, vec_b[l]], [dg_b])
                vi = 0
                for ti, (t0, TT, isctx) in enumerate(TILES):
                    if last and (isctx or ti >= NOWN):
                        continue
                    v0 = (VT_CTX0 if isctx else VT_LAT0 + t0) - 16
                    rdl = [dbufs["VT"][ti], vtpad_b]
                    if not isctx:
                        rdl.append(dbufs["VT"][(ti - 1) % 8])
                        rdl.append(dbufs["VT"][(ti + 1) % 8])
                    for c in range(8):
                        vh_, vh_b = vh[vi % 3]
                        vi += 1
                        crow = VT[c * 128:(c + 1) * 128, :]
                        if isctx or ti not in (0, 3, 4, 7):
                            K.dma("sp", vh_[:, 0:TT + 32], crow[:, v0:v0 + TT + 32], rdl, [vh_b])
                        elif ti in (0, 4):
                            sidx = 0 if ti == 0 else 2
                            K.dma("sp", vh_[:, 0:16], SEAM[sidx][c * 128:(c + 1) * 128, :], [seam_b[sidx]], [vh_b])
                            K.dma("sp", vh_[:, 16:TT + 32], crow[:, v0 + 16:v0 + TT + 32], rdl, [vh_b])
                        else:
                            sidx = 3 if ti == 3 else 1
                            K.dma("sp", vh_[:, 0:TT + 16], crow[:, v0:v0 + TT + 16], rdl, [vh_b])
                            K.dma("sp", vh_[:, TT + 16:TT + 32], SEAM[sidx][c * 128:(c + 1) * 128, :], [seam_b[sidx]],
                                  [vh_b])
                        p = nextps([1, 2, 3, 4])
                        K.mm(ps[p][:, :TT], psb[p], [(dg[:, c, k, :], vh_[:, k + 1:k + 1 + TT]) for k in range(31)],
                             [dg_b, vh_b])
                        K.op("act", lambda e, c=c, p=p: e.activation(out=cvb[:, c, :TT], in_=ps[p][:, :TT],
                                                                     func=AF.Identity, bias=V(l, "conv_b", c)),
                             [psb[p], vec_b[l]], [cvb_b])
                        K.op("act", lambda e, c=c, p=p: e.activation(out=sq[:, c, :TT], in_=ps[p][:, :TT],
                                                                     func=AF.Square, bias=V(l, "conv_b", c)),
                             [psb[p], vec_b[l]], [sq_b])
                    K.mm(ps[5][:, :TT], psb[5], [(ones[:], cvb[:, c, :TT]) for c in range(8)], [ones_b, cvb_b])
                    K.mm(ps[6][:, :TT], psb[6], [(ones[:], sq[:, c, :TT]) for c in range(8)], [ones_b, sq_b])
                    K.op("dve", lambda e: e.tensor_scalar(out=mean[:, :TT], in0=ps[5][:, :TT], scalar1=1.0 / 1024,
                                                          scalar2=None, op0=ALU.mult), [psb[5]], [mean_b])
                    K.op("dve", lambda e: e.tensor_tensor(out=m2[:, :TT], in0=mean[:, :TT], in1=mean[:, :TT],
                                                          op=ALU.mult), [mean_b], [m2_b])
                    K.op("dve", lambda e: e.scalar_tensor_tensor(out=rstd[:, :TT], in0=ps[6][:, :TT], scalar=1.0 / 1024,
                                                                 in1=m2[:, :TT], op0=ALU.mult, op1=ALU.subtract),
                         [psb[6], m2_b], [rstd_b])
                    K.op("act", lambda e: e.activation(out=rstd[:, :TT], in_=rstd[:, :TT], func=AF.Sqrt,
                                                       bias=epsb[:, 0:1]), [rstd_b, epsb_b], [rstd_b])
                    K.op("dve", lambda e: e.reciprocal(out=rstd[:, :TT], in_=rstd[:, :TT]), [rstd_b], [rstd_b])
                    K.op("dve", lambda e: e.scalar_tensor_tensor(out=nmr[:, :TT], in0=mean[:, :TT], scalar=-1.0,
                                                                 in1=rstd[:, :TT], op0=ALU.mult, op1=ALU.mult),
                         [mean_b, rstd_b], [nmr_b])
                    for c in range(8):
                        tm_, tm_b = tmps[c % 2]
                        K.op("dve", lambda e, c=c, tm_=tm_: e.tensor_tensor(out=tm_[:, :TT], in0=cvb[:, c, :TT],
                                                                            in1=rstd[:, :TT], op=ALU.mult),
                             [cvb_b, rstd_b], [tm_b])
                        K.op("pool", lambda e, tm_=tm_: e.tensor_tensor(out=tm_[:, :TT], in0=tm_[:, :TT],
                                                                        in1=nmr[:, :TT], op=ALU.add), [tm_b, nmr_b],
                             [tm_b])
                        K.op("act", lambda e, c=c, tm_=tm_: e.activation(
                            out=cs[:, c, :TT], in_=tm_[:, :TT], func=AF.Silu, scale=V(l, "ln_g", c),
                            bias=V(l, "ln_b", c)), [tm_b, vec_b[l]], [cs_b])
                    K.dma("act", CST[:, t0:t0 + TT].rearrange("(c p) t -> p c t", p=128), cs[:, :, :TT], [cs_b],
                          [dbufs["CST"][ti]])
                K.barrier()

        def stage4_attn(l, last):
            with ExitStack() as st:
                kvn, kvn_b = TL(st, "akvn", [128, 4, T], BF16)
                kra, kra_b = TL(st, "akr", [128, T], BF16)
                v4, v4_b = TL(st, "v4", [128, 34, 512], BF16)
                kh = [TL(st, f"kh{i}", [128, T], BF16) for i in range(2)]
                qn = [TL(st, f"aqn{i}", [128, 6, 512], BF16) for i in range(3)]
                rp = [TL(st, f"arp{i}", [64, 2, 512], F32) for i in range(3)]
                qnp = [TL(st, f"qnp{i}", [128, 512], BF16) for i in range(2)]
                qr = [TL(st, f"qr{i}", [128, 512], BF16) for i in range(2)]
                r1, r1_b = TL(st, "ar1", [64, 512], F32)
                r2, r2_b = TL(st, "ar2", [64, 512], F32)
                pt = [TL(st, f"pt{i}", [128, 512], BF16) for i in range(6)]
                accs = [TL(st, f"acc{i}", [128, 512], F32) for i in range(2)]
                dhi, dhi_b = TL(st, "dhi", [128, 512], BF16)
                dlo, dlo_b = TL(st, "dlo", [128, 512], BF16)
                rd, rd_b = TL(st, "rd", [128, 512], F32)
                ob = [TL(st, f"ob{i}", [128, 512], BF16) for i in range(2)]
                wuk = [TL(st, f"wuk{i}", [128, 4, 128], BF16) for i in range(2)]
                wuv, wuv_b = TL(st, "wuv", [128, 4, 512], BF16)
                wuq = [TL(st, f"wuq{i}", [128, 6, 256], BF16) for i in range(2)]
                trg, _ = TL(st, "trg", [128, 1], F32)
                K.dma("sp", kvn[:], KVNT.rearrange("(c p) t -> p c t", p=128), dbufs["KVNT"], [kvn_b])
                K.op("dve", lambda e: e.memset(kra[64:128, :], 0.0), [], [kra_b])
                for qr_i, qr_ib in qr:
                    K.op("dve", lambda e, qr_i=qr_i: e.memset(qr_i[64:128, :], 0.0), [], [qr_ib])
                K.dma("sp", kra[0:64, :], KPET, dbufs["KPET"], [kra_b])
                qi = 0
                pi = 0
                oi = 0
                for hg in range(4):
                    srcv, _, _ = wtile(l, "wuv", hg)
                    K.dma("sp", wuv[:], srcv, lastw["b"], [wuv_b])
                    for ck in range(34):
                        p = nextps([0, 1, 2, 7, 6])
                        K.mm(ps[p][:, :], psb[p], [(kvn[:, kc, ck * 128:(ck + 1) * 128], wuv[:, kc, :]) for kc in range(4)],
                             [kvn_b, wuv_b])
                        if ck % 2 == 0:
                            K.op("dve", lambda e, ck=ck, p=p: e.tensor_copy(out=v4[:, ck, :], in_=ps[p][:, :]), [psb[p]],
                                 [v4_b])
                        else:
                            K.op("act", lambda e, ck=ck, p=p: e.activation(out=v4[:, ck, :], in_=ps[p][:, :],
                                                                           func=AF.Identity), [psb[p]], [v4_b])
                    for hl in range(4):
                        h = hg * 4 + hl
                        wk_, wk_b = wuk[h % 2]
                        wq_, wq_b = wuq[h % 2]
                        kh_, kh_b = kh[h % 2]
                        srck, _, _ = wtile(l, "wuk", h)
                        K.dma("sp", wk_[:], srck, lastw["b"], [wk_b])
                        srcq, _, _ = wtile(l, "wuq", h)
                        K.dma("sp", wq_[:], srcq, lastw["b"], [wq_b])
                        for kt in range(9):
                            k0 = kt * 512
                            KW = min(512, T - k0)
                            p = nextps([0, 1, 2, 7, 6])
                            K.mm(ps[p][:, :KW], psb[p], [(wk_[:, kc, :], kvn[:, kc, k0:k0 + KW]) for kc in range(4)],
                                 [wk_b, kvn_b])
                            K.op("dve", lambda e, p=p, k0=k0, KW=KW, kh_=kh_: e.tensor_copy(out=kh_[:, k0:k0 + KW],
                                                                                           in_=ps[p][:, :KW]),
                                 [psb[p]], [kh_b])
                        qtiles = [(ti, t0, TT, isctx) for ti, (t0, TT, isctx) in enumerate(TILES) if not (last and (isctx or ti >= NOWN))]
                        qst = {}

                        def load_q(idx, qtiles=qtiles, qst=qst):
                            ti, t0, TT, isctx = qtiles[idx]
                            qn_, qn_b = qn[idx % 3]
                            rp_, rp_b = rp[idx % 3]
                            K.dma("sp", qn_[:, :, :TT], QNT[:, t0:t0 + TT].rearrange("(c p) t -> p c t", p=128),
                                  [dbufs["QNT"][ti]], [qn_b])
                            K.dma("sp", rp_[:, :, :TT], rope_in[:, :, t0:t0 + TT], [nobuf], [rp_b])

                        def proj_q(idx, qtiles=qtiles, qst=qst, wq_=wq_, wq_b=wq_b):
                            ti, t0, TT, isctx = qtiles[idx]
                            qn_, qn_b = qn[idx % 3]
                            rp_, rp_b = rp[idx % 3]
                            qnp_, qnp_b = qnp[idx % 2]
                            qr_, qr_b = qr[idx % 2]
                            pa = nextps([0, 1, 2, 7, 6])
                            K.mm(ps[pa][:, :TT], psb[pa], [(wq_[:, kc, 0:128], qn_[:, kc, :TT]) for kc in range(6)],
                                 [wq_b, qn_b])
                            K.op("act", lambda e: e.activation(out=qnp_[:, :TT], in_=ps[pa][:, :TT],
                                                               func=AF.Identity, scale=ATTN_SCALE), [psb[pa]], [qnp_b])
                            pb1 = nextps([0, 1, 2, 7, 6])
                            K.mm(ps[pb1][0:64, :TT], psb[pb1], [(wq_[:, kc, 128:192], qn_[:, kc, :TT]) for kc in range(6)],
                                 [wq_b, qn_b])
                            K.op("dve", lambda e: e.scalar_tensor_tensor(
                                out=r1[:, :TT], in0=ps[pb1][0:64, :TT], scalar=ATTN_SCALE, in1=rp_[:, 0, :TT],
                                op0=ALU.mult, op1=ALU.mult), [psb[pb1], rp_b], [r1_b])
                            pb2 = nextps([0, 1, 2, 7, 6])
                            K.mm(ps[pb2][0:64, :TT], psb[pb2], [(wq_[:, kc, 192:256], qn_[:, kc, :TT]) for kc in range(6)],
                                 [wq_b, qn_b])
                            K.op("dve", lambda e: e.scalar_tensor_tensor(
                                out=r2[:, :TT], in0=ps[pb2][0:64, :TT], scalar=ATTN_SCALE, in1=rp_[:, 1, :TT],
                                op0=ALU.mult, op1=ALU.mult), [psb[pb2], rp_b], [r2_b])
                            K.op("dve", lambda e: e.tensor_tensor(out=qr_[0:64, :TT], in0=r1[:, :TT], in1=r2[:, :TT],
                                                                  op=ALU.add), [r1_b, r2_b], [qr_b])

                        load_q(0)
                        if len(qtiles) > 1:
                            load_q(1)
                        proj_q(0)
                        for idx, (ti, t0, TT, isctx) in enumerate(qtiles):
                            if idx + 2 < len(qtiles):
                                load_q(idx + 2)
                            if idx + 1 < len(qtiles):
                                proj_q(idx + 1)
                            qnp_, qnp_b = qnp[idx % 2]
                            qr_, qr_b = qr[idx % 2]
                            cks = [32, 33] if isctx else list(range(34))
                            ncks = len(cks)
                            po = 3 + (oi % 2)
                            pd = 5
                            ob_, ob_b = ob[oi % 2]
                            oi += 1
                            sbank = {}
                            used = {}

                            def emit_S(i):
                                ck = cks[i]
                                p = nextps([0, 1, 2, 7, 6])
                                K.mm(ps[p][:, :TT], psb[p],
                                     [(kh_[:, ck * 128:(ck + 1) * 128], qnp_[:, :TT]),
                                      (kra[:, ck * 128:(ck + 1) * 128], qr_[:, :TT])],
                                     [kh_b, qnp_b, kra_b, qr_b])
                                sbank[i] = p

                            LOOK = 3
                            for i in range(min(LOOK, ncks)):
                                emit_S(i)
                            for i, ck in enumerate(cks):
                                if i + LOOK < ncks:
                                    emit_S(i + LOOK)
                                p = sbank[i]
                                pt_, pt_b = pt[pi % 6]
                                pi += 1
                                K.op("act", lambda e, p=p, pt_=pt_: e.activation(out=pt_[:, :TT], in_=ps[p][:, :TT],
                                                                                 func=AF.Exp), [psb[p]], [pt_b])
                                K.mm(ps[po][:, :TT], psb[po], [(v4[:, ck, hl * 128:(hl + 1) * 128], pt_[:, :TT])],
                                     [v4_b, pt_b], start=(i == 0), stop=(i == ncks - 1))
                                r3 = i % 3
                                if r3 == 0:
                                    K.mm(ps[pd][:, :TT], psb[pd], [(ones[:], pt_[:, :TT])], [ones_b, pt_b],
                                         start=(i == 0), stop=False)
                                else:
                                    ae = "pool" if r3 == 1 else "dve"
                                    ac_, ac_b = accs[r3 - 1]
                                    if r3 not in used:
                                        used[r3] = 1
                                        K.op(ae, lambda e, ac_=ac_, pt_=pt_: e.tensor_copy(out=ac_[:, :TT], in_=pt_[:, :TT]),
                                             [pt_b], [ac_b])
                                    else:
                                        K.op(ae, lambda e, ac_=ac_, pt_=pt_: e.tensor_tensor(
                                            out=ac_[:, :TT], in0=ac_[:, :TT], in1=pt_[:, :TT], op=ALU.add),
                                            [pt_b, ac_b], [ac_b])
                            if 2 in used:
                                K.op("dve", lambda e: e.tensor_tensor(out=accs[0][0][:, :TT], in0=accs[0][0][:, :TT],
                                                                      in1=accs[1][0][:, :TT], op=ALU.add),
                                     [accs[0][1], accs[1][1]], [accs[0][1]])
                            K.op("dve", lambda e: e.tensor_copy(out=dhi[:, :TT], in_=accs[0][0][:, :TT]), [accs[0][1]],
                                 [dhi_b])
                            K.op("dve", lambda e: e.tensor_tensor(out=dlo[:, :TT], in0=accs[0][0][:, :TT], in1=dhi[:, :TT],
                                                                  op=ALU.subtract), [accs[0][1], dhi_b], [dlo_b])
                            K.mm(ps[pd][:, :TT], psb[pd], [(ones[:], dhi[:, :TT]), (ones[:], dlo[:, :TT])],
                                 [ones_b, dhi_b, dlo_b], start=False, stop=True)
                            K.op("dve", lambda e, pd=pd: e.reciprocal(out=rd[:, :TT], in_=ps[pd][:, :TT]), [psb[pd]],
                                 [rd_b])
                            K.op("dve", lambda e, po=po, ob_=ob_: e.tensor_tensor(out=ob_[:, :TT], in0=ps[po][:, :TT],
                                                                                  in1=rd[:, :TT], op=ALU.mult),
                                 [psb[po], rd_b], [ob_b])
                            K.dma("act", OT[h * 128:(h + 1) * 128, t0:t0 + TT], ob_[:, :TT], [ob_b],
                                  [dbufs["OT"][ti]])
                        if h == 0 and l + 1 < DEPTH:
                            trig_b = Buf()
                            K.op("dve", lambda e: e.memset(trg[:], 0.0), [], [trig_b])
                            convert_weights(l + 1, trig_b)
                K.barrier()

        def stage5a(l, last):
            with ExitStack() as st:
                hx, hx_b = TL(st, "mhx", [128, 16, 512], BF16)
                yf, yf_b = TL(st, "myf", [128, 8, 512], BF16)
                cs, cs_b = TL(st, "mcs", [128, 8, 512], BF16)
                ot, ot_b = TL(st, "mot", [128, 16, 512], BF16)
                gw = [TL(st, f"gw{i}", [128, 16, 768], BF16) for i in range(2)]
                wf = [TL(st, f"mwf{i}", [128, 8, 256], BF16) for i in range(2)]
                wc = [TL(st, f"mwc{i}", [128, 8, 256], BF16) for i in range(2)]
                wo = [TL(st, f"mwo{i}", [128, 16, 256], BF16) for i in range(2)]
                wout = [TL(st, f"mwout{i}", [128, 16, 256], BF16) for i in range(2)]
                mg, mg_b = TL(st, "mg", [128, 16, 512], BF16)
                sgs = [TL(st, f"msg{i}", [128, 512], F32) for i in range(3)]
                ms = [TL(st, f"mm{i}", [128, 512], F32) for i in range(3)]
                xr = [TL(st, f"xr{i}", [128, 512], F32) for i in range(2)]
                xo = [TL(st, f"xo{i}", [128, 512], F32) for i in range(2)]
                wi = 0
                xi = 0
                for ti, (t0, TT, isctx) in enumerate(TILES):
                    if last and (isctx or ti >= NOWN):
                        continue
                    jj_ = 1 if isctx else 0
                    K.dma("sp", hx[:, :, :TT], HXT[:, t0:t0 + TT].rearrange("(k p) t -> p k t", p=128),
                          [dbufs["HXT"][ti]], [hx_b])
                    K.dma("sp", yf[:, :, :TT], YFT[:, t0:t0 + TT].rearrange("(k p) t -> p k t", p=128),
                          [dbufs["YFT"][ti]], [yf_b])
                    K.dma("sp", cs[:, :, :TT], CST[:, t0:t0 + TT].rearrange("(k p) t -> p k t", p=128),
                          [dbufs["CST"][ti]], [cs_b])
                    K.dma("sp", ot[:, :, :TT], OT[:, t0:t0 + TT].rearrange("(k p) t -> p k t", p=128),
                          [dbufs["OT"][ti]], [ot_b])
                    for jp in range(8):
                        gw_, gw_b = gw[wi % 2]
                        wf_, wf_b = wf[wi % 2]
                        wc_, wc_b = wc[wi % 2]
                        wo_, wo_b = wo[wi % 2]
                        wi += 1
                        K.dma("sp", gw_[:], wtile(l, "gate", jp)[0], lastw["b"], [gw_b])
                        K.dma("sp", wf_[:], wtile(l, "wf", jp)[0], lastw["b"], [wf_b])
                        K.dma("sp", wc_[:], wtile(l, "wc", jp)[0], lastw["b"], [wc_b])
                        K.dma("sp", wo_[:], wtile(l, "wo", jp)[0], lastw["b"], [wo_b])
                        for jj in range(2):
                            j = jp * 2 + jj
                            pg = [0, 1, 2]
                            py = [3, 4, 5]
                            for bi in range(3):
                                K.mm(ps[pg[bi]][:, :TT], psb[pg[bi]],
                                     [(gw_[:, k, jj * 384 + bi * 128:jj * 384 + (bi + 1) * 128], hx[:, k, :TT])
                                      for k in range(KC)], [gw_b, hx_b])
                                K.op("act", lambda e, bi=bi, j=j: e.activation(
                                    out=sgs[bi][0][:, :TT], in_=ps[pg[bi]][:, :TT], func=AF.Sigmoid,
                                    bias=V(l, "bgate", bi * 16 + j)), [psb[pg[bi]], vec_b[l]], [sgs[bi][1]])
                            K.mm(ps[3][:, :TT], psb[3], [(wf_[:, k, jj * 128:(jj + 1) * 128], yf[:, k, :TT])
                                                         for k in range(8)], [wf_b, yf_b])
                            K.mm(ps[4][:, :TT], psb[4], [(wc_[:, k, jj * 128:(jj + 1) * 128], cs[:, k, :TT])
                                                         for k in range(8)], [wc_b, cs_b])
                            K.mm(ps[5][:, :TT], psb[5], [(wo_[:, k, jj * 128:(jj + 1) * 128], ot[:, k, :TT])
                                                         for k in range(KC)], [wo_b, ot_b])
                            for bi in range(3):
                                K.op("dve", lambda e, bi=bi: e.tensor_tensor(
                                    out=ms[bi][0][:, :TT], in0=ps[py[bi]][:, :TT], in1=sgs[bi][0][:, :TT], op=ALU.mult),
                                    [psb[py[bi]], sgs[bi][1]], [ms[bi][1]])
                            K.op("pool", lambda e: e.tensor_tensor(out=ms[0][0][:, :TT], in0=ms[0][0][:, :TT],
                                                                   in1=ms[1][0][:, :TT], op=ALU.add),
                                 [ms[0][1], ms[1][1]], [ms[0][1]])
                            K.op("pool", lambda e, j=j: e.tensor_tensor(out=mg[:, j, :TT], in0=ms[0][0][:, :TT],
                                                                        in1=ms[2][0][:, :TT], op=ALU.add),
                                 [ms[0][1], ms[2][1]], [mg_b])
                    xs_ap = (ctxT_in if isctx else xT_in[:, t0:t0 + TT]) if l == 0 else XB[:, t0:t0 + TT]
                    xs_b = nobuf if l == 0 else dbufs["XB"][ti]
                    for mp in range(8):
                        wo2, wo2_b = wout[mp % 2]
                        K.dma("sp", wo2[:], wtile(l, "wout", mp)[0], lastw["b"], [wo2_b])
                        for jj in range(2):
                            j = mp * 2 + jj
                            xr_, xr_b = xr[xi % 2]
                            xo_, xo_b = xo[xi % 2]
                            xi += 1
                            K.dma("sp", xr_[:, :TT], xs_ap[j * 128:(j + 1) * 128, :], [xs_b], [xr_b])
                            p = nextps([6, 7])
                            K.mm(ps[p][:, :TT], psb[p], [(wo2[:, k, jj * 128:(jj + 1) * 128], mg[:, k, :TT])
                                                         for k in range(KC)], [wo2_b, mg_b])
                            K.op("dve", lambda e, p=p, j=j, xr_=xr_, xo_=xo_: e.scalar_tensor_tensor(
                                out=xo_[:, :TT], in0=ps[p][:, :TT], scalar=modv[:, 32 + j, jj_:jj_ + 1], in1=xr_[:, :TT],
                                op0=ALU.mult, op1=ALU.add), [psb[p], xr_b, modv_b], [xo_b])
                            K.dma("act", XA[j * 128:(j + 1) * 128, t0:t0 + TT], xo_[:, :TT], [xo_b], [dbufs["XA"][ti]])
                K.barrier()

        def stage5b(l, last):
            with ExitStack() as st:
                xt, xt_b = TL(st, "fxt", [128, 16, 512], F32)
                sq, sq_b = TL(st, "fsq", [128, 16, 512], BF16)
                h2, h2_b = TL(st, "h2", [128, 16, 512], BF16)
                rstd, rstd_b = TL(st, "frstd", [128, 512], F32)
                tmps = [TL(st, f"ftmp{i}", [128, 512], F32) for i in range(2)]
                wg = [TL(st, f"fwg{i}", [128, 16, 256], BF16) for i in range(2)]
                wu = [TL(st, f"fwu{i}", [128, 16, 256], BF16) for i in range(2)]
                wd = [TL(st, f"fwd{i}", [128, 44, 128], BF16) for i in range(2)]
                hid, hid_b = TL(st, "hid", [128, HC, 512], BF16)
                sgl = [TL(st, f"fsg{i}", [128, 512], BF16) for i in range(2)]
                xo = [TL(st, f"fxo{i}", [128, 512], F32) for i in range(2)]
                wi = 0
                si = 0
                for ti, (t0, TT, isctx) in enumerate(TILES):
                    if last and (isctx or ti >= NOWN):
                        continue
                    jj_ = 1 if isctx else 0
                    front(XA[:, t0:t0 + TT], dbufs["XA"][ti], TT, xt, xt_b, sq, sq_b, rstd, rstd_b, tmps,
                          lambda k: (h2[:, k, :TT], h2_b), lambda k: S2[:, k, jj_:jj_ + 1],
                          lambda k: modv[:, 48 + k, jj_:jj_ + 1], [S2_b, modv_b])
                    for hp in range(22):
                        wg_, wg_b = wg[wi % 2]
                        wu_, wu_b = wu[wi % 2]
                        wi += 1
                        K.dma("sp", wg_[:], wtile(l, "wg", hp)[0], lastw["b"], [wg_b])
                        K.dma("sp", wu_[:], wtile(l, "wu", hp)[0], lastw["b"], [wu_b])
                        for jj in range(2):
                            hc = hp * 2 + jj
                            pg = nextps([1, 2])
                            pu = nextps([3, 4])
                            K.mm(ps[pg][:, :TT], psb[pg], [(wg_[:, k, jj * 128:(jj + 1) * 128], h2[:, k, :TT])
                                                           for k in range(KC)], [wg_b, h2_b])
                            K.mm(ps[pu][:, :TT], psb[pu], [(wu_[:, k, jj * 128:(jj + 1) * 128], h2[:, k, :TT])
                                                           for k in range(KC)], [wu_b, h2_b])
                            sg_, sg_b = sgl[si % 2]
                            si += 1
                            K.op("act", lambda e, pg=pg, sg_=sg_: e.activation(out=sg_[:, :TT], in_=ps[pg][:, :TT],
                                                                               func=AF.Silu), [psb[pg]], [sg_b])
                            K.op("dve", lambda e, pu=pu, hc=hc, sg_=sg_: e.tensor_tensor(
                                out=hid[:, hc, :TT], in0=ps[pu][:, :TT], in1=sg_[:, :TT], op=ALU.mult),
                                [psb[pu], sg_b], [hid_b])
                    for j in range(16):
                        wd_, wd_b = wd[j % 2]
                        xo_, xo_b = xo[j % 2]
                        K.dma("sp", wd_[:], wtile(l, "wd", j)[0], lastw["b"], [wd_b])
                        p = nextps([5, 6, 7])
                        K.mm(ps[p][:, :TT], psb[p], [(wd_[:, k, :], hid[:, k, :TT]) for k in range(HC)], [wd_b, hid_b])
                        K.op("dve", lambda e, p=p, j=j, xo_=xo_: e.scalar_tensor_tensor(
                            out=xo_[:, :TT], in0=ps[p][:, :TT], scalar=modv[:, 80 + j, jj_:jj_ + 1], in1=xt[:, j, :TT],
                            op0=ALU.mult, op1=ALU.add), [psb[p], xt_b, modv_b], [xo_b])
                        K.dma("act", XB[j * 128:(j + 1) * 128, t0:t0 + TT], xo_[:, :TT], [xo_b], [dbufs["XB"][ti]])
                K.barrier()

        def final_norm():
            with ExitStack() as st:
                xt, xt_b = TL(st, "nxt", [128, 16, 512], F32)
                sq, sq_b = TL(st, "nsq", [128, 16, 512], BF16)
                yo, yo_b = TL(st, "nyo", [128, 16, 512], F32)
                rstd, rstd_b = TL(st, "nrstd", [128, 512], F32)
                tmps = [TL(st, f"ntmp{i}", [128, 512], F32) for i in range(2)]
                lastl = DEPTH - 1
                for ti, (t0, TT, isctx) in enumerate(TILES):
                    if isctx or ti >= NOWN:
                        continue
                    front(XB[:, t0:t0 + TT], dbufs["XB"][ti], TT, xt, xt_b, sq, sq_b, rstd, rstd_b, tmps,
                          lambda k: (yo[:, k, :TT], yo_b), lambda k: V(lastl, "gfin", k), lambda k: zb[:, 0:1],
                          [vec_b[lastl], zb_b])
                    K.dma("act", yT[:, t0:t0 + TT].rearrange("(k p) t -> p k t", p=128), yo[:, :, :TT], [yo_b],
                          [dbufs["Y"][ti]])
                K.barrier()

        for l in range(DEPTH):
            last = l == DEPTH - 1
            stage0(l)
            if l == 0:
                convert_weights(0)
            stage1(l, last)
            stage2_dft(l, last)
            stage3_conv(l, last)
            stage4_attn(l, last)
            stage5a(l, last)
            stage5b(l, last)
        final_norm()
        K.barrier()
    return nc


_NC_CACHE = {}


def kernel(**inp):
    inp = {k: np.asarray(v) for k, v in inp.items()}
    wpack = np.concatenate([_pack_weights(inp, l) for l in range(DEPTH)]).reshape(-1, 2048)
    vec = np.stack([_pack_vec(inp, l) for l in range(DEPTH)])
    if "nc" not in _NC_CACHE:
        _NC_CACHE["nc"] = build()
    nc = _NC_CACHE["nc"]
    H = NLAT // 2
    in_maps = []
    for c in range(NCORE):
        b, half = c // 2, c % 2
        cst = _constants(half)
        cv = np.stack([_pm(inp["c"][b]), _pm(inp["c_ctx"])], axis=-1).reshape(128, 32)
        xb = inp["x"][b]
        if half == 1:
            xb = np.concatenate([xb[H:], xb[:H]], axis=0)
        in_maps.append({
            "xT": np.ascontiguousarray(xb.T),
            "ctxT": np.ascontiguousarray(inp["ctx"][b].T),
            "cvec": np.ascontiguousarray(cv.astype(np.float32)),
            "wpack": wpack, "vec": vec, "cst": cst["cst"], "rope": cst["rope"],
            "tabl": cst["tabl"].reshape(8, 128, -1), "tabc": cst["tabc"].reshape(128, -1),
        })
    res = run_bass_kernel_spmd(nc, in_maps, core_ids=list(range(NCORE)))
    out = np.empty((4, NLAT, D), np.float32)
    for c in range(NCORE):
        b, half = c // 2, c % 2
        out[b, half * H:(half + 1) * H, :] = res.results[c]["yT"].T
    return out
```
